# Optimizing a Trainium2 kernel written in Bass

```python
import jax, jax.numpy as jnp
from jax import lax
import numpy as np

D_MODEL = 1024
BATCH = 8
SEQ = 2048
DEPTH = 2
DEC_BATCH = 128
DEC_SEQ = 8
PAST_LEN = 8192
PAGE_SIZE = 128

HEAD_DIM = 64
N_HEADS = 8
N_KV_HEADS = 2
GROUP = N_HEADS // N_KV_HEADS
ATTN_DIM = N_HEADS * HEAD_DIM
KV_DIM = N_KV_HEADS * HEAD_DIM
WINDOW = 128
BLOCK = WINDOW
CONV_DIM = D_MODEL // 2
CONV_WIDTH = 3
D_FF = 4 * D_MODEL
N_BRANCH = 2
IN_DIM = ATTN_DIM + 2 * KV_DIM + 3 * CONV_DIM + N_BRANCH * D_MODEL
CACHE_LEN = min(WINDOW, PAST_LEN)
EPS = 1e-6

kernel_name = "hybrid_swa_sink_shortconv_step"


def rms_norm(x, g):
    xf = x.astype(jnp.float32)
    y = xf * lax.rsqrt(jnp.mean(xf * xf, axis=-1, keepdims=True) + EPS)
    return (y * g.astype(jnp.float32)).astype(x.dtype)


def split_projection(z):
    outs = []
    off = 0
    for width in (ATTN_DIM, KV_DIM, KV_DIM, CONV_DIM, CONV_DIM, CONV_DIM, N_BRANCH * D_MODEL):
        outs.append(z[..., off:off + width])
        off += width
    return outs


def attend(q, k, v, mask, sinks):
    s = jnp.einsum('...qkgd,...skd->...kgqs', q, k).astype(jnp.float32) * (HEAD_DIM ** -0.5)
    s = jnp.where(mask, s, -jnp.inf)
    sink = jnp.broadcast_to(sinks.astype(jnp.float32)[:, :, None, None], s.shape[:-1] + (1,))
    p = jax.nn.softmax(jnp.concatenate([s, sink], axis=-1), axis=-1)[..., :-1]
    return jnp.einsum('...kgqs,...skd->...qkgd', p.astype(v.dtype), v)


def window_attention_prompt(q, k, v, sinks):
    n, s = q.shape[:2]
    nb = s // BLOCK
    qb = q.reshape(n, nb, BLOCK, N_KV_HEADS, GROUP, HEAD_DIM)

    def with_prev(t):
        tb = t.reshape(n, nb, BLOCK, N_KV_HEADS, HEAD_DIM)
        prev = jnp.pad(tb[:, :-1], ((0, 0), (1, 0), (0, 0), (0, 0), (0, 0)))
        return jnp.concatenate([prev, tb], axis=2)

    kk, vv = with_prev(k), with_prev(v)
    rel = jnp.arange(BLOCK)[:, None] + BLOCK - jnp.arange(2 * BLOCK)[None, :]
    band = (rel >= 0) & (rel < WINDOW)
    kpos = (jnp.arange(nb)[:, None] - 1) * BLOCK + jnp.arange(2 * BLOCK)[None, :]
    mask = (band[None] & (kpos >= 0)[:, None, :])[:, None, None]
    o = attend(qb, kk, vv, mask, sinks)
    return o.reshape(n, s, ATTN_DIM)


def window_attention_sample(q, k_new, v_new, k_buf, v_buf, sinks):
    n, t = q.shape[:2]
    l = k_buf.shape[1]
    kk = jnp.concatenate([k_buf, k_new], axis=1)
    vv = jnp.concatenate([v_buf, v_new], axis=1)
    rel = (l + jnp.arange(t))[:, None] - jnp.arange(l + t)[None, :]
    mask = (rel >= 0) & (rel < WINDOW)
    o = attend(q, kk, vv, mask, sinks)
    return o.reshape(n, t, ATTN_DIM), kk[:, t:], vv[:, t:]


def causal_short_conv(u, u_prev, w):
    t = u.shape[1]
    up = jnp.concatenate([u_prev, u], axis=1)
    z = w[0] * up[:, 0:t]
    for i in range(1, CONV_WIDTH):
        z = z + w[i] * up[:, i:i + t]
    return z, up[:, up.shape[1] - (CONV_WIDTH - 1):]


def decoder_layer(x, buf_k, buf_v, buf_conv, g_mix_pre, g_mix_post, g_mlp_pre, g_mlp_post,
                  w_in, attn_sinks, conv_w, w_attn_o, w_conv_o, w_out, w_up, w_down):
    n, t, _ = x.shape
    h = rms_norm(x, g_mix_pre)
    q, k, v, b_gate, c_gate, u_in, gate_logits = split_projection(h @ w_in)
    q = q.reshape(n, t, N_KV_HEADS, GROUP, HEAD_DIM)
    k = k.reshape(n, t, N_KV_HEADS, HEAD_DIM)
    v = v.reshape(n, t, N_KV_HEADS, HEAD_DIM)
    sinks = attn_sinks.reshape(N_KV_HEADS, GROUP)
    u = c_gate * u_in
    if buf_k is None:
        attn = window_attention_prompt(q, k, v, sinks)
        new_k = k[:, t - CACHE_LEN:]
        new_v = v[:, t - CACHE_LEN:]
        conv_prev = jnp.zeros((n, CONV_WIDTH - 1, CONV_DIM), u.dtype)
    else:
        attn, new_k, new_v = window_attention_sample(q, k, v, buf_k, buf_v, sinks)
        conv_prev = buf_conv
    z, new_conv = causal_short_conv(u, conv_prev, conv_w)
    attn_branch = attn @ w_attn_o
    conv_branch = (b_gate * z) @ w_conv_o
    gates = jax.nn.sigmoid(gate_logits).reshape(n, t, N_BRANCH, D_MODEL)
    mixed = (gates[..., 0, :] * attn_branch + gates[..., 1, :] * conv_branch) @ w_out
    x = x + rms_norm(mixed, g_mix_post)
    hm = rms_norm(x, g_mlp_pre)
    ff = jnp.square(jax.nn.relu(hm @ w_up)) @ w_down
    x = x + rms_norm(ff, g_mlp_post)
    return x, new_k, new_v, new_conv


def setup_inputs(seed: int = 0) -> dict:
    key = jax.random.key(seed)
    ks = jax.random.split(key, 20)
    f32 = jnp.float32

    def nrm(k, shape, scale):
        return jax.random.normal(k, shape, f32) * scale

    def gain(k):
        return 1.0 + 0.01 * jax.random.normal(k, (DEPTH, D_MODEL), f32)

    return {
        'x_prompt': nrm(ks[0], (BATCH, SEQ, D_MODEL), 1.0),
        'x_sample': nrm(ks[1], (DEC_BATCH, DEC_SEQ, D_MODEL), 1.0),
        'cache_k': nrm(ks[2], (DEPTH, DEC_BATCH, CACHE_LEN, N_KV_HEADS, HEAD_DIM), 1.0),
        'cache_v': nrm(ks[3], (DEPTH, DEC_BATCH, CACHE_LEN, N_KV_HEADS, HEAD_DIM), 1.0),
        'state_conv': nrm(ks[4], (DEPTH, DEC_BATCH, CONV_WIDTH - 1, CONV_DIM), 1.0),
        'g_mix_pre': gain(ks[5]),
        'g_mix_post': gain(ks[6]),
        'g_mlp_pre': gain(ks[7]),
        'g_mlp_post': gain(ks[8]),
        'w_in': nrm(ks[9], (DEPTH, D_MODEL, IN_DIM), D_MODEL ** -0.5),
        'attn_sinks': nrm(ks[10], (DEPTH, N_HEADS), 0.5),
        'conv_w': nrm(ks[11], (DEPTH, CONV_WIDTH, CONV_DIM), CONV_WIDTH ** -0.5),
        'w_attn_o': nrm(ks[12], (DEPTH, ATTN_DIM, D_MODEL), ATTN_DIM ** -0.5),
        'w_conv_o': nrm(ks[13], (DEPTH, CONV_DIM, D_MODEL), CONV_DIM ** -0.5),
        'w_out': nrm(ks[14], (DEPTH, D_MODEL, D_MODEL), D_MODEL ** -0.5),
        'w_up': nrm(ks[15], (DEPTH, D_MODEL, D_FF), D_MODEL ** -0.5),
        'w_down': nrm(ks[16], (DEPTH, D_FF, D_MODEL), D_FF ** -0.5),
    }


def reference(x_prompt, x_sample, cache_k, cache_v, state_conv, g_mix_pre, g_mix_post,
              g_mlp_pre, g_mlp_post, w_in, attn_sinks, conv_w, w_attn_o, w_conv_o, w_out,
              w_up, w_down):
    yp, ys = x_prompt, x_sample
    kp, vp, cp, kd, vd, cd = [], [], [], [], [], []
    for l in range(DEPTH):
        params = (g_mix_pre[l], g_mix_post[l], g_mlp_pre[l], g_mlp_post[l], w_in[l],
                  attn_sinks[l], conv_w[l], w_attn_o[l], w_conv_o[l], w_out[l], w_up[l], w_down[l])
        yp, nk, nv, nc = decoder_layer(yp, None, None, None, *params)
        kp.append(nk); vp.append(nv); cp.append(nc)
        ys, nk, nv, nc = decoder_layer(ys, cache_k[l], cache_v[l], state_conv[l], *params)
        kd.append(nk); vd.append(nv); cd.append(nc)
    new_k_prompt = jnp.stack(kp)
    new_v_prompt = jnp.stack(vp)
    new_conv_prompt = jnp.stack(cp)
    new_k_sample = jnp.stack(kd)
    new_v_sample = jnp.stack(vd)
    new_conv_sample = jnp.stack(cd)
    return (yp, ys, new_k_prompt, new_v_prompt, new_conv_prompt, new_k_sample, new_v_sample, new_conv_sample)
```

```python
import numpy as np
from contextlib import ExitStack

import concourse.bass as bass
import concourse.mybir as mybir
from concourse.bass_utils import run_bass_kernel_spmd

F32 = mybir.dt.float32
BF16 = mybir.dt.bfloat16
AF = mybir.ActivationFunctionType
ALU = mybir.AluOpType

NCORES = 8
D = 1024
KD = 8
SEQ = 2048
IN_DIM = 4352
DFF = 4096
EPS = 1e-6
TMAX = 768
NSLOT = 4
SLOT_ELEMS = 4096
SAME_ENGINE_SYNC = True

PARTS = [
    dict(tiles=[("p", i) for i in range(0, 6)], blocks=[(0, 384), (384, 384)]),
    dict(tiles=[("p", i) for i in range(6, 12)], blocks=[(0, 384), (384, 384)]),
    dict(tiles=[("p", 12), ("p", 13), ("p", 14), ("p", 15), ("s", 0)], blocks=[(0, 384), (384, 256)]),
]


class Tok:
    __slots__ = ("sem", "val", "know", "eng")

    def __init__(self, sem, val, know, eng):
        self.sem = sem
        self.val = val
        self.know = know
        self.eng = eng


class Sched:
    def __init__(self, nc, es, ring=8):
        self.nc = nc
        self.engs = {"pe": nc.tensor, "act": nc.scalar, "dve": nc.vector, "pool": nc.gpsimd, "sp": nc.sync}
        self.csem = {}
        self.ccnt = {}
        for e in ("pe", "act", "dve", "pool"):
            self.csem[e] = es.enter_context(nc.semaphore("c_" + e))
            self.ccnt[e] = 0
        self.know = {e: {} for e in self.engs}
        self.dring = {}
        self.dpos = {}
        self.dlast = {}
        for q in ("sp", "pool"):
            self.dring[q] = [es.enter_context(nc.semaphore("d_%s%d" % (q, i))) for i in range(ring)]
            self.dpos[q] = 0
        self.lastw = {}
        self.readers = {}
        self.pending = []
        self.all_dma_toks = []
        self.nwaits = 0
        self.ninst = 0

    def _wait(self, eng, tok):
        if eng == "pe" and tok.eng == "pe":
            return
        if tok.val is None:
            raise RuntimeError("dependency on an unsignaled PE op")
        k = self.know[eng]
        sid = id(tok.sem)
        if k.get(sid, 0) >= tok.val:
            return
        own = tok.eng == eng and tok.sem is self.csem.get(eng)
        if own and (eng == "pe" or not SAME_ENGINE_SYNC):
            k[sid] = tok.val
            return
        self.engs[eng].wait_ge(tok.sem, tok.val)
        self.nwaits += 1
        for s, v in tok.know.items():
            if k.get(s, 0) < v:
                k[s] = v
        k[sid] = tok.val

    def _deps(self, reads, writes):
        deps = []
        for c in reads:
            t = self.lastw.get(c)
            if t is not None:
                deps.append(t)
        for c in writes:
            t = self.lastw.get(c)
            if t is not None:
                deps.append(t)
            r = self.readers.get(c)
            if r:
                deps.extend(r.values())
        return deps

    def _record(self, tok, reads, writes):
        for c in writes:
            self.lastw[c] = tok
            self.readers[c] = {}
        for c in reads:
            self.readers.setdefault(c, {})[id(tok.sem)] = tok

    def op(self, eng, fn, reads=(), writes=(), signal=True):
        ps_r = [c for c in reads if c[0] == "ps"]
        if ps_r:
            reads = [c for c in reads if c[0] != "ps"]
            writes = list(writes) + ps_r
        for t in self._deps(reads, writes):
            self._wait(eng, t)
        inst = fn(self.engs[eng])
        self.ninst += 1
        if signal:
            inst.then_inc(self.csem[eng], 1)
            self.ccnt[eng] += 1
            tok = Tok(self.csem[eng], self.ccnt[eng], dict(self.know[eng]), eng)
            if eng == "pe" and self.pending:
                for p in self.pending:
                    p.val = tok.val
                    p.know = tok.know
                self.pending = []
        else:
            assert eng == "pe"
            tok = Tok(self.csem[eng], None, None, eng)
            self.pending.append(tok)
        self._record(tok, reads, writes)
        return tok

    def dma(self, q, out, in_, reads=(), writes=()):
        ring = self.dring[q]
        sem = ring[self.dpos[q] % len(ring)]
        self.dpos[q] += 1
        prev = self.dlast.get(id(sem))
        for t in self._deps(reads, writes):
            self._wait(q, t)
        if prev is not None:
            self._wait(q, prev)
        val = (prev.val if prev is not None else 0) + 16
        self.engs[q].dma_start(out=out, in_=in_).then_inc(sem, 16)
        self.ninst += 1
        tok = Tok(sem, val, dict(self.know[q]), q)
        self.dlast[id(sem)] = tok
        self._record(tok, reads, writes)
        return tok

    def drain(self, q):
        for t in list(self.dlast.values()):
            self._wait(q, t)


def build_program():
    nc = bass.Bass("TRN2", target_bir_lowering=False)

    def din(name, shape):
        return nc.dram_tensor(name, shape, F32, kind="ExternalInput").ap()

    def dout(name, shape):
        return nc.dram_tensor(name, shape, F32, kind="ExternalOutput").ap()

    xp = din("xp", [SEQ, D])
    xs = din("xs", [128, D])
    ck = din("ck", [2, 16, 128, 128])
    cv = din("cv", [2, 16, 128, 128])
    sc = din("sc", [2, 32, 512])
    gvec = [din(n, [2, D]) for n in ("g_mix_pre", "g_mix_post", "g_mlp_pre", "g_mlp_post")]
    w_in = din("w_in", [2, D, IN_DIM])
    sinks = din("attn_sinks", [2, 8])
    conv_w = din("conv_w", [2, 3, 512])
    w_ao = din("w_attn_o", [2, 512, D])
    w_co = din("w_conv_o", [2, 512, D])
    w_out = din("w_out", [2, D, D])
    w_up = din("w_up", [2, D, DFF])
    w_down = din("w_down", [2, DFF, D])

    yp = dout("yp", [SEQ, D])
    ys = dout("ys", [128, D])
    nkp = dout("nkp", [2, 128, 128])
    nvp = dout("nvp", [2, 128, 128])
    ncp = dout("ncp", [2, 2, 512])
    nks = dout("nks", [2, 16, 128, 128])
    nvs = dout("nvs", [2, 16, 128, 128])
    ncs = dout("ncs", [2, 32, 512])

    es = ExitStack()
    with es:
        S = Sched(nc, es)

        def sb(name, shape, dt):
            return es.enter_context(nc.sbuf_tensor(name, shape, dt))

        xT = sb("xT", [128, KD, TMAX], F32)
        HR = sb("HR", [128, KD * TMAX * 2], BF16)
        hT = HR[:, 0:KD * TMAX].rearrange("p (k t) -> p k t", k=KD)
        R2 = HR[:, :].bitcast(F32).rearrange("p (k t) -> p k t", k=KD)
        R1N = 71232 // 2
        R1 = sb("R1", [128, R1N], BF16)

        def carve(off_bytes, nelem, dt):
            if dt == BF16:
                return R1[:, off_bytes // 2: off_bytes // 2 + nelem]
            return R1[:, off_bytes // 2: off_bytes // 2 + nelem * 2].bitcast(F32)

        qT = carve(0, 4 * TMAX, BF16).rearrange("p (c t) -> p c t", c=4)
        aT = carve(6144, 4 * TMAX, BF16).rearrange("p (c t) -> p c t", c=4)
        kT = carve(12288, 128 + TMAX, BF16)
        Vt = carve(14080, 7 * 128, BF16).rearrange("p (s d) -> p s d", s=7)
        Cf = carve(15872, 4 * TMAX, F32).rearrange("p (c t) -> p c t", c=4)
        uf = carve(28160, 4 * (TMAX + 2), F32).rearrange("p (c t) -> p c t", c=4)
        BzT = carve(40512, 4 * TMAX, BF16).rearrange("p (c t) -> p c t", c=4)
        mT = carve(46656, KD * TMAX, BF16).rearrange("p (c t) -> p c t", c=KD)
        gt = carve(58944, 4 * TMAX, F32).rearrange("p (a t) -> p a t", a=4)
        actT = carve(0, 32 * TMAX, BF16).rearrange("p (c t) -> p c t", c=32)
        rtmp = carve(49152, 2 * 384, F32).rearrange("p (a t) -> p a t", a=2)
        xstg = carve(0, 2 * 1024, F32).rearrange("p (a t) -> p a t", a=2)

        wslots = sb("wslots", [128, NSLOT, SLOT_ELEMS], BF16)
        Pt = sb("Pt", [128, 2, 2, 2, 4, 128], BF16)
        Kc = sb("Kc", [128, 16, 128], BF16)
        KcT = sb("KcT", [128, 16, 128], BF16)
        Vc = sb("Vc", [128, 16, 128], BF16)
        sqb = sb("sqb", [128, KD, 384], BF16)
        rt = sb("rt", [128, 384], F32)
        rr = sb("rr", [128, 2, 384], F32)
        identf = sb("identf", [128, 128], F32)
        identb = sb("identb", [128, 128], BF16)
        onesb = sb("onesb", [128, 128], BF16)
        mtmp = sb("mtmp", [128, 4, 128], F32)
        M_prompt = sb("M_prompt", [128, 2, 4, 128], BF16)
        M_first = sb("M_first", [128, 2, 4, 128], BF16)
        M_sample = sb("M_sample", [128, 2, 4, 128], BF16)
        gstage = sb("gstage", [128, 128], F32)
        gcol = sb("gcol", [128, 128], F32)
        esk = sb("esk", [128, 16], F32)
        sinkcol = sb("sinkcol", [128, 2, 4], F32)
        kcarry = sb("kcarry", [128, 2, 128], BF16)
        vcarry = sb("vcarry", [128, 2, 128], BF16)
        ucarry = sb("ucarry", [128, 2, 4, 2], F32)
        us = sb("us", [128, 4, 16, 10], F32)
        scst = sb("scst", [32, 512], F32)
        ncstg = sb("ncstg", [128, 4, 32], F32)
        ncout = sb("ncout", [32, 512], F32)
        kvtok = sb("kvtok", [128, 2, 256], F32)
        dent = sb("dent", [128, 512], F32)
        fence_scr = sb("fence_scr", [128, 4], F32)

        psb = [es.enter_context(nc.psum_tensor("psb%d" % i, [128, 512], F32)) for i in range(8)]
        PB_PROJ = [[0, 1], [2, 3]]
        PB_S = [4, 5]
        PB_O = 6
        PB_D = 7
        PB_ST = [6, 7]

        def PS(b):
            return ("ps", b)

        def fence(cell):
            S.op("dve", lambda e: e.memset(fence_scr[:, 0:1], 0.0), reads=[], writes=[cell, ("fscr",)])

        R1U = ("R1use",)
        HRU = ("HRuse",)

        S.op("pool", lambda e: e.memset(identf[:], 1.0), writes=[("identf",)])
        S.op("pool", lambda e: e.affine_select(out=identf[:], in_=identf[:], pattern=[[-1, 128]],
                                                compare_op=ALU.is_equal, fill=0.0, base=0, channel_multiplier=1),
             reads=[("identf",)], writes=[("identf",)])
        S.op("dve", lambda e: e.tensor_copy(out=identb[:], in_=identf[:]), reads=[("identf",)], writes=[("identb",)])
        S.op("dve", lambda e: e.memset(onesb[:], 1.0), writes=[("onesb",)])
        S.op("pool", lambda e: e.memset(mtmp[:], 1.0), writes=[("mtmp",)])
        S.op("pool", lambda e: e.affine_select(out=mtmp[:, 0, :], in_=mtmp[:, 0, :], pattern=[[1, 128]],
                                                compare_op=ALU.is_ge, fill=0.0, base=0, channel_multiplier=-1),
             reads=[("mtmp",)], writes=[("mtmp",)])
        S.op("pool", lambda e: e.affine_select(out=mtmp[:, 1, :], in_=mtmp[:, 1, :], pattern=[[-1, 128]],
                                                compare_op=ALU.is_gt, fill=0.0, base=0, channel_multiplier=1),
             reads=[("mtmp",)], writes=[("mtmp",)])
        m2v = mtmp[:, 2, :].rearrange("p (s t) -> p s t", s=16)
        S.op("pool", lambda e: e.affine_select(out=m2v, in_=m2v, pattern=[[8, 16], [1, 8]],
                                                compare_op=ALU.is_ge, fill=0.0, base=0, channel_multiplier=-1),
             reads=[("mtmp",)], writes=[("mtmp",)])
        S.op("pool", lambda e: e.affine_select(out=m2v, in_=m2v, pattern=[[-8, 16], [0, 8]],
                                                compare_op=ALU.is_ge, fill=0.0, base=0, channel_multiplier=1),
             reads=[("mtmp",)], writes=[("mtmp",)])
        m3v = mtmp[:, 3, :].rearrange("p (s t) -> p s t", s=16)
        S.op("pool", lambda e: e.affine_select(out=m3v, in_=m3v, pattern=[[0, 16], [-1, 8]],
                                                compare_op=ALU.is_gt, fill=0.0, base=0, channel_multiplier=1),
             reads=[("mtmp",)], writes=[("mtmp",)])
        S.op("dve", lambda e: e.memset(M_first[:, 0], 0.0), writes=[("masks",)])
        for c in range(4):
            S.op("dve", lambda e, c=c: e.tensor_copy(out=M_prompt[:, 0, c, :], in_=mtmp[:, 1, :]), reads=[("mtmp",)], writes=[("masks",)])
            S.op("dve", lambda e, c=c: e.tensor_copy(out=M_prompt[:, 1, c, :], in_=mtmp[:, 0, :]), reads=[("mtmp",)], writes=[("masks",)])
            S.op("dve", lambda e, c=c: e.tensor_copy(out=M_first[:, 1, c, :], in_=mtmp[:, 0, :]), reads=[("mtmp",)], writes=[("masks",)])
            S.op("dve", lambda e, c=c: e.tensor_copy(out=M_sample[:, 0, c, :], in_=mtmp[:, 3, :]), reads=[("mtmp",)], writes=[("masks",)])
            S.op("dve", lambda e, c=c: e.tensor_copy(out=M_sample[:, 1, c, :], in_=mtmp[:, 2, :]), reads=[("mtmp",)], writes=[("masks",)])

        S.op("dve", lambda e: e.memset(gstage[:], 0.0), writes=[("gstage",)])
        for v in range(4):
            S.dma("sp", gstage[v * 16:(v + 1) * 16, :], gvec[v].rearrange("l (k p) -> (l k) p", p=128),
                  writes=[("gstage",)])
        S.dma("sp", gstage[64:88, :], conv_w.rearrange("l i (j p) -> (l i j) p", p=128), writes=[("gstage",)])
        S.op("pe", lambda e: e.transpose(out=psb[6][:, 0:128], in_=gstage[:], identity=identf[:]),
             reads=[("gstage",), ("identf",)], writes=[PS(6)])
        S.op("act", lambda e: e.copy(out=gcol[:], in_=psb[6][:, 0:128]), reads=[PS(6)], writes=[("gcol",)])

        def gc_(vec, l, k):
            i = vec * 16 + l * 8 + k
            return gcol[:, i:i + 1]

        def cw_(l, i, j):
            n = 64 + l * 12 + i * 4 + j
            return gcol[:, n:n + 1]

        S.dma("sp", esk[:], sinks.rearrange("l h -> (l h)").partition_broadcast(128), writes=[("esk",)])
        S.op("act", lambda e: e.activation(out=esk[:], in_=esk[:], func=AF.Exp), reads=[("esk",)], writes=[("esk",)])
        for l in range(2):
            S.op("dve", lambda e, l=l: e.tensor_copy(out=sinkcol[0:64, l, :], in_=esk[0:64, l * 8:l * 8 + 4]),
                 reads=[("esk",)], writes=[("sinkcol",)])
            S.op("dve", lambda e, l=l: e.tensor_copy(out=sinkcol[64:128, l, :], in_=esk[64:128, l * 8 + 4:l * 8 + 8]),
                 reads=[("esk",)], writes=[("sinkcol",)])

        for l in range(2):
            S.dma("sp", nks[l, :, 0:120, :], ck[l, :, 8:128, :])
            S.dma("sp", nvs[l, :, 0:120, :], cv[l, :, 8:128, :])

        def granules(l):
            g = []

            def wq(slot):
                toks = []
                for c in range(4):
                    for hf, h in enumerate((c, 4 + c)):
                        src = w_in[l, :, h * 64:(h + 1) * 64].rearrange("(k p) n -> p k n", p=128)
                        dst = slot[:, 0:KD * 512].rearrange("p (k n) -> p k n", k=KD)[:, :, c * 128 + hf * 64: c * 128 + hf * 64 + 64]
                        toks.append((dst, src))
                return toks
            g.append(("q", wq))

            def cols(c0, n, K=KD, src_t=None):
                def f(slot, c0=c0, n=n):
                    src = w_in[l, :, c0:c0 + n].rearrange("(k p) n -> p k n", p=128)
                    dst = slot[:, 0:KD * n].rearrange("p (k n) -> p k n", k=KD)
                    return [(dst, src)]
                return f
            g.append(("kv", cols(512, 256)))
            g.append(("C", cols(1280, 512)))
            g.append(("u", cols(1792, 512)))
            g.append(("B", cols(768, 512)))
            for m in range(2):
                g.append(("ga%d" % m, cols(2304 + m * 512, 512)))

                def wo(slot, m=m):
                    dst = slot[:, 0:KD * 512].rearrange("p (k n) -> p k n", k=KD)
                    r = []
                    r.append((dst[0:64, 0:4, :], w_ao[l, 0:256, m * 512:(m + 1) * 512].rearrange("(c p) n -> p c n", p=64)))
                    r.append((dst[64:128, 0:4, :], w_ao[l, 256:512, m * 512:(m + 1) * 512].rearrange("(c p) n -> p c n", p=64)))
                    r.append((dst[:, 4:8, :], w_co[l, :, m * 512:(m + 1) * 512].rearrange("(k p) n -> p k n", p=128)))
                    return r
                g.append(("aoco%d" % m, wo))
                g.append(("gc%d" % m, cols(3328 + m * 512, 512)))
            for m in range(2):
                def wout(slot, m=m):
                    return [(slot[:, 0:KD * 512].rearrange("p (k n) -> p k n", k=KD),
                             w_out[l, :, m * 512:(m + 1) * 512].rearrange("(k p) n -> p k n", p=128))]
                g.append(("wout%d" % m, wout))
            for m in range(8):
                def wup(slot, m=m):
                    return [(slot[:, 0:KD * 512].rearrange("p (k n) -> p k n", k=KD),
                             w_up[l, :, m * 512:(m + 1) * 512].rearrange("(k p) n -> p k n", p=128))]
                g.append(("wup%d" % m, wup))
            for i in range(8):
                def wdn(slot, i=i):
                    return [(slot[:, 0:32 * 128].rearrange("p (k n) -> p k n", k=32),
                             w_down[l, :, i * 128:(i + 1) * 128].rearrange("(k p) n -> p k n", p=128))]
                g.append(("wdn%d" % i, wdn))
            return g

        gran_seq = []
        for pi in range(len(PARTS)):
            for l in range(2):
                for name, f in granules(l):
                    gran_seq.append((pi, l, name, f))
        wstate = dict(issued=0, cur=0)

        def w_issue():
            i = wstate["issued"]
            if i >= len(gran_seq):
                return
            slot_i = i % NSLOT
            slot = wslots[:, slot_i, :]
            for dst, src in gran_seq[i][3](slot):
                S.dma("pool", dst, src, writes=[("w", slot_i)])
            wstate["issued"] += 1

        for _ in range(NSLOT):
            w_issue()

        def w_next(pi, l, name):
            i = wstate["cur"]
            assert gran_seq[i][:3] == (pi, l, name), (gran_seq[i][:3], (pi, l, name))
            wstate["cur"] += 1
            return i % NSLOT

        def w_release():
            w_issue()

        par_state = dict(p=0)

        def proj_job(blocks, K, lhs_fn, rhs_fn, reads_fn, epilogue, wcell):
            par = par_state["p"]
            par_state["p"] ^= 1
            for b, (c0, n) in enumerate(blocks):
                bank = PB_PROJ[par][b]
                for k in range(K):
                    S.op("pe", lambda e, k=k, bank=bank, c0=c0, n=n: e.matmul(
                        psb[bank][:, 0:n], lhsT=lhs_fn(k), rhs=rhs_fn(k, c0, n), start=(k == 0), stop=(k == K - 1)),
                        reads=[wcell] + reads_fn(k, b), writes=[PS(bank)], signal=(k == K - 1))
                epilogue(b, c0, n, psb[bank][:, 0:n], PS(bank))

        for pi, part in enumerate(PARTS):
            tiles = part["tiles"]
            blocks = part["blocks"]
            NT = len(tiles)
            T = NT * 128
            has_sample = tiles[-1][0] == "s"
            NTP = NT - 1 if has_sample else NT
            TP = NTP * 128
            scol = TP

            def blk_of_tile(ti):
                c = ti * 128
                for b, (c0, n) in enumerate(blocks):
                    if c0 <= c < c0 + n:
                        return b
                raise AssertionError

            def tiles_of_blk(b):
                c0, n = blocks[b]
                return list(range(c0 // 128, (c0 + n) // 128))

            fence(R1U)
            for ti, (kind, g) in enumerate(tiles):
                src = xp[g * 128:(g + 1) * 128, :] if kind == "p" else xs[:, :]
                sbuf_i = ti % 2
                S.dma("sp", xstg[:, sbuf_i, :], src, reads=[R1U], writes=[("xstg", sbuf_i)])
                b = blk_of_tile(ti)
                for half in range(2):
                    bank = PB_ST[half]
                    for kk in range(4):
                        k = half * 4 + kk
                        S.op("pe", lambda e, k=k, kk=kk, bank=bank, sbuf_i=sbuf_i: e.transpose(
                            out=psb[bank][:, kk * 128:(kk + 1) * 128], in_=xstg[:, sbuf_i, k * 128:(k + 1) * 128],
                            identity=identf[:]),
                            reads=[("xstg", sbuf_i), ("identf",), R1U], writes=[PS(bank)], signal=(kk == 3))
                    S.op("act", lambda e, half=half, bank=bank, ti=ti: e.copy(
                        out=xT[:, half * 4:(half + 1) * 4, ti * 128:(ti + 1) * 128],
                        in_=psb[bank][:, :].rearrange("p (k t) -> p k t", k=4)),
                        reads=[PS(bank)], writes=[("x", half * 4 + kk, b) for kk in range(4)])

            def stats(src, cell, b):
                c0, n = blocks[b]
                S.op("act", lambda e: e.activation(out=sqb[:, :, 0:n], in_=src[:, :, c0:c0 + n], func=AF.Square),
                     reads=[(cell, k, b) for k in range(KD)] + [HRU], writes=[("sq",)])
                bank = PB_ST[b]
                for k in range(KD):
                    S.op("pe", lambda e, k=k: e.matmul(psb[bank][:, 0:n], lhsT=onesb[:], rhs=sqb[:, k, 0:n],
                                                       start=(k == 0), stop=(k == KD - 1)),
                         reads=[("sq",), ("onesb",)], writes=[PS(bank)], signal=(k == KD - 1))
                S.op("act", lambda e: e.activation(out=rt[:, 0:n], in_=psb[bank][:, 0:n], func=AF.Sqrt,
                                                   bias=EPS, scale=1.0 / D),
                     reads=[PS(bank)], writes=[("rt",)])
                S.op("dve", lambda e: e.reciprocal(out=rr[:, b, 0:n], in_=rt[:, 0:n]), reads=[("rt",)], writes=[("r", b)])

            def make_h(vec, l):
                fence(HRU)
                for b, (c0, n) in enumerate(blocks):
                    stats(xT, "x", b)
                    for k in range(KD):
                        S.op("dve", lambda e, k=k, b=b, c0=c0, n=n: e.scalar_tensor_tensor(
                            out=hT[:, k, c0:c0 + n], in0=xT[:, k, c0:c0 + n], scalar=gc_(vec, l, k),
                            in1=rr[:, b, 0:n], op0=ALU.mult, op1=ALU.mult),
                            reads=[("x", k, b), ("r", b), ("gcol",), HRU], writes=[("h", k, b)])

            def post_norm(vec, l):
                for b, (c0, n) in enumerate(blocks):
                    stats(R2, "r2", b)
                    for k in range(KD):
                        S.op("dve", lambda e, k=k, b=b, c0=c0, n=n: e.scalar_tensor_tensor(
                            out=R2[:, k, c0:c0 + n], in0=R2[:, k, c0:c0 + n], scalar=gc_(vec, l, k),
                            in1=rr[:, b, 0:n], op0=ALU.mult, op1=ALU.mult),
                            reads=[("r2", k, b), ("r", b), ("gcol",), HRU], writes=[("r2", k, b)])
                        S.op("dve", lambda e, k=k, b=b, c0=c0, n=n: e.tensor_tensor(
                            out=xT[:, k, c0:c0 + n], in0=xT[:, k, c0:c0 + n], in1=R2[:, k, c0:c0 + n], op=ALU.add),
                            reads=[("r2", k, b), ("x", k, b), HRU], writes=[("x", k, b)])

            for l in range(2):
                fence(R1U)
                make_h(0, l)

                if pi == 0:
                    S.op("dve", lambda e: e.memset(kT[:, 0:128], 0.0), reads=[R1U], writes=[("k", -1)])
                    S.op("dve", lambda e: e.memset(Vt[:, 0, :], 0.0), reads=[R1U], writes=[("v", -1)])
                    S.op("dve", lambda e: e.memset(uf[:, :, 0:2], 0.0), reads=[R1U], writes=[("upre",)])
                else:
                    S.op("dve", lambda e: e.tensor_copy(out=kT[:, 0:128], in_=kcarry[:, l, :]),
                         reads=[("kcarry", l), R1U], writes=[("k", -1)])
                    S.op("dve", lambda e: e.tensor_copy(out=Vt[:, 0, :], in_=vcarry[:, l, :]),
                         reads=[("vcarry", l), R1U], writes=[("v", -1)])
                    S.op("dve", lambda e: e.tensor_copy(out=uf[:, :, 0:2], in_=ucarry[:, l, :, :]),
                         reads=[("ucarry", l), R1U], writes=[("upre",)])

                if has_sample:
                    S.dma("pool", Kc[:], ck[l].rearrange("s k d -> k s d"), writes=[("Kc",)])
                    S.dma("pool", Vc[:], cv[l].rearrange("s k d -> k s d"), writes=[("Vc",)])
                    for grp in range(2):
                        bank = PB_ST[grp]
                        pbf = psb[bank][:, :].bitcast(BF16).rearrange("p (s k) -> p s k", s=8)
                        for s8 in range(8):
                            s = grp * 8 + s8
                            S.op("pe", lambda e, s=s, s8=s8, pbf=pbf: e.transpose(out=pbf[:, s8, :], in_=Kc[:, s, :], identity=identb[:]),
                                 reads=[("Kc",), ("identb",)], writes=[PS(bank)], signal=(s8 == 7))
                        S.op("act", lambda e, grp=grp, pbf=pbf: e.copy(out=KcT[:, grp * 8:(grp + 1) * 8, :], in_=pbf),
                             reads=[PS(bank)], writes=[("KcT",)])
                    S.dma("sp", scst[:], sc[l], writes=[("scst",)])
                    bank = PB_ST[0]
                    for j in range(4):
                        S.op("pe", lambda e, j=j: e.transpose(out=psb[bank][:, j * 32:(j + 1) * 32], in_=scst[:, j * 128:(j + 1) * 128],
                                                              identity=identf[0:32, 0:32]),
                             reads=[("scst",), ("identf",)], writes=[PS(bank)], signal=(j == 3))
                    S.op("act", lambda e: e.copy(out=us[:, :, :, 0:2],
                                                 in_=psb[bank][:, 0:128].rearrange("p (j s r) -> p j s r", j=4, s=16)),
                         reads=[PS(bank)], writes=[("us",)])

                sl = w_next(pi, l, "q")
                wq = wslots[:, sl, 0:KD * 512].rearrange("p (k n) -> p k n", k=KD)
                for c in range(4):
                    def ep(b, c0, n, ps, pcell, c=c):
                        S.op("act", lambda e: e.copy(out=qT[:, c, c0:c0 + n], in_=ps), reads=[pcell, R1U],
                             writes=[("q", c, t) for t in tiles_of_blk(b)])
                    proj_job(blocks, KD, lambda k, c=c: wq[:, k, c * 128:(c + 1) * 128],
                             lambda k, c0, n: hT[:, k, c0:c0 + n], lambda k, b: [("h", k, b), HRU], ep, ("w", sl))
                w_release()
                sl = w_next(pi, l, "kv")
                wkv = wslots[:, sl, 0:KD * 256].rearrange("p (k n) -> p k n", k=KD)

                def ep(b, c0, n, ps, pcell):
                    S.op("act", lambda e: e.copy(out=kT[:, 128 + c0:128 + c0 + n], in_=ps), reads=[pcell, R1U],
                         writes=[("k", t) for t in tiles_of_blk(b)])
                proj_job(blocks, KD, lambda k: wkv[:, k, 0:128], lambda k, c0, n: hT[:, k, c0:c0 + n],
                         lambda k, b: [("h", k, b), HRU], ep, ("w", sl))
                for ti, (kind, g) in enumerate(tiles):
                    b = blk_of_tile(ti)
                    need_out = (kind == "s") or (kind == "p" and g == 15)
                    bank = PB_ST[ti % 2]
                    c0w, nw = (0, 256) if need_out else (128, 128)
                    for k in range(KD):
                        S.op("pe", lambda e, k=k, ti=ti, bank=bank, c0w=c0w, nw=nw: e.matmul(
                            psb[bank][:, 0:nw], lhsT=hT[:, k, ti * 128:(ti + 1) * 128], rhs=wkv[:, k, c0w:c0w + nw],
                            start=(k == 0), stop=(k == KD - 1)),
                            reads=[("w", sl), ("h", k, b), HRU], writes=[PS(bank)], signal=(k == KD - 1))
                    voff = 128 if need_out else 0
                    S.op("act", lambda e, ti=ti, bank=bank, voff=voff: e.copy(out=Vt[:, 1 + ti, :], in_=psb[bank][:, voff:voff + 128]),
                         reads=[PS(bank), R1U], writes=[("v", ti)])
                    if need_out:
                        kb = 0 if kind == "p" else 1
                        S.op("dve", lambda e, bank=bank, kb=kb: e.tensor_copy(out=kvtok[:, kb, :], in_=psb[bank][:, 0:256]),
                             reads=[PS(bank)], writes=[("kvtok", kb)])
                        if kind == "p":
                            S.dma("sp", nkp[l], kvtok[:, kb, 0:128], reads=[("kvtok", kb)])
                            S.dma("sp", nvp[l], kvtok[:, kb, 128:256], reads=[("kvtok", kb)])
                        else:
                            for s in range(16):
                                S.dma("sp", nks[l, s, 120:128, :], kvtok[s * 8:(s + 1) * 8, kb, 0:128], reads=[("kvtok", kb)])
                                S.dma("sp", nvs[l, s, 120:128, :], kvtok[s * 8:(s + 1) * 8, kb, 128:256], reads=[("kvtok", kb)])
                w_release()

                def att_S(ti):
                    kind, g = tiles[ti]
                    buf = ti % 2
                    tcols = slice(ti * 128, (ti + 1) * 128)
                    prevc = slice(ti * 128, (ti + 1) * 128)
                    ownc = slice(128 + ti * 128, 128 + (ti + 1) * 128)
                    for gi in range(2):
                        for hf in range(2):
                            bank = PB_S[hf]
                            rows = slice(hf * 64, (hf + 1) * 64)
                            for c2 in range(2):
                                c = gi * 2 + c2
                                base = c2 * 256
                                if kind == "p":
                                    S.op("pe", lambda e, c=c, base=base: e.matmul(
                                        psb[bank][:, base:base + 128], lhsT=kT[rows, prevc], rhs=qT[rows, c, tcols],
                                        start=True, stop=True),
                                        reads=[("k", ti - 1), ("q", c, ti), R1U], writes=[PS(bank)], signal=False)
                                else:
                                    for s in range(16):
                                        S.op("pe", lambda e, c=c, base=base, s=s: e.matmul(
                                            psb[bank][:, base + s * 8:base + s * 8 + 8], lhsT=KcT[rows, s, :],
                                            rhs=qT[rows, c, scol + s * 8:scol + s * 8 + 8], start=True, stop=True),
                                            reads=[("KcT",), ("q", c, ti), R1U], writes=[PS(bank)], signal=False)
                                S.op("pe", lambda e, c=c, base=base: e.matmul(
                                    psb[bank][:, base + 128:base + 256], lhsT=kT[rows, ownc], rhs=qT[rows, c, tcols],
                                    start=True, stop=True),
                                    reads=[("k", ti), ("q", c, ti), R1U], writes=[PS(bank)], signal=(c2 == 1))
                            S.op("act", lambda e, gi=gi, hf=hf, bank=bank: e.activation(
                                out=Pt[:, buf, hf, :, gi * 2:gi * 2 + 2, :].rearrange("p a c q -> p c a q"),
                                in_=psb[bank][:, :].rearrange("p (c a q) -> p c a q", c=2, a=2),
                                func=AF.Exp, scale=0.125),
                                reads=[PS(bank)], writes=[("Pt", buf, hf, gi)])
                    if kind == "s":
                        M = M_sample
                    elif g == 0:
                        M = M_first
                    else:
                        M = M_prompt
                    for hf in range(2):
                        S.op("dve", lambda e, hf=hf, M=M: e.tensor_tensor(
                            out=Pt[:, buf, hf], in0=Pt[:, buf, hf], in1=M[:], op=ALU.mult),
                            reads=[("Pt", buf, hf, 0), ("Pt", buf, hf, 1), ("masks",)],
                            writes=[("Pt", buf, hf, 0), ("Pt", buf, hf, 1)])

                def att_O(ti):
                    kind, g = tiles[ti]
                    buf = ti % 2
                    for hf in range(2):
                        rows = slice(hf * 64, (hf + 1) * 64)
                        pcells = [("Pt", buf, hf, 0), ("Pt", buf, hf, 1)]
                        if kind == "p":
                            S.op("pe", lambda e, hf=hf: e.matmul(
                                psb[PB_O][rows, :], lhsT=Vt[:, ti, hf * 64:(hf + 1) * 64],
                                rhs=Pt[:, buf, hf, 0].rearrange("p c q -> p (c q)"), start=True, stop=False),
                                reads=pcells + [("v", ti - 1), R1U], writes=[PS(PB_O)], signal=False)
                            S.op("pe", lambda e, hf=hf: e.matmul(
                                psb[PB_O][rows, :], lhsT=Vt[:, 1 + ti, hf * 64:(hf + 1) * 64],
                                rhs=Pt[:, buf, hf, 1].rearrange("p c q -> p (c q)"), start=False, stop=True),
                                reads=pcells + [("v", ti), R1U], writes=[PS(PB_O)], signal=False)
                        else:
                            S.op("pe", lambda e, hf=hf: e.matmul(
                                psb[PB_O][rows, :], lhsT=Vt[:, 1 + ti, hf * 64:(hf + 1) * 64],
                                rhs=Pt[:, buf, hf, 1].rearrange("p c (s t) -> p s c t", s=16), start=True, stop=False),
                                reads=pcells + [("v", ti), R1U], writes=[PS(PB_O)], signal=False)
                            for s in range(16):
                                S.op("pe", lambda e, hf=hf, s=s: e.matmul(
                                    psb[PB_O][rows, s * 32:(s + 1) * 32],
                                    lhsT=Vc[:, s, hf * 64:(hf + 1) * 64],
                                    rhs=Pt[:, buf, hf, 0, :, s * 8:(s + 1) * 8], start=False, stop=(s == 15)),
                                    reads=pcells + [("Vc",)], writes=[PS(PB_O)], signal=False)
                        if kind == "p":
                            r0 = Pt[:, buf, hf, 0].rearrange("p c q -> p (c q)")
                            r1 = Pt[:, buf, hf, 1].rearrange("p c q -> p (c q)")
                        else:
                            r0 = Pt[:, buf, hf, 0].rearrange("p c (s t) -> p s c t", s=16)
                            r1 = Pt[:, buf, hf, 1].rearrange("p c (s t) -> p s c t", s=16)
                        S.op("pe", lambda e, hf=hf, r0=r0: e.matmul(
                            psb[PB_D][rows, :], lhsT=onesb[:, 0:64], rhs=r0, start=True, stop=False),
                            reads=pcells + [("onesb",)], writes=[PS(PB_D)], signal=False)
                        S.op("pe", lambda e, hf=hf, r1=r1: e.matmul(
                            psb[PB_D][rows, :], lhsT=onesb[:, 0:64], rhs=r1, start=False, stop=True),
                            reads=pcells + [("onesb",)], writes=[PS(PB_D)], signal=(hf == 1))
                    if kind == "p":
                        dview = lambda ap: ap.rearrange("p (c q) -> p c q", c=4)
                        aview = aT[:, :, ti * 128:(ti + 1) * 128]
                    else:
                        dview = lambda ap: ap.rearrange("p (s c t) -> p c s t", s=16, c=4)
                        aview = aT[:, :, ti * 128:(ti + 1) * 128].rearrange("p c (s t) -> p c s t", s=16)
                    for c in range(4):
                        S.op("dve", lambda e, c=c: e.tensor_scalar(
                            out=dview(dent[:])[:, c], in0=dview(psb[PB_D][:, :])[:, c],
                            scalar1=sinkcol[:, l, c:c + 1], scalar2=None, op0=ALU.add),
                            reads=[PS(PB_D), ("sinkcol",)], writes=[("dent",)])
                    S.op("dve", lambda e: e.reciprocal(out=dent[:], in_=dent[:]), reads=[("dent",)], writes=[("dent",)])
                    S.op("dve", lambda e: e.tensor_tensor(
                        out=aview, in0=dview(psb[PB_O][:, :]), in1=dview(dent[:]), op=ALU.mult),
                        reads=[PS(PB_O), ("dent",), R1U], writes=[("a", ti)])

                att_stages = []
                for ti in range(NT):
                    att_stages.append(("S", ti))
                    if ti >= 1:
                        att_stages.append(("O", ti - 1))
                att_stages.append(("O", NT - 1))

                conv_jobs = []
                slC = dict()

                def job_C(j):
                    if j == 0:
                        slC["C"] = w_next(pi, l, "C")
                    sl_ = slC["C"]
                    wC = wslots[:, sl_, 0:KD * 512].rearrange("p (k n) -> p k n", k=KD)

                    def ep(b, c0, n, ps, pcell):
                        S.op("act", lambda e: e.copy(out=Cf[:, j, c0:c0 + n], in_=ps), reads=[pcell, R1U],
                             writes=[("C", j, b)])
                    proj_job(blocks, KD, lambda k: wC[:, k, j * 128:(j + 1) * 128], lambda k, c0, n: hT[:, k, c0:c0 + n],
                             lambda k, b: [("h", k, b), HRU], ep, ("w", sl_))
                    if j == 3:
                        w_release()

                def job_u(j):
                    if j == 0:
                        slC["u"] = w_next(pi, l, "u")
                    sl_ = slC["u"]
                    wU = wslots[:, sl_, 0:KD * 512].rearrange("p (k n) -> p k n", k=KD)

                    def ep(b, c0, n, ps, pcell):
                        S.op("dve", lambda e: e.tensor_tensor(out=uf[:, j, 2 + c0:2 + c0 + n], in0=ps, in1=Cf[:, j, c0:c0 + n],
                                                              op=ALU.mult),
                             reads=[pcell, ("C", j, b), R1U], writes=[("u", j, b)])
                    proj_job(blocks, KD, lambda k: wU[:, k, j * 128:(j + 1) * 128], lambda k, c0, n: hT[:, k, c0:c0 + n],
                             lambda k, b: [("h", k, b), HRU], ep, ("w", sl_))
                    if j == 3:
                        w_release()
                    allb = list(range(len(blocks)))
                    ucells = [("u", j, b) for b in allb] + [("upre",)]
                    ccells = [("C", j, b) for b in allb]
                    S.op("dve", lambda e: e.tensor_scalar(out=Cf[:, j, 0:TP], in0=uf[:, j, 0:TP], scalar1=cw_(l, 0, j),
                                                          scalar2=None, op0=ALU.mult),
                         reads=ucells + [("gcol",), R1U], writes=ccells)
                    for i in (1, 2):
                        S.op("dve", lambda e, i=i: e.scalar_tensor_tensor(
                            out=Cf[:, j, 0:TP], in0=uf[:, j, i:i + TP], scalar=cw_(l, i, j), in1=Cf[:, j, 0:TP],
                            op0=ALU.mult, op1=ALU.add),
                            reads=ucells + ccells + [("gcol",), R1U], writes=ccells)
                    if has_sample:
                        S.op("dve", lambda e: e.tensor_copy(
                            out=us[:, j, :, 2:10], in_=uf[:, j, 2 + scol:2 + scol + 128].rearrange("p (s t) -> p s t", s=16)),
                            reads=ucells + [R1U], writes=[("us",)])
                        zs = Cf[:, j, scol:scol + 128].rearrange("p (s t) -> p s t", s=16)
                        S.op("dve", lambda e: e.tensor_scalar(out=zs, in0=us[:, j, :, 0:8], scalar1=cw_(l, 0, j),
                                                              scalar2=None, op0=ALU.mult),
                             reads=[("us",), ("gcol",), R1U], writes=ccells)
                        for i in (1, 2):
                            S.op("dve", lambda e, i=i: e.scalar_tensor_tensor(
                                out=zs, in0=us[:, j, :, i:i + 8], scalar=cw_(l, i, j), in1=zs, op0=ALU.mult, op1=ALU.add),
                                reads=[("us",), ("gcol",), R1U] + ccells, writes=ccells)
                        S.op("dve", lambda e: e.tensor_copy(out=ncstg[:, j, :].rearrange("p (s r) -> p s r", s=16),
                                                            in_=us[:, j, :, 8:10]),
                             reads=[("us",)], writes=[("ncstg", j)])
                    if pi < len(PARTS) - 1:
                        S.op("dve", lambda e: e.tensor_copy(out=ucarry[:, l, j, :], in_=uf[:, j, TP:TP + 2]),
                             reads=ucells + [R1U], writes=[("ucarry", l)])

                def job_B(j):
                    if j == 0:
                        slC["B"] = w_next(pi, l, "B")
                    sl_ = slC["B"]
                    wB = wslots[:, sl_, 0:KD * 512].rearrange("p (k n) -> p k n", k=KD)

                    def ep(b, c0, n, ps, pcell):
                        S.op("dve", lambda e: e.tensor_tensor(out=BzT[:, j, c0:c0 + n], in0=ps, in1=Cf[:, j, c0:c0 + n],
                                                              op=ALU.mult),
                             reads=[pcell, ("C", j, b), R1U], writes=[("bz", j, b)])
                    proj_job(blocks, KD, lambda k: wB[:, k, j * 128:(j + 1) * 128], lambda k, c0, n: hT[:, k, c0:c0 + n],
                             lambda k, b: [("h", k, b), HRU], ep, ("w", sl_))
                    if j == 3:
                        w_release()

                for j in range(4):
                    conv_jobs.append(lambda j=j: job_C(j))
                for j in range(4):
                    conv_jobs.append(lambda j=j: job_u(j))
                for j in range(4):
                    conv_jobs.append(lambda j=j: job_B(j))

                ai = 0
                ji = 0
                while ai < len(att_stages) or ji < len(conv_jobs):
                    if ai < len(att_stages):
                        kind_, ti_ = att_stages[ai]
                        (att_S if kind_ == "S" else att_O)(ti_)
                        ai += 1
                    if ji < len(conv_jobs):
                        conv_jobs[ji]()
                        ji += 1

                if pi < len(PARTS) - 1:
                    S.op("dve", lambda e: e.tensor_copy(out=kcarry[:, l, :], in_=kT[:, TP:TP + 128]),
                         reads=[("k", NTP - 1), R1U], writes=[("kcarry", l)])
                    S.op("dve", lambda e: e.tensor_copy(out=vcarry[:, l, :], in_=Vt[:, NTP, :]),
                         reads=[("v", NTP - 1), R1U], writes=[("vcarry", l)])

                if has_sample:
                    bank = PB_ST[0]
                    for j in range(4):
                        S.op("pe", lambda e, j=j: e.transpose(out=psb[bank][0:2, j * 128:(j + 1) * 128],
                                                              in_=uf[:, j, TP:TP + 2], identity=identf[:]),
                             reads=[("u", j, b) for b in range(len(blocks))] + [("identf",), R1U], writes=[PS(bank)],
                             signal=(j == 3))
                    S.op("act", lambda e: e.copy(out=ncout[0:2, :], in_=psb[bank][0:2, :]), reads=[PS(bank)], writes=[("ncout",)])
                    S.dma("sp", ncp[l], ncout[0:2, :], reads=[("ncout",)])
                    bank = PB_ST[1]
                    for j in range(4):
                        S.op("pe", lambda e, j=j: e.transpose(out=psb[bank][0:32, j * 128:(j + 1) * 128],
                                                              in_=ncstg[:, j, :], identity=identf[:]),
                             reads=[("ncstg", j), ("identf",)], writes=[PS(bank)], signal=(j == 3))
                    S.op("act", lambda e: e.copy(out=ncout[:, :], in_=psb[bank][0:32, :]), reads=[PS(bank)], writes=[("ncout",)])
                    S.dma("sp", ncs[l], ncout[:, :], reads=[("ncout",)])

                for m in range(2):
                    sl_ga = w_next(pi, l, "ga%d" % m)
                    sl_oc = w_next(pi, l, "aoco%d" % m)
                    sl_gc = w_next(pi, l, "gc%d" % m)
                    wga = wslots[:, sl_ga, 0:KD * 512].rearrange("p (k n) -> p k n", k=KD)
                    woc = wslots[:, sl_oc, 0:KD * 512].rearrange("p (k n) -> p k n", k=KD)
                    wgc = wslots[:, sl_gc, 0:KD * 512].rearrange("p (k n) -> p k n", k=KD)
                    for jj in range(4):
                        j = m * 4 + jj
                        gp = j % 2
                        A = gp * 2
                        Cc = gp * 2 + 1

                        def ep_ga(b, c0, n, ps, pcell, A=A):
                            S.op("act", lambda e: e.activation(out=gt[:, A, c0:c0 + n], in_=ps, func=AF.Sigmoid),
                                 reads=[pcell, R1U], writes=[("gt", A, b)])
                        proj_job(blocks, KD, lambda k, jj=jj: wga[:, k, jj * 128:(jj + 1) * 128],
                                 lambda k, c0, n: hT[:, k, c0:c0 + n], lambda k, b: [("h", k, b), HRU], ep_ga, ("w", sl_ga))

                        def ep_a(b, c0, n, ps, pcell, A=A):
                            S.op("dve", lambda e: e.tensor_tensor(out=gt[:, A, c0:c0 + n], in0=ps, in1=gt[:, A, c0:c0 + n],
                                                                  op=ALU.mult),
                                 reads=[pcell, ("gt", A, b), R1U], writes=[("gt", A, b)])
                        proj_job(blocks, 4, lambda k, jj=jj: woc[:, k, jj * 128:(jj + 1) * 128],
                                 lambda k, c0, n: aT[:, k, c0:c0 + n],
                                 lambda k, b: [("a", t) for t in tiles_of_blk(b)] + [R1U], ep_a, ("w", sl_oc))

                        def ep_gc(b, c0, n, ps, pcell, Cc=Cc):
                            S.op("act", lambda e: e.activation(out=gt[:, Cc, c0:c0 + n], in_=ps, func=AF.Sigmoid),
                                 reads=[pcell, R1U], writes=[("gt", Cc, b)])
                        proj_job(blocks, KD, lambda k, jj=jj: wgc[:, k, jj * 128:(jj + 1) * 128],
                                 lambda k, c0, n: hT[:, k, c0:c0 + n], lambda k, b: [("h", k, b), HRU], ep_gc, ("w", sl_gc))

                        def ep_c(b, c0, n, ps, pcell, A=A, Cc=Cc, j=j):
                            S.op("dve", lambda e: e.tensor_tensor(out=gt[:, Cc, c0:c0 + n], in0=ps, in1=gt[:, Cc, c0:c0 + n],
                                                                  op=ALU.mult),
                                 reads=[pcell, ("gt", Cc, b), R1U], writes=[("gt", Cc, b)])
                            S.op("dve", lambda e: e.tensor_tensor(out=mT[:, j, c0:c0 + n], in0=gt[:, A, c0:c0 + n],
                                                                  in1=gt[:, Cc, c0:c0 + n], op=ALU.add),
                                 reads=[("gt", A, b), ("gt", Cc, b), R1U], writes=[("m", j, b)])
                        proj_job(blocks, 4, lambda k, jj=jj: woc[:, 4 + k, jj * 128:(jj + 1) * 128],
                                 lambda k, c0, n: BzT[:, k, c0:c0 + n],
                                 lambda k, b: [("bz", k, b), R1U], ep_c, ("w", sl_oc))
                    w_release()
                    w_release()
                    w_release()

                fence(HRU)
                for m in range(2):
                    sl_ = w_next(pi, l, "wout%d" % m)
                    wo_ = wslots[:, sl_, 0:KD * 512].rearrange("p (k n) -> p k n", k=KD)
                    for jj in range(4):
                        i = m * 4 + jj

                        def ep(b, c0, n, ps, pcell, i=i):
                            S.op("act", lambda e: e.copy(out=R2[:, i, c0:c0 + n], in_=ps), reads=[pcell, HRU],
                                 writes=[("r2", i, b)])
                        proj_job(blocks, KD, lambda k, jj=jj: wo_[:, k, jj * 128:(jj + 1) * 128],
                                 lambda k, c0, n: mT[:, k, c0:c0 + n], lambda k, b: [("m", k, b), R1U], ep, ("w", sl_))
                    w_release()
                post_norm(1, l)

                fence(R1U)
                make_h(2, l)
                for m in range(8):
                    sl_ = w_next(pi, l, "wup%d" % m)
                    wu_ = wslots[:, sl_, 0:KD * 512].rearrange("p (k n) -> p k n", k=KD)
                    for jj in range(4):
                        f = m * 4 + jj

                        def ep(b, c0, n, ps, pcell, f=f):
                            S.op("act", lambda e: e.activation(out=rtmp[:, b, 0:n], in_=ps, func=AF.Relu),
                                 reads=[pcell, R1U], writes=[("rtmp", b)])
                            S.op("dve", lambda e: e.tensor_tensor(out=actT[:, f, c0:c0 + n], in0=rtmp[:, b, 0:n],
                                                                  in1=rtmp[:, b, 0:n], op=ALU.mult),
                                 reads=[("rtmp", b), R1U], writes=[("act", f, b)])
                        proj_job(blocks, KD, lambda k, jj=jj: wu_[:, k, jj * 128:(jj + 1) * 128],
                                 lambda k, c0, n: hT[:, k, c0:c0 + n], lambda k, b: [("h", k, b), HRU], ep, ("w", sl_))
                    w_release()
                fence(HRU)
                for i in range(8):
                    sl_ = w_next(pi, l, "wdn%d" % i)
                    wd_ = wslots[:, sl_, 0:32 * 128].rearrange("p (k n) -> p k n", k=32)

                    def ep(b, c0, n, ps, pcell, i=i):
                        S.op("act", lambda e: e.copy(out=R2[:, i, c0:c0 + n], in_=ps), reads=[pcell, HRU],
                             writes=[("r2", i, b)])
                    proj_job(blocks, 32, lambda k: wd_[:, k, :], lambda k, c0, n: actT[:, k, c0:c0 + n],
                             lambda k, b: [("act", k, b), R1U], ep, ("w", sl_))
                    w_release()
                post_norm(3, l)

            fence(R1U)
            for ti, (kind, g) in enumerate(tiles):
                b = blk_of_tile(ti)
                sbuf_i = ti % 2
                for half in range(2):
                    bank = PB_ST[half]
                    for kk in range(4):
                        k = half * 4 + kk
                        S.op("pe", lambda e, k=k, kk=kk, bank=bank, ti=ti: e.transpose(
                            out=psb[bank][:, kk * 128:(kk + 1) * 128], in_=xT[:, k, ti * 128:(ti + 1) * 128],
                            identity=identf[:]),
                            reads=[("x", k, b), ("identf",)], writes=[PS(bank)], signal=(kk == 3))
                    S.op("act", lambda e, half=half, bank=bank, sbuf_i=sbuf_i: e.copy(
                        out=xstg[:, sbuf_i, half * 512:(half + 1) * 512], in_=psb[bank][:, :]),
                        reads=[PS(bank), R1U], writes=[("xstg", sbuf_i)])
                dst = yp[g * 128:(g + 1) * 128, :] if kind == "p" else ys[:, :]
                S.dma("sp", dst, xstg[:, sbuf_i, :], reads=[("xstg", sbuf_i), R1U])

        assert wstate["cur"] == len(gran_seq)
        S.drain("sp")
        build_program.stats = dict(ninst=S.ninst, nwaits=S.nwaits, cnt=dict(S.ccnt))
    return nc


_CACHE = {}


def kernel(**inputs):
    f32 = lambda a: np.ascontiguousarray(np.asarray(a, dtype=np.float32))
    x_prompt = f32(inputs["x_prompt"])
    x_sample = f32(inputs["x_sample"])
    cache_k = f32(inputs["cache_k"])
    cache_v = f32(inputs["cache_v"])
    state_conv = f32(inputs["state_conv"])
    shared = {n: f32(inputs[n]) for n in ("g_mix_pre", "g_mix_post", "g_mlp_pre", "g_mlp_post", "w_in", "attn_sinks",
                                          "conv_w", "w_attn_o", "w_conv_o", "w_out", "w_up", "w_down")}
    if "nc" not in _CACHE:
        _CACHE["nc"] = build_program()
    nc = _CACHE["nc"]
    in_maps = []
    for c in range(NCORES):
        s0, s1 = 16 * c, 16 * (c + 1)
        m = dict(shared)
        m["xp"] = x_prompt[c]
        m["xs"] = np.ascontiguousarray(x_sample[s0:s1].reshape(128, D))
        m["ck"] = np.ascontiguousarray(cache_k[:, s0:s1].reshape(2, 16, 128, 128))
        m["cv"] = np.ascontiguousarray(cache_v[:, s0:s1].reshape(2, 16, 128, 128))
        m["sc"] = np.ascontiguousarray(state_conv[:, s0:s1].reshape(2, 32, 512))
        in_maps.append(m)
    res = run_bass_kernel_spmd(nc, in_maps, core_ids=list(range(NCORES)))
    R = res.results
    y_prompt = np.stack([R[c]["yp"] for c in range(NCORES)], axis=0).astype(np.float32)
    y_sample = np.concatenate([R[c]["ys"].reshape(16, 8, D) for c in range(NCORES)], axis=0).astype(np.float32)
    nk_p = np.stack([R[c]["nkp"].reshape(2, 128, 2, 64) for c in range(NCORES)], axis=1).astype(np.float32)
    nv_p = np.stack([R[c]["nvp"].reshape(2, 128, 2, 64) for c in range(NCORES)], axis=1).astype(np.float32)
    nc_p = np.stack([R[c]["ncp"] for c in range(NCORES)], axis=1).astype(np.float32)
    nk_s = np.concatenate([R[c]["nks"].reshape(2, 16, 128, 2, 64) for c in range(NCORES)], axis=1).astype(np.float32)
    nv_s = np.concatenate([R[c]["nvs"].reshape(2, 16, 128, 2, 64) for c in range(NCORES)], axis=1).astype(np.float32)
    nc_s = np.concatenate([R[c]["ncs"].reshape(2, 16, 2, 512) for c in range(NCORES)], axis=1).astype(np.float32)
    return (y_prompt, y_sample, nk_p, nv_p, nc_p, nk_s, nv_s, nc_s)
```

```python
import numpy as np
from contextlib import ExitStack

import concourse.bass as bass
import concourse.mybir as mybir
from concourse.bass_utils import run_bass_kernel_spmd

F32 = mybir.dt.float32
BF16 = mybir.dt.bfloat16
AF = mybir.ActivationFunctionType
ALU = mybir.AluOpType

NCORES = 8
D = 1024
KD = 8
SEQ = 2048
IN_DIM = 4352
DFF = 4096
EPS = 1e-6
NSLOT = 5
SLOT_ELEMS = 4096
SAME_ENGINE_SYNC = True

PARTS = [
    [[("p", 0), ("p", 1), ("p", 2)], [("p", 3), ("p", 4), ("p", 5)]],
    [[("p", 6), ("p", 7), ("p", 8)], [("p", 9), ("p", 10), ("p", 11)]],
    [[("p", 12), ("p", 13), ("p", 14)], [("p", 15), ("s", 0)]],
]
BMAX = 384
SKEW = 26000


class Tok:
    __slots__ = ("sem", "val", "know", "eng")

    def __init__(self, sem, val, know, eng):
        self.sem = sem
        self.val = val
        self.know = know
        self.eng = eng


class Sched:
    def __init__(self, nc, es, ring=8):
        self.nc = nc
        self.engs = {"pe": nc.tensor, "act": nc.scalar, "dve": nc.vector, "pool": nc.gpsimd, "sp": nc.sync}
        self.csem = {}
        self.ccnt = {}
        for e in ("pe", "act", "dve", "pool"):
            self.csem[e] = es.enter_context(nc.semaphore("c_" + e))
            self.ccnt[e] = 0
        self.know = {e: {} for e in self.engs}
        self.dring = {}
        self.dpos = {}
        self.dlast = {}
        for q in ("sp", "pool"):
            self.dring[q] = [es.enter_context(nc.semaphore("d_%s%d" % (q, i))) for i in range(ring)]
            self.dpos[q] = 0
        self.lastw = {}
        self.readers = {}
        self.pending = []
        self.all_dma_toks = []
        self.nwaits = 0
        self.ninst = 0

    def _wait(self, eng, tok):
        if eng == "pe" and tok.eng == "pe":
            return
        if tok.val is None:
            raise RuntimeError("dependency on an unsignaled PE op")
        k = self.know[eng]
        sid = id(tok.sem)
        if k.get(sid, 0) >= tok.val:
            return
        own = tok.eng == eng and tok.sem is self.csem.get(eng)
        if own and (eng == "pe" or not SAME_ENGINE_SYNC):
            k[sid] = tok.val
            return
        self.engs[eng].wait_ge(tok.sem, tok.val)
        self.nwaits += 1
        for s, v in tok.know.items():
            if k.get(s, 0) < v:
                k[s] = v
        k[sid] = tok.val

    def _deps(self, reads, writes):
        deps = []
        for c in reads:
            t = self.lastw.get(c)
            if t is not None:
                deps.append(t)
        for c in writes:
            t = self.lastw.get(c)
            if t is not None:
                deps.append(t)
            r = self.readers.get(c)
            if r:
                deps.extend(r.values())
        return deps

    def _record(self, tok, reads, writes):
        for c in writes:
            self.lastw[c] = tok
            self.readers[c] = {}
        for c in reads:
            self.readers.setdefault(c, {})[id(tok.sem)] = tok

    def op(self, eng, fn, reads=(), writes=(), signal=True):
        ps_r = [c for c in reads if c[0] == "ps"]
        if ps_r:
            reads = [c for c in reads if c[0] != "ps"]
            writes = list(writes) + ps_r
        for t in self._deps(reads, writes):
            self._wait(eng, t)
        inst = fn(self.engs[eng])
        self.ninst += 1
        if signal:
            inst.then_inc(self.csem[eng], 1)
            self.ccnt[eng] += 1
            tok = Tok(self.csem[eng], self.ccnt[eng], dict(self.know[eng]), eng)
            if eng == "pe" and self.pending:
                for p in self.pending:
                    p.val = tok.val
                    p.know = tok.know
                self.pending = []
        else:
            assert eng == "pe"
            tok = Tok(self.csem[eng], None, None, eng)
            self.pending.append(tok)
        self._record(tok, reads, writes)
        return tok

    def dma(self, q, out, in_, reads=(), writes=()):
        ring = self.dring[q]
        sem = ring[self.dpos[q] % len(ring)]
        self.dpos[q] += 1
        prev = self.dlast.get(id(sem))
        for t in self._deps(reads, writes):
            self._wait(q, t)
        if prev is not None:
            self._wait(q, prev)
        val = (prev.val if prev is not None else 0) + 16
        self.engs[q].dma_start(out=out, in_=in_).then_inc(sem, 16)
        self.ninst += 1
        tok = Tok(sem, val, dict(self.know[q]), q)
        self.dlast[id(sem)] = tok
        self._record(tok, reads, writes)
        return tok

    def drain(self, q):
        for t in list(self.dlast.values()):
            self._wait(q, t)


def build_program():
    nc = bass.Bass("TRN2", target_bir_lowering=False)

    def din(name, shape):
        return nc.dram_tensor(name, shape, F32, kind="ExternalInput").ap()

    def dout(name, shape):
        return nc.dram_tensor(name, shape, F32, kind="ExternalOutput").ap()

    xp = din("xp", [SEQ, D])
    xs = din("xs", [128, D])
    ck = din("ck", [2, 16, 128, 128])
    cv = din("cv", [2, 16, 128, 128])
    sc = din("sc", [2, 32, 512])
    gvec = [din(n, [2, D]) for n in ("g_mix_pre", "g_mix_post", "g_mlp_pre", "g_mlp_post")]
    w_in = din("w_in", [2, D, IN_DIM])
    sinks = din("attn_sinks", [2, 8])
    conv_w = din("conv_w", [2, 3, 512])
    w_ao = din("w_attn_o", [2, 512, D])
    w_co = din("w_conv_o", [2, 512, D])
    w_out = din("w_out", [2, D, D])
    w_up = din("w_up", [2, D, DFF])
    w_down = din("w_down", [2, DFF, D])

    yp = dout("yp", [SEQ, D])
    ys = dout("ys", [128, D])
    nkp = dout("nkp", [2, 128, 128])
    nvp = dout("nvp", [2, 128, 128])
    ncp = dout("ncp", [2, 2, 512])
    nks = dout("nks", [2, 16, 128, 128])
    nvs = dout("nvs", [2, 16, 128, 128])
    ncs = dout("ncs", [2, 32, 512])

    es = ExitStack()
    with es:
        S = Sched(nc, es)

        def sb(name, shape, dt):
            return es.enter_context(nc.sbuf_tensor(name, shape, dt))

        NB = BMAX
        R1BYTES = 35904

        class SB:
            pass
        B = []
        for s in range(2):
            o = SB()
            o.xT = sb("xT%d" % s, [128, KD, NB], F32)
            HR = sb("HR%d" % s, [128, KD * NB * 2], BF16)
            o.hT = HR[:, 0:KD * NB].rearrange("p (k t) -> p k t", k=KD)
            o.R2 = HR[:, :].bitcast(F32).rearrange("p (k t) -> p k t", k=KD)
            R1 = sb("R1_%d" % s, [128, R1BYTES // 2], BF16)

            def carve(off_bytes, nelem, dt, R1=R1):
                if dt == BF16:
                    return R1[:, off_bytes // 2: off_bytes // 2 + nelem]
                return R1[:, off_bytes // 2: off_bytes // 2 + nelem * 2].bitcast(F32)
            o.qT = carve(0, 4 * NB, BF16).rearrange("p (c t) -> p c t", c=4)
            o.aT = carve(3072, 4 * NB, BF16).rearrange("p (c t) -> p c t", c=4)
            o.kT = carve(6144, 128 + NB, BF16)
            o.Vt = carve(7168, 4 * 128, BF16).rearrange("p (s d) -> p s d", s=4)
            o.Cf = carve(8192, 4 * NB, F32).rearrange("p (c t) -> p c t", c=4)
            o.Kc = carve(8192, 16 * 128, BF16).rearrange("p (s d) -> p s d", s=16)
            o.uf = carve(14336, 4 * (NB + 2), F32).rearrange("p (c t) -> p c t", c=4)
            o.BzT = carve(20544, 4 * NB, BF16).rearrange("p (c t) -> p c t", c=4)
            o.mT = carve(23616, KD * NB, BF16).rearrange("p (c t) -> p c t", c=KD)
            o.gt = carve(29760, 4 * NB, F32).rearrange("p (a t) -> p a t", a=4)
            o.actT = carve(0, 32 * NB, BF16).rearrange("p (c t) -> p c t", c=32)
            o.rtmp = carve(24576, 2 * NB, F32).rearrange("p (a t) -> p a t", a=2)
            o.xin = carve(0, 3 * 1024, F32).rearrange("p (a t) -> p a t", a=3)
            o.yout = carve(12288, 3 * 1024, F32).rearrange("p (a t) -> p a t", a=3)
            o.Pt = sb("Pt%d" % s, [128, 2, 2, 4, 128], BF16)
            o.sq = sb("sq%d" % s, [128, KD, NB], BF16)
            o.rt = sb("rt%d" % s, [128, NB], F32)
            o.rr = sb("rr%d" % s, [128, NB], F32)
            o.dent = sb("dent%d" % s, [128, 512], F32)
            o.R1U = ("R1use", s)
            o.HRU = ("HRuse", s)
            o.par = 0
            o.sqi = 0
            B.append(o)

        wslots = sb("wslots", [128, NSLOT, SLOT_ELEMS], BF16)
        KcT = sb("KcT", [128, 16, 128], BF16)
        Vc = sb("Vc", [128, 16, 128], BF16)
        identf = sb("identf", [128, 128], F32)
        identb = sb("identb", [128, 128], BF16)
        onesb = sb("onesb", [128, 128], BF16)
        M_prompt = sb("M_prompt", [128, 2, 128], BF16)
        M_first = sb("M_first", [128, 2, 128], BF16)
        M_sample = sb("M_sample", [128, 2, 128], BF16)
        gcol = sb("gcol", [128, 128], F32)
        esk = sb("esk", [128, 16], F32)
        sinkcol = sb("sinkcol", [128, 2, 4], F32)
        kcarry = sb("kcarry", [128, 2, 128], BF16)
        vcarry = sb("vcarry", [128, 2, 128], BF16)
        ucarry = sb("ucarry", [128, 2, 4, 2], F32)
        us = sb("us", [128, 4, 16, 10], F32)
        ncstg = sb("ncstg", [128, 4, 32], F32)
        ostg = sb("ostg", [128, 512], F32)
        fence_scr = sb("fence_scr", [128, 4], F32)
        mtmp = B[1].gt[:, :, 0:128]
        gstage = B[1].gt[:, 0, 128:256]

        psb = [es.enter_context(nc.psum_tensor("psb%d" % i, [128, 512], F32)) for i in range(8)]
        PB_ALL = [0, 1, 2, 3]
        projctr = [0]
        PB_S = [4, 5]
        PB_O = 6
        PB_D = 7
        PB_ST = [6, 7]

        def PS(b):
            return ("ps", b)

        def fence(cell):
            S.op("dve", lambda e: e.memset(fence_scr[:, 0:1], 0.0), reads=[], writes=[cell, ("fscr",)])

        def granules(l):
            g = []


            def cols(c0, n_):
                def f(slot):
                    src = w_in[l, :, c0:c0 + n_].rearrange("(k p) n -> p k n", p=128)
                    dst = slot[:, 0:KD * n_].rearrange("p (k n) -> p k n", k=KD)
                    return [(dst, src)]
                return f
            g.append(("q", cols(0, 512)))
            g.append(("kv", cols(512, 256)))
            g.append(("C", cols(1280, 512)))
            g.append(("u", cols(1792, 512)))
            g.append(("B", cols(768, 512)))
            for m in range(2):
                def gx(slot, m=m, half=0):
                    dst = slot[:, 0:KD * 512].rearrange("p (k n) -> p k n", k=KD)
                    c_a = 2304 + m * 512 + half * 256
                    c_c = 3328 + m * 512 + half * 256
                    return [(dst[:, :, 0:256], w_in[l, :, c_a:c_a + 256].rearrange("(k p) n -> p k n", p=128)),
                            (dst[:, :, 256:512], w_in[l, :, c_c:c_c + 256].rearrange("(k p) n -> p k n", p=128))]
                g.append(("gx%d_0" % m, lambda slot, m=m: gx(slot, m, 0)))

                def wo(slot, m=m):
                    dst = slot[:, 0:KD * 512].rearrange("p (k n) -> p k n", k=KD)
                    r = []
                    r.append((dst[0:64, 0:4, :], w_ao[l, 0:256, m * 512:(m + 1) * 512].rearrange("(c p) n -> p c n", p=64)))
                    r.append((dst[64:128, 0:4, :], w_ao[l, 256:512, m * 512:(m + 1) * 512].rearrange("(c p) n -> p c n", p=64)))
                    r.append((dst[:, 4:8, :], w_co[l, :, m * 512:(m + 1) * 512].rearrange("(k p) n -> p k n", p=128)))
                    return r
                g.append(("aoco%d" % m, wo))
                g.append(("gx%d_1" % m, lambda slot, m=m: gx(slot, m, 1)))
            for m in range(2):
                def wout(slot, m=m):
                    return [(slot[:, 0:KD * 512].rearrange("p (k n) -> p k n", k=KD),
                             w_out[l, :, m * 512:(m + 1) * 512].rearrange("(k p) n -> p k n", p=128))]
                g.append(("wout%d" % m, wout))
            for m in range(8):
                def wup(slot, m=m):
                    return [(slot[:, 0:KD * 512].rearrange("p (k n) -> p k n", k=KD),
                             w_up[l, :, m * 512:(m + 1) * 512].rearrange("(k p) n -> p k n", p=128))]
                g.append(("wup%d" % m, wup))
            for i in range(8):
                def wdn(slot, i=i):
                    return [(slot[:, 0:32 * 128].rearrange("p (k n) -> p k n", k=32),
                             w_down[l, :, i * 128:(i + 1) * 128].rearrange("(k p) n -> p k n", p=128))]
                g.append(("wdn%d" % i, wdn))
            return g

        gran_seq = []
        gran_idx = {}
        for pi in range(len(PARTS)):
            for l in range(2):
                for name, f in granules(l):
                    gran_idx[(pi, l, name)] = len(gran_seq)
                    gran_seq.append(f)
        G = dict(issued=0, done=[0] * len(gran_seq))

        def g_issuable(i):
            j = i - NSLOT
            return j < 0 or G["done"][j] >= 2

        def g_prefetch():
            while G["issued"] < len(gran_seq) and g_issuable(G["issued"]):
                i = G["issued"]
                slot_i = i % NSLOT
                for dst, src in gran_seq[i](wslots[:, slot_i, :]):
                    S.dma("pool", dst, src, writes=[("w", slot_i)])
                G["issued"] += 1

        def g_available(i):
            return i < G["issued"]

        def g_done(i):
            G["done"][i] += 1
            g_prefetch()


        SETUP = [B[1].R1U]
        for tj_, (kind_j, g_j) in enumerate(PARTS[0][0]):
            S.dma("sp", B[0].xin[:, tj_, :], xp[g_j * 128:(g_j + 1) * 128, :], reads=[B[0].R1U], writes=[("xin", 0, tj_)])
        S.op("pool", lambda e: e.memset(identf[:], 1.0), writes=[("identf",)])
        S.op("pool", lambda e: e.affine_select(out=identf[:], in_=identf[:], pattern=[[-1, 128]],
                                                compare_op=ALU.is_equal, fill=0.0, base=0, channel_multiplier=1),
             reads=[("identf",)], writes=[("identf",)])
        S.op("dve", lambda e: e.memset(gstage, 0.0), reads=SETUP, writes=[("gstage", i) for i in range(6)])
        for v in range(4):
            S.dma("sp", gstage[v * 16:(v + 1) * 16, :], gvec[v].rearrange("l (k p) -> (l k) p", p=128),
                  reads=SETUP, writes=[("gstage", v)])
        S.dma("sp", gstage[64:88, :], conv_w.rearrange("l i (j p) -> (l i j) p", p=128), reads=SETUP, writes=[("gstage", 4)])
        S.dma("sp", esk[:], sinks.rearrange("l h -> (l h)").partition_broadcast(128), writes=[("esk",)])
        g_prefetch()
        S.op("dve", lambda e: e.tensor_copy(out=identb[:], in_=identf[:]), reads=[("identf",)], writes=[("identb",)])
        S.op("dve", lambda e: e.memset(onesb[:], 1.0), writes=[("onesb",)])
        S.op("pe", lambda e: e.transpose(out=psb[6][:, 0:128], in_=gstage, identity=identf[:]),
             reads=SETUP + [("gstage", i) for i in range(6)] + [("identf",)], writes=[PS(6)])
        S.op("act", lambda e: e.copy(out=gcol[:], in_=psb[6][:, 0:128]), reads=[PS(6)], writes=[("gcol",)])

        def gc_(vec, l, k):
            i = vec * 16 + l * 8 + k
            return gcol[:, i:i + 1]

        def cw_(l, i, j):
            n_ = 64 + l * 12 + i * 4 + j
            return gcol[:, n_:n_ + 1]

        S.op("act", lambda e: e.activation(out=esk[:], in_=esk[:], func=AF.Exp), reads=[("esk",)], writes=[("esk",)])
        for l in range(2):
            S.op("dve", lambda e, l=l: e.tensor_copy(out=sinkcol[0:64, l, :], in_=esk[0:64, l * 8:l * 8 + 4]),
                 reads=[("esk",)], writes=[("sinkcol",)])
            S.op("dve", lambda e, l=l: e.tensor_copy(out=sinkcol[64:128, l, :], in_=esk[64:128, l * 8 + 4:l * 8 + 8]),
                 reads=[("esk",)], writes=[("sinkcol",)])
        S.op("pool", lambda e: e.memset(mtmp, 1.0), reads=SETUP, writes=[("mtmp",)])
        S.op("pool", lambda e: e.affine_select(out=mtmp[:, 0, :], in_=mtmp[:, 0, :], pattern=[[1, 128]],
                                                compare_op=ALU.is_ge, fill=0.0, base=0, channel_multiplier=-1),
             reads=SETUP + [("mtmp",)], writes=[("mtmp",)])
        S.op("pool", lambda e: e.affine_select(out=mtmp[:, 1, :], in_=mtmp[:, 1, :], pattern=[[-1, 128]],
                                                compare_op=ALU.is_gt, fill=0.0, base=0, channel_multiplier=1),
             reads=SETUP + [("mtmp",)], writes=[("mtmp",)])
        m2v = mtmp[:, 2, :].rearrange("p (s t) -> p s t", s=16)
        S.op("pool", lambda e: e.affine_select(out=m2v, in_=m2v, pattern=[[8, 16], [1, 8]],
                                                compare_op=ALU.is_ge, fill=0.0, base=0, channel_multiplier=-1),
             reads=SETUP + [("mtmp",)], writes=[("mtmp",)])
        S.op("pool", lambda e: e.affine_select(out=m2v, in_=m2v, pattern=[[-8, 16], [0, 8]],
                                                compare_op=ALU.is_ge, fill=0.0, base=0, channel_multiplier=1),
             reads=SETUP + [("mtmp",)], writes=[("mtmp",)])
        m3v = mtmp[:, 3, :].rearrange("p (s t) -> p s t", s=16)
        S.op("pool", lambda e: e.affine_select(out=m3v, in_=m3v, pattern=[[0, 16], [-1, 8]],
                                                compare_op=ALU.is_gt, fill=0.0, base=0, channel_multiplier=1),
             reads=SETUP + [("mtmp",)], writes=[("mtmp",)])
        S.op("dve", lambda e: e.memset(M_first[:, 0, :], 0.0), writes=[("masks",)])
        for dst, srci in ((M_prompt[:, 0, :], 1), (M_prompt[:, 1, :], 0), (M_first[:, 1, :], 0),
                          (M_sample[:, 0, :], 3), (M_sample[:, 1, :], 2)):
            S.op("dve", lambda e, dst=dst, srci=srci: e.tensor_copy(out=dst, in_=mtmp[:, srci, :]),
                 reads=SETUP + [("mtmp",)], writes=[("masks",)])
        for l in range(2):
            S.dma("sp", nks[l, :, 0:120, :], ck[l, :, 8:128, :])
            S.dma("sp", nvs[l, :, 0:120, :], cv[l, :, 8:128, :])

        marks = set()

        cur_gran = [-1, -1]
        alive = [True, True]
        MAXLEAD = 2

        def check_need(nd, s_):
            if nd[0] == "gran":
                if alive[1 - s_] and nd[1] - cur_gran[1 - s_] > MAXLEAD:
                    return False
                return g_available(nd[1])
            if nd[0] == "mark":
                return nd[1] in marks
            raise AssertionError(nd)

        def item(cost, *needs):
            return (cost, needs)

        def stream(s):
            o = B[s]
            xT, hT, R2 = o.xT, o.hT, o.R2
            R1U, HRU = o.R1U, o.HRU

            def Xc(k):
                return ("x", s, k)

            def Hc(k):
                return ("h", s, k)

            def R2c(k):
                return ("r2", s, k)

            def proj(K, n, lhs_fn, rhs_fn, reads_fn, wcell, after_mm=None):
                bank = PB_ALL[projctr[0] % 4]
                projctr[0] += 1
                for k in range(K):
                    S.op("pe", lambda e, k=k: e.matmul(psb[bank][:, 0:n], lhsT=lhs_fn(k), rhs=rhs_fn(k),
                                                       start=(k == 0), stop=(k == K - 1)),
                         reads=[wcell] + reads_fn(k), writes=[PS(bank)], signal=(k == K - 1))
                return psb[bank][:, 0:n], PS(bank)

            def stat_sq(src_ap, n, reads, i):
                S.op("act", lambda e: e.activation(out=o.sq[:, i, 0:n], in_=src_ap, func=AF.Square),
                     reads=reads, writes=[("sq", s, i)])

            def stat_mms(n):
                bank = PB_ST[s]
                for k in range(KD):
                    S.op("pe", lambda e, k=k: e.matmul(psb[bank][:, 0:n], lhsT=onesb[:], rhs=o.sq[:, k, 0:n],
                                                       start=(k == 0), stop=(k == KD - 1)),
                         reads=[("sq", s, k), ("onesb",)], writes=[PS(bank)], signal=(k == KD - 1))

            def stat_fin(n):
                bank = PB_ST[s]
                S.op("act", lambda e: e.activation(out=o.rt[:, 0:n], in_=psb[bank][:, 0:n], func=AF.Ln,
                                                   bias=EPS, scale=1.0 / D),
                     reads=[PS(bank)], writes=[("rt", s)])
                S.op("act", lambda e: e.activation(out=o.rr[:, 0:n], in_=o.rt[:, 0:n], func=AF.Exp, scale=-0.5),
                     reads=[("rt", s)], writes=[("r", s)])

            def stat_fin_mlp(n):
                bank = PB_ST[s]
                S.op("dve", lambda e: e.scalar_tensor_tensor(out=o.rr[:, 0:n], in0=o.rt[:, 0:n], scalar=EPS * D,
                                                             in1=psb[bank][:, 0:n], op0=ALU.mult, op1=ALU.add),
                     reads=[("rt", s), PS(bank)], writes=[("r", s)])
                S.op("act", lambda e: e.activation(out=o.rr[:, 0:n], in_=o.rr[:, 0:n], func=AF.Ln, scale=1.0 / D),
                     reads=[("r", s)], writes=[("r", s)])
                S.op("act", lambda e: e.activation(out=o.rr[:, 0:n], in_=o.rr[:, 0:n], func=AF.Exp, scale=-0.5),
                     reads=[("r", s)], writes=[("r", s)])

            for pi in range(len(PARTS)):
                tiles = PARTS[pi][s]
                NT = len(tiles)
                n = NT * 128
                has_sample = tiles[-1][0] == "s"
                NTP = NT - 1 if has_sample else NT
                TP = NTP * 128
                scol = TP
                last_block = (pi == len(PARTS) - 1 and s == 1)
                prev_blk = (pi, 0) if s == 1 else ((pi - 1, 1) if pi > 0 else None)
                JOB = KD * BMAX

                def gi_(l, name):
                    return gran_idx[(pi, l, name)]

                def issue_xin(pj):
                    for tj, (kind_j, g_j) in enumerate(PARTS[pj][s]):
                        src = xp[g_j * 128:(g_j + 1) * 128, :] if kind_j == "p" else xs[:, :]
                        S.dma("sp", o.xin[:, tj, :], src, reads=[R1U], writes=[("xin", s, tj)])

                if pi == 0:
                    yield item(0)
                    fence(R1U)
                    if s == 1:
                        issue_xin(0)
                for ti, (kind, g) in enumerate(tiles):
                    yield item(2048)
                    for half in range(2):
                        bank = (PB_ST + PB_S)[(ti * 2 + half) % 4]
                        for kk in range(4):
                            k = half * 4 + kk
                            S.op("pe", lambda e, k=k, kk=kk, bank=bank: e.transpose(
                                out=psb[bank][:, kk * 128:(kk + 1) * 128], in_=o.xin[:, ti, k * 128:(k + 1) * 128],
                                identity=identf[:]),
                                reads=[("xin", s, ti), ("identf",), R1U], writes=[PS(bank)], signal=(kk == 3))
                        S.op("act", lambda e, half=half, bank=bank: e.copy(
                            out=xT[:, half * 4:(half + 1) * 4, ti * 128:(ti + 1) * 128],
                            in_=psb[bank][:, :].rearrange("p (k t) -> p k t", k=4)),
                            reads=[PS(bank)], writes=[Xc(half * 4 + kk) for kk in range(4)])

                def make_h(vec, l):
                    yield item(0)
                    fence(HRU)
                    yield item(8000)
                    for k in range(KD):
                        stat_sq(xT[:, k, 0:n], n, [Xc(k)], k)
                    yield item(3500)
                    stat_mms(n)
                    stat_fin(n)
                    for k in range(KD):
                        yield item(1000)
                        S.op("dve", lambda e, k=k: e.scalar_tensor_tensor(
                            out=hT[:, k, 0:n], in0=xT[:, k, 0:n], scalar=gc_(vec, l, k),
                            in1=o.rr[:, 0:n], op0=ALU.mult, op1=ALU.mult),
                            reads=[Xc(k), ("r", s), ("gcol",), HRU], writes=[Hc(k)])

                def make_xg(l):
                    yield item(0)
                    fence(HRU)
                    for k in range(KD):
                        yield item(500)
                        S.op("act", lambda e, k=k: e.mul(hT[:, k, 0:n], xT[:, k, 0:n], gc_(2, l, k)),
                             reads=[Xc(k), ("gcol",), HRU], writes=[Hc(k)])
                    yield item(1500)
                    for k in range(KD):
                        stat_sq(xT[:, k, 0:n], n, [Xc(k)], k)
                    yield item(1500)
                    stat_mms(n)
                    bank_ = PB_ST[s]
                    S.op("act", lambda e: e.activation(out=o.rt[:, 0:n], in_=psb[bank_][:, 0:n], func=AF.Square,
                                                       bias=EPS, scale=1.0 / D),
                         reads=[PS(bank_)], writes=[("rt", s)])

                def post_norm(vec, l):
                    yield item(3000)
                    for k in range(KD):
                        yield item(1800)
                        S.op("dve", lambda e, k=k: e.scalar_tensor_tensor(
                            out=R2[:, k, 0:n], in0=R2[:, k, 0:n], scalar=gc_(vec, l, k),
                            in1=o.rr[:, 0:n], op0=ALU.mult, op1=ALU.mult),
                            reads=[R2c(k), ("r", s), ("gcol",), HRU], writes=[R2c(k)])
                        S.op("dve", lambda e, k=k: e.tensor_tensor(
                            out=xT[:, k, 0:n], in0=xT[:, k, 0:n], in1=R2[:, k, 0:n], op=ALU.add),
                            reads=[R2c(k), Xc(k), HRU], writes=[Xc(k)])

                def proj_to_R2(l, names, K, nper, lhs_of, rhs_of, rcell_of, rfence, fin=None):
                    for i in range(KD):
                        gname = names[i // nper]
                        gidx = gi_(l, gname)
                        first = (i % nper == 0)
                        yield item(K * BMAX, ("gran", gidx)) if first else item(K * BMAX)
                        sl_ = gidx % NSLOT
                        ps, pcell = proj(K, n, lambda k, i=i, sl_=sl_: lhs_of(sl_, i, k), rhs_of,
                                         lambda k: [rcell_of(k), rfence], ("w", sl_))
                        S.op("act", lambda e, i=i, ps=ps: e.copy(out=R2[:, i, 0:n], in_=ps), reads=[pcell, HRU],
                             writes=[R2c(i)])
                        stat_sq(ps, n, [pcell], i)
                        if i % nper == nper - 1:
                            g_done(gidx)
                    yield item(3500)
                    stat_mms(n)
                    (fin or stat_fin)(n)

                for l in range(2):
                    yield item(0)
                    fence(R1U)
                    for it in make_h(0, l):
                        yield it

                    if has_sample:
                        yield item(4000)
                        S.dma("pool", o.Kc, ck[l].rearrange("s k d -> k s d"), reads=[R1U], writes=[("Kc",), ("C", s)])
                        S.dma("pool", Vc[:], cv[l].rearrange("s k d -> k s d"), writes=[("Vc",)])
                        for grp in range(2):
                            bank = PB_ST[grp]
                            pbf = psb[bank][:, :].bitcast(BF16).rearrange("p (s k) -> p s k", s=8)
                            for s8 in range(8):
                                sq_ = grp * 8 + s8
                                S.op("pe", lambda e, sq_=sq_, s8=s8, pbf=pbf: e.transpose(out=pbf[:, s8, :], in_=o.Kc[:, sq_, :],
                                                                                         identity=identb[:]),
                                     reads=[("Kc",), ("identb",), R1U], writes=[PS(bank)], signal=(s8 == 7))
                            S.op("act", lambda e, grp=grp, pbf=pbf: e.copy(out=KcT[:, grp * 8:(grp + 1) * 8, :], in_=pbf),
                                 reads=[PS(bank)], writes=[("KcT",)])
                        S.dma("sp", ostg[0:32, :], sc[l], writes=[("ostg",)])
                        bank = PB_ST[0]
                        for j in range(4):
                            S.op("pe", lambda e, j=j: e.transpose(out=psb[bank][:, j * 32:(j + 1) * 32],
                                                                  in_=ostg[0:32, j * 128:(j + 1) * 128],
                                                                  identity=identf[0:32, 0:32]),
                                 reads=[("ostg",), ("identf",)], writes=[PS(bank)], signal=(j == 3))
                        S.op("act", lambda e: e.copy(out=us[:, :, :, 0:2],
                                                     in_=psb[bank][:, 0:128].rearrange("p (j s r) -> p j s r", j=4, s=16)),
                             reads=[PS(bank)], writes=[("us",)])

                    gq = gi_(l, "q")
                    slq = gq % NSLOT
                    wq = wslots[:, slq, 0:KD * 512].rearrange("p (k n) -> p k n", k=KD)
                    for c in range(4):
                        yield item(JOB, ("gran", gq))
                        bank = PB_ALL[projctr[0] % 4]
                        projctr[0] += 1
                        for k in range(KD):
                            for hf_, h_ in enumerate((c, 4 + c)):
                                S.op("pe", lambda e, k=k, hf_=hf_, h_=h_: e.matmul(
                                    psb[bank][hf_ * 64:(hf_ + 1) * 64, 0:n], lhsT=wq[:, k, h_ * 64:(h_ + 1) * 64],
                                    rhs=hT[:, k, 0:n], start=(k == 0), stop=(k == KD - 1)),
                                    reads=[("w", slq), Hc(k), HRU], writes=[PS(bank)], signal=(k == KD - 1 and hf_ == 1))
                        ps, pcell = psb[bank][:, 0:n], PS(bank)
                        S.op("act", lambda e, c=c, ps=ps: e.copy(out=o.qT[:, c, 0:n], in_=ps), reads=[pcell, R1U],
                             writes=[("q", s, c)])
                    g_done(gq)
                    gkv = gi_(l, "kv")
                    slkv = gkv % NSLOT
                    wkv = wslots[:, slkv, 0:KD * 256].rearrange("p (k n) -> p k n", k=KD)
                    yield item(JOB, ("gran", gkv))
                    ps, pcell = proj(KD, n, lambda k: wkv[:, k, 0:128], lambda k: hT[:, k, 0:n],
                                     lambda k: [Hc(k), HRU], ("w", slkv))
                    S.op("act", lambda e, ps=ps: e.copy(out=o.kT[:, 128:128 + n], in_=ps), reads=[pcell, R1U], writes=[("k", s)])
                    for ti, (kind, g) in enumerate(tiles):
                        yield item(KD * 256)
                        need_out = (kind == "s") or (kind == "p" and g == 15)
                        bank = PB_ST[ti % 2]
                        c0w, nw = (0, 256) if need_out else (128, 128)
                        for k in range(KD):
                            S.op("pe", lambda e, k=k, bank=bank, c0w=c0w, nw=nw: e.matmul(
                                psb[bank][:, 0:nw], lhsT=hT[:, k, ti * 128:(ti + 1) * 128], rhs=wkv[:, k, c0w:c0w + nw],
                                start=(k == 0), stop=(k == KD - 1)),
                                reads=[("w", slkv), Hc(k), HRU], writes=[PS(bank)], signal=(k == KD - 1))
                        voff = 128 if need_out else 0
                        S.op("act", lambda e, bank=bank, voff=voff: e.copy(out=o.Vt[:, 1 + ti, :], in_=psb[bank][:, voff:voff + 128]),
                             reads=[PS(bank), R1U], writes=[("v", s)])
                        if need_out:
                            S.op("dve", lambda e, bank=bank: e.tensor_copy(out=ostg[:, 0:256], in_=psb[bank][:, 0:256]),
                                 reads=[PS(bank)], writes=[("ostg",)])
                            if kind == "p":
                                S.dma("sp", nkp[l], ostg[:, 0:128], reads=[("ostg",)])
                                S.dma("sp", nvp[l], ostg[:, 128:256], reads=[("ostg",)])
                            else:
                                for sq_ in range(16):
                                    S.dma("sp", nks[l, sq_, 120:128, :], ostg[sq_ * 8:(sq_ + 1) * 8, 0:128], reads=[("ostg",)])
                                    S.dma("sp", nvs[l, sq_, 120:128, :], ostg[sq_ * 8:(sq_ + 1) * 8, 128:256], reads=[("ostg",)])
                    g_done(gkv)
                    if prev_blk is None:
                        yield item(0)
                        S.op("dve", lambda e: e.memset(o.kT[:, 0:128], 0.0), reads=[R1U], writes=[("k", s)])
                        S.op("dve", lambda e: e.memset(o.Vt[:, 0, :], 0.0), reads=[R1U], writes=[("v", s)])
                    else:
                        yield item(0, ("mark", ("kv", prev_blk, l)))
                        S.op("dve", lambda e: e.tensor_copy(out=o.kT[:, 0:128], in_=kcarry[:, l, :]),
                             reads=[("kcarry", l), R1U], writes=[("k", s)])
                        S.op("dve", lambda e: e.tensor_copy(out=o.Vt[:, 0, :], in_=vcarry[:, l, :]),
                             reads=[("vcarry", l), R1U], writes=[("v", s)])
                    if not last_block:
                        S.op("dve", lambda e: e.tensor_copy(out=kcarry[:, l, :], in_=o.kT[:, TP:TP + 128]),
                             reads=[("k", s), R1U], writes=[("kcarry", l)])
                        S.op("dve", lambda e: e.tensor_copy(out=vcarry[:, l, :], in_=o.Vt[:, NTP, :]),
                             reads=[("v", s), R1U], writes=[("vcarry", l)])
                    marks.add(("kv", (pi, s), l))

                    def att_S(ti, gi):
                        kind, g = tiles[ti]
                        tcols = slice(ti * 128, (ti + 1) * 128)
                        prevc = slice(ti * 128, (ti + 1) * 128)
                        ownc = slice(128 + ti * 128, 128 + (ti + 1) * 128)
                        if True:
                            for hf in range(2):
                                bank = PB_S[hf]
                                rows = slice(hf * 64, (hf + 1) * 64)
                                for c2 in range(2):
                                    c = gi * 2 + c2
                                    base = c2 * 256
                                    if kind == "p":
                                        S.op("pe", lambda e, c=c, base=base: e.matmul(
                                            psb[bank][:, base:base + 128], lhsT=o.kT[rows, prevc], rhs=o.qT[rows, c, tcols],
                                            start=True, stop=True),
                                            reads=[("k", s), ("q", s, c), R1U], writes=[PS(bank)], signal=False)
                                    else:
                                        for sq_ in range(16):
                                            S.op("pe", lambda e, c=c, base=base, sq_=sq_: e.matmul(
                                                psb[bank][:, base + sq_ * 8:base + sq_ * 8 + 8], lhsT=KcT[rows, sq_, :],
                                                rhs=o.qT[rows, c, scol + sq_ * 8:scol + sq_ * 8 + 8], start=True, stop=True),
                                                reads=[("KcT",), ("q", s, c), R1U], writes=[PS(bank)], signal=False)
                                    S.op("pe", lambda e, c=c, base=base: e.matmul(
                                        psb[bank][:, base + 128:base + 256], lhsT=o.kT[rows, ownc], rhs=o.qT[rows, c, tcols],
                                        start=True, stop=True),
                                        reads=[("k", s), ("q", s, c), R1U], writes=[PS(bank)], signal=(c2 == 1))
                                S.op("act", lambda e, gi=gi, hf=hf, bank=bank: e.activation(
                                    out=o.Pt[:, hf, :, gi * 2:gi * 2 + 2, :].rearrange("p a c q -> p c a q"),
                                    in_=psb[bank][:, :].rearrange("p (c a q) -> p c a q", c=2, a=2),
                                    func=AF.Exp, scale=0.125),
                                    reads=[PS(bank)], writes=[("Pt", s, hf, gi)])
                        if gi == 0:
                            return
                        if kind == "s":
                            M = M_sample
                        elif g == 0:
                            M = M_first
                        else:
                            M = M_prompt
                        for hf in range(2):
                            S.op("dve", lambda e, hf=hf, M=M: e.tensor_tensor(
                                out=o.Pt[:, hf], in0=o.Pt[:, hf], in1=M[:, :, None, :].to_broadcast([128, 2, 4, 128]), op=ALU.mult),
                                reads=[("Pt", s, hf, 0), ("Pt", s, hf, 1), ("masks",)],
                                writes=[("Pt", s, hf, 0), ("Pt", s, hf, 1)])

                    def att_O(ti):
                        kind, g = tiles[ti]
                        Pt = o.Pt
                        for hf in range(2):
                            rows = slice(hf * 64, (hf + 1) * 64)
                            pcells = [("Pt", s, hf, 0), ("Pt", s, hf, 1)]
                            if kind == "p":
                                S.op("pe", lambda e, hf=hf: e.matmul(
                                    psb[PB_O][rows, :], lhsT=o.Vt[:, ti, hf * 64:(hf + 1) * 64],
                                    rhs=Pt[:, hf, 0].rearrange("p c q -> p (c q)"), start=True, stop=False),
                                    reads=pcells + [("v", s), R1U], writes=[PS(PB_O)], signal=False)
                                S.op("pe", lambda e, hf=hf: e.matmul(
                                    psb[PB_O][rows, :], lhsT=o.Vt[:, 1 + ti, hf * 64:(hf + 1) * 64],
                                    rhs=Pt[:, hf, 1].rearrange("p c q -> p (c q)"), start=False, stop=True),
                                    reads=pcells + [("v", s), R1U], writes=[PS(PB_O)], signal=False)
                                r0 = Pt[:, hf, 0].rearrange("p c q -> p (c q)")
                                r1 = Pt[:, hf, 1].rearrange("p c q -> p (c q)")
                            else:
                                S.op("pe", lambda e, hf=hf: e.matmul(
                                    psb[PB_O][rows, :], lhsT=o.Vt[:, 1 + ti, hf * 64:(hf + 1) * 64],
                                    rhs=Pt[:, hf, 1].rearrange("p c (s t) -> p s c t", s=16), start=True, stop=False),
                                    reads=pcells + [("v", s), R1U], writes=[PS(PB_O)], signal=False)
                                for sq_ in range(16):
                                    S.op("pe", lambda e, hf=hf, sq_=sq_: e.matmul(
                                        psb[PB_O][rows, sq_ * 32:(sq_ + 1) * 32],
                                        lhsT=Vc[:, sq_, hf * 64:(hf + 1) * 64],
                                        rhs=Pt[:, hf, 0, :, sq_ * 8:(sq_ + 1) * 8], start=False, stop=(sq_ == 15)),
                                        reads=pcells + [("Vc",)], writes=[PS(PB_O)], signal=False)
                                r0 = Pt[:, hf, 0].rearrange("p c (s t) -> p s c t", s=16)
                                r1 = Pt[:, hf, 1].rearrange("p c (s t) -> p s c t", s=16)
                            S.op("pe", lambda e, hf=hf, r0=r0: e.matmul(
                                psb[PB_D][rows, :], lhsT=onesb[:, 0:64], rhs=r0, start=True, stop=False),
                                reads=pcells + [("onesb",)], writes=[PS(PB_D)], signal=False)
                            S.op("pe", lambda e, hf=hf, r1=r1: e.matmul(
                                psb[PB_D][rows, :], lhsT=onesb[:, 0:64], rhs=r1, start=False, stop=True),
                                reads=pcells + [("onesb",)], writes=[PS(PB_D)], signal=(hf == 1))
                        if kind == "p":
                            dview = lambda ap: ap.rearrange("p (c q) -> p c q", c=4)
                            aview = o.aT[:, :, ti * 128:(ti + 1) * 128]
                        else:
                            dview = lambda ap: ap.rearrange("p (s c t) -> p c s t", s=16, c=4)
                            aview = o.aT[:, :, ti * 128:(ti + 1) * 128].rearrange("p c (s t) -> p c s t", s=16)
                        for c in range(4):
                            S.op("act", lambda e, c=c: e.activation(
                                out=dview(o.dent[:])[:, c], in_=dview(psb[PB_D][:, :])[:, c], func=AF.Ln,
                                bias=sinkcol[:, l, c:c + 1]),
                                reads=[PS(PB_D), ("sinkcol",)], writes=[("dent", s)])
                        S.op("act", lambda e: e.activation(out=o.dent[:], in_=o.dent[:], func=AF.Exp, scale=-1.0),
                             reads=[("dent", s)], writes=[("dent", s)])
                        S.op("dve", lambda e: e.tensor_tensor(
                            out=aview, in0=dview(psb[PB_O][:, :]), in1=dview(o.dent[:]), op=ALU.mult),
                            reads=[PS(PB_O), ("dent", s), R1U], writes=[("a", s)])

                    att_stages = []
                    for ti in range(NT):
                        att_stages.append(("S0", ti))
                        att_stages.append(("S1", ti))
                        att_stages.append(("O", ti))

                    def conv_job(which, j):
                        gname = {"C": "C", "u": "u", "B": "B"}[which]
                        gidx = gi_(l, gname)
                        sl_ = gidx % NSLOT
                        wv = wslots[:, sl_, 0:KD * 512].rearrange("p (k n) -> p k n", k=KD)
                        ps, pcell = proj(KD, n, lambda k: wv[:, k, j * 128:(j + 1) * 128], lambda k: hT[:, k, 0:n],
                                         lambda k: [Hc(k), HRU], ("w", sl_))
                        if which == "C":
                            wr = [("C", s, j)] + ([("Kc",)] if has_sample else [])
                            S.op("act", lambda e: e.copy(out=o.Cf[:, j, 0:n], in_=ps), reads=[pcell, R1U], writes=wr)
                        elif which == "u":
                            S.op("dve", lambda e: e.tensor_tensor(out=o.uf[:, j, 2:2 + n], in0=ps, in1=o.Cf[:, j, 0:n], op=ALU.mult),
                                 reads=[pcell, ("C", s, j), R1U], writes=[("u", s, j)])
                        else:
                            S.op("dve", lambda e: e.tensor_tensor(out=o.BzT[:, j, 0:n], in0=ps, in1=o.Cf[:, j, 0:n], op=ALU.mult),
                                 reads=[pcell, ("C", s, j), R1U], writes=[("bz", s, j)])
                        if j == 3:
                            g_done(gidx)

                    def conv_prefix():
                        if prev_blk is None:
                            S.op("dve", lambda e: e.memset(o.uf[:, :, 0:2], 0.0), reads=[R1U], writes=[("upre", s)])
                        else:
                            S.op("dve", lambda e: e.tensor_copy(out=o.uf[:, :, 0:2], in_=ucarry[:, l, :, :]),
                                 reads=[("ucarry", l), R1U], writes=[("upre", s)])

                    def conv_chunk(j):
                        ucells = [("u", s, j), ("upre", s)]
                        ccells = [("C", s, j)]
                        S.op("dve", lambda e: e.tensor_scalar(out=o.Cf[:, j, 0:TP], in0=o.uf[:, j, 0:TP], scalar1=cw_(l, 0, j),
                                                              scalar2=None, op0=ALU.mult),
                             reads=ucells + [("gcol",), R1U], writes=ccells)
                        for i in (1, 2):
                            S.op("dve", lambda e, i=i: e.scalar_tensor_tensor(
                                out=o.Cf[:, j, 0:TP], in0=o.uf[:, j, i:i + TP], scalar=cw_(l, i, j), in1=o.Cf[:, j, 0:TP],
                                op0=ALU.mult, op1=ALU.add),
                                reads=ucells + ccells + [("gcol",), R1U], writes=ccells)
                        if has_sample:
                            S.op("dve", lambda e: e.tensor_copy(
                                out=us[:, j, :, 2:10], in_=o.uf[:, j, 2 + scol:2 + scol + 128].rearrange("p (s t) -> p s t", s=16)),
                                reads=ucells + [R1U], writes=[("us",)])
                            zs = o.Cf[:, j, scol:scol + 128].rearrange("p (s t) -> p s t", s=16)
                            S.op("dve", lambda e: e.tensor_scalar(out=zs, in0=us[:, j, :, 0:8], scalar1=cw_(l, 0, j),
                                                                  scalar2=None, op0=ALU.mult),
                                 reads=[("us",), ("gcol",), R1U], writes=ccells)
                            for i in (1, 2):
                                S.op("dve", lambda e, i=i: e.scalar_tensor_tensor(
                                    out=zs, in0=us[:, j, :, i:i + 8], scalar=cw_(l, i, j), in1=zs, op0=ALU.mult, op1=ALU.add),
                                    reads=[("us",), ("gcol",), R1U] + ccells, writes=ccells)
                            S.op("dve", lambda e: e.tensor_copy(out=ncstg[:, j, :].rearrange("p (s r) -> p s r", s=16),
                                                                in_=us[:, j, :, 8:10]),
                                 reads=[("us",)], writes=[("ncstg", j)])

                    def conv_publish():
                        allu = [("u", s, j) for j in range(4)]
                        if not last_block:
                            S.op("dve", lambda e: e.tensor_copy(out=ucarry[:, l, :, :], in_=o.uf[:, :, TP:TP + 2]),
                                 reads=allu + [("upre", s), R1U], writes=[("ucarry", l)])
                        marks.add(("u", (pi, s), l))
                        if has_sample:
                            bank = PB_ST[0]
                            for j in range(4):
                                S.op("pe", lambda e, j=j: e.transpose(out=psb[bank][0:2, j * 128:(j + 1) * 128],
                                                                      in_=o.uf[:, j, TP:TP + 2], identity=identf[:]),
                                     reads=[("u", s, j), ("upre", s), ("identf",), R1U], writes=[PS(bank)], signal=(j == 3))
                            S.op("act", lambda e: e.copy(out=ostg[0:2, :], in_=psb[bank][0:2, :]), reads=[PS(bank)], writes=[("ostg",)])
                            S.dma("sp", ncp[l], ostg[0:2, :], reads=[("ostg",)])
                            bank = PB_ST[1]
                            for j in range(4):
                                S.op("pe", lambda e, j=j: e.transpose(out=psb[bank][0:32, j * 128:(j + 1) * 128],
                                                                      in_=ncstg[:, j, :], identity=identf[:]),
                                     reads=[("ncstg", j), ("identf",)], writes=[PS(bank)], signal=(j == 3))
                            S.op("act", lambda e: e.copy(out=ostg[0:32, :], in_=psb[bank][0:32, :]), reads=[PS(bank)], writes=[("ostg",)])
                            S.dma("sp", ncs[l], ostg[0:32, :], reads=[("ostg",)])

                    conv_seq = [("C", j) for j in range(4)] + [("prefix", 0)]
                    for j in range(4):
                        conv_seq += [("u", j), ("conv", j)]
                    conv_seq += [("publish", 0)] + [("B", j) for j in range(4)]
                    ai = 0
                    ji = 0
                    while ai < len(att_stages) or ji < len(conv_seq):
                        if ai < len(att_stages):
                            kind_, ti_ = att_stages[ai]
                            if kind_ in ("S0", "S1"):
                                yield item(2500)
                                att_S(ti_, 0 if kind_ == "S0" else 1)
                            else:
                                yield item(4500)
                                att_O(ti_)
                            ai += 1
                        if ji < len(conv_seq):
                            which, j = conv_seq[ji]
                            if which == "prefix":
                                if prev_blk is None:
                                    yield item(0)
                                else:
                                    yield item(0, ("mark", ("u", prev_blk, l)))
                                conv_prefix()
                            elif which == "conv":
                                yield item(1500)
                                conv_chunk(j)
                            elif which == "publish":
                                yield item(500)
                                conv_publish()
                            else:
                                yield item(JOB, ("gran", gi_(l, which)))
                                conv_job(which, j)
                            ji += 1

                    for m in range(2):
                        g_x = [gi_(l, "gx%d_0" % m), gi_(l, "gx%d_1" % m)]
                        g_oc = gi_(l, "aoco%d" % m)
                        s_oc = g_oc % NSLOT
                        woc = wslots[:, s_oc, 0:KD * 512].rearrange("p (k n) -> p k n", k=KD)
                        for jj in range(4):
                            j = m * 4 + jj
                            half, jl = jj // 2, jj % 2
                            g_gx = g_x[half]
                            s_gx = g_gx % NSLOT
                            wgx = wslots[:, s_gx, 0:KD * 512].rearrange("p (k n) -> p k n", k=KD)
                            gp = j % 2
                            A = gp * 2
                            Cc = gp * 2 + 1
                            yield item(JOB, ("gran", g_gx))
                            ps, pcell = proj(KD, n, lambda k: wgx[:, k, jl * 128:(jl + 1) * 128], lambda k: hT[:, k, 0:n],
                                             lambda k: [Hc(k), HRU], ("w", s_gx))
                            S.op("act", lambda e, ps=ps: e.activation(out=o.gt[:, A, 0:n], in_=ps, func=AF.Sigmoid),
                                 reads=[pcell, R1U], writes=[("gt", s, A)])
                            yield item(4 * BMAX, ("gran", g_oc))
                            ps, pcell = proj(4, n, lambda k: woc[:, k, jj * 128:(jj + 1) * 128], lambda k: o.aT[:, k, 0:n],
                                             lambda k: [("a", s), R1U], ("w", s_oc))
                            S.op("dve", lambda e, ps=ps: e.tensor_tensor(out=o.gt[:, A, 0:n], in0=ps, in1=o.gt[:, A, 0:n], op=ALU.mult),
                                 reads=[pcell, ("gt", s, A), R1U], writes=[("gt", s, A)])
                            yield item(JOB)
                            ps, pcell = proj(KD, n, lambda k: wgx[:, k, 256 + jl * 128:256 + (jl + 1) * 128], lambda k: hT[:, k, 0:n],
                                             lambda k: [Hc(k), HRU], ("w", s_gx))
                            S.op("act", lambda e, ps=ps: e.activation(out=o.gt[:, Cc, 0:n], in_=ps, func=AF.Sigmoid),
                                 reads=[pcell, R1U], writes=[("gt", s, Cc)])
                            yield item(4 * BMAX)
                            ps, pcell = proj(4, n, lambda k: woc[:, 4 + k, jj * 128:(jj + 1) * 128], lambda k: o.BzT[:, k, 0:n],
                                             lambda k: [("bz", s, k), R1U], ("w", s_oc))
                            S.op("dve", lambda e, ps=ps: e.tensor_tensor(out=o.gt[:, Cc, 0:n], in0=ps, in1=o.gt[:, Cc, 0:n], op=ALU.mult),
                                 reads=[pcell, ("gt", s, Cc), R1U], writes=[("gt", s, Cc)])
                            S.op("dve", lambda e, j=j: e.tensor_tensor(out=o.mT[:, j, 0:n], in0=o.gt[:, A, 0:n],
                                                                       in1=o.gt[:, Cc, 0:n], op=ALU.add),
                                 reads=[("gt", s, A), ("gt", s, Cc), R1U], writes=[("m", s, j)])
                            if jl == 1:
                                g_done(g_gx)
                        g_done(g_oc)

                    yield item(0)
                    fence(HRU)
                    for it in proj_to_R2(l, ["wout0", "wout1"], KD, 4,
                                         lambda sl_, i, k: wslots[:, sl_, 0:KD * 512].rearrange("p (k n) -> p k n", k=KD)[:, k, (i % 4) * 128:(i % 4 + 1) * 128],
                                         lambda k: o.mT[:, k, 0:n], lambda k: ("m", s, k), R1U):
                        yield it
                    for it in post_norm(1, l):
                        yield it

                    yield item(0)
                    fence(R1U)
                    for it in make_xg(l):
                        yield it
                    for m in range(8):
                        gidx = gi_(l, "wup%d" % m)
                        sl_ = gidx % NSLOT
                        wu_ = wslots[:, sl_, 0:KD * 512].rearrange("p (k n) -> p k n", k=KD)
                        for jj in range(4):
                            f = m * 4 + jj
                            yield item(JOB, ("gran", gidx))
                            ps, pcell = proj(KD, n, lambda k: wu_[:, k, jj * 128:(jj + 1) * 128], lambda k: hT[:, k, 0:n],
                                             lambda k: [Hc(k), HRU], ("w", sl_))
                            rb = f % 2
                            S.op("act", lambda e, ps=ps, rb=rb: e.activation(out=o.rtmp[:, rb, 0:n], in_=ps, func=AF.Relu),
                                 reads=[pcell, R1U], writes=[("rtmp", s, rb)])
                            S.op("dve", lambda e, f=f, rb=rb: e.tensor_tensor(out=o.actT[:, f, 0:n], in0=o.rtmp[:, rb, 0:n],
                                                                              in1=o.rtmp[:, rb, 0:n], op=ALU.mult),
                                 reads=[("rtmp", s, rb), R1U], writes=[("act", s, f)])
                        g_done(gidx)
                    yield item(0)
                    fence(HRU)
                    for it in proj_to_R2(l, ["wdn%d" % i for i in range(8)], 32, 1,
                                         lambda sl_, i, k: wslots[:, sl_, 0:32 * 128].rearrange("p (k n) -> p k n", k=32)[:, k, :],
                                         lambda k: o.actT[:, k, 0:n], lambda k: ("act", s, k), R1U, fin=stat_fin_mlp):
                        yield it
                    if l == 1:
                        yield item(0)
                        fence(R1U)
                        if pi + 1 < len(PARTS):
                            issue_xin(pi + 1)
                    for it in post_norm(3, l):
                        yield it

                for ti, (kind, g) in enumerate(tiles):
                    yield item(2048)
                    for half in range(2):
                        bank = (PB_ST + PB_S)[(ti * 2 + half) % 4]
                        for kk in range(4):
                            k = half * 4 + kk
                            S.op("pe", lambda e, k=k, kk=kk, bank=bank: e.transpose(
                                out=psb[bank][:, kk * 128:(kk + 1) * 128], in_=xT[:, k, ti * 128:(ti + 1) * 128],
                                identity=identf[:]),
                                reads=[Xc(k), ("identf",)], writes=[PS(bank)], signal=(kk == 3))
                        S.op("act", lambda e, half=half, bank=bank: e.copy(
                            out=o.yout[:, ti, half * 512:(half + 1) * 512], in_=psb[bank][:, :]),
                            reads=[PS(bank), R1U], writes=[("yout", s, ti)])
                    dst = yp[g * 128:(g + 1) * 128, :] if kind == "p" else ys[:, :]
                    S.dma("sp", dst, o.yout[:, ti, :], reads=[("yout", s, ti), R1U])

        gens = [stream(0), stream(1)]
        clocks = [0, SKEW]
        prev = [next(g) for g in gens]
        while any(alive):
            order = sorted([s for s in (0, 1) if alive[s]], key=lambda s: (clocks[s], s))
            for s in order:
                cost, needs = prev[s]
                if all(check_need(nd, s) for nd in needs):
                    clocks[s] += cost
                    for nd in needs:
                        if nd[0] == "gran":
                            cur_gran[s] = max(cur_gran[s], nd[1])
                    try:
                        prev[s] = next(gens[s])
                    except StopIteration:
                        alive[s] = False
                    break
            else:
                raise RuntimeError("both streams blocked: %r" % (prev,))

        assert G["issued"] == len(gran_seq) and all(d == 2 for d in G["done"])
        S.drain("sp")
        build_program.stats = dict(ninst=S.ninst, nwaits=S.nwaits, cnt=dict(S.ccnt), sbuf_left=nc.sbuf_bytes_remaining)
    return nc


_CACHE = {}


def kernel(**inputs):
    f32 = lambda a: np.ascontiguousarray(np.asarray(a, dtype=np.float32))
    x_prompt = f32(inputs["x_prompt"])
    x_sample = f32(inputs["x_sample"])
    cache_k = f32(inputs["cache_k"])
    cache_v = f32(inputs["cache_v"])
    state_conv = f32(inputs["state_conv"])
    shared = {n: f32(inputs[n]) for n in ("g_mix_pre", "g_mix_post", "g_mlp_pre", "g_mlp_post", "w_in", "attn_sinks",
                                          "conv_w", "w_attn_o", "w_conv_o", "w_out", "w_up", "w_down")}
    if "nc" not in _CACHE:
        _CACHE["nc"] = build_program()
    nc = _CACHE["nc"]
    in_maps = []
    for c in range(NCORES):
        s0, s1 = 16 * c, 16 * (c + 1)
        m = dict(shared)
        m["xp"] = x_prompt[c]
        m["xs"] = np.ascontiguousarray(x_sample[s0:s1].reshape(128, D))
        m["ck"] = np.ascontiguousarray(cache_k[:, s0:s1].reshape(2, 16, 128, 128))
        m["cv"] = np.ascontiguousarray(cache_v[:, s0:s1].reshape(2, 16, 128, 128))
        m["sc"] = np.ascontiguousarray(state_conv[:, s0:s1].reshape(2, 32, 512))
        in_maps.append(m)
    res = run_bass_kernel_spmd(nc, in_maps, core_ids=list(range(NCORES)))
    R = res.results
    y_prompt = np.stack([R[c]["yp"] for c in range(NCORES)], axis=0).astype(np.float32)
    y_sample = np.concatenate([R[c]["ys"].reshape(16, 8, D) for c in range(NCORES)], axis=0).astype(np.float32)
    nk_p = np.stack([R[c]["nkp"].reshape(2, 128, 2, 64) for c in range(NCORES)], axis=1).astype(np.float32)
    nv_p = np.stack([R[c]["nvp"].reshape(2, 128, 2, 64) for c in range(NCORES)], axis=1).astype(np.float32)
    nc_p = np.stack([R[c]["ncp"] for c in range(NCORES)], axis=1).astype(np.float32)
    nk_s = np.concatenate([R[c]["nks"].reshape(2, 16, 128, 2, 64) for c in range(NCORES)], axis=1).astype(np.float32)
    nv_s = np.concatenate([R[c]["nvs"].reshape(2, 16, 128, 2, 64) for c in range(NCORES)], axis=1).astype(np.float32)
    nc_s = np.concatenate([R[c]["ncs"].reshape(2, 16, 2, 512) for c in range(NCORES)], axis=1).astype(np.float32)
    return (y_prompt, y_sample, nk_p, nv_p, nc_p, nk_s, nv_s, nc_s)
```

```python
import numpy as np
from contextlib import ExitStack

import concourse.bass as bass
import concourse.mybir as mybir
from concourse.bass_utils import run_bass_kernel_spmd

F32 = mybir.dt.float32
BF16 = mybir.dt.bfloat16
AF = mybir.ActivationFunctionType
ALU = mybir.AluOpType

NCORES = 8
D = 1024
KD = 8
SEQ = 2048
IN_DIM = 4352
DFF = 4096
EPS = 1e-6
NSLOT = 5
SLOT_ELEMS = 4096
SAME_ENGINE_SYNC = True

PARTS = [
    [[("p", 0), ("p", 1), ("p", 2)], [("p", 3), ("p", 4), ("p", 5)]],
    [[("p", 6), ("p", 7), ("p", 8)], [("p", 9), ("p", 10), ("p", 11)]],
    [[("p", 12), ("p", 13), ("p", 14)], [("p", 15), ("s", 0)]],
]
BMAX = 384
SKEW = 26000


class Tok:
    __slots__ = ("sem", "val", "know", "eng")

    def __init__(self, sem, val, know, eng):
        self.sem = sem
        self.val = val
        self.know = know
        self.eng = eng


class Sched:
    def __init__(self, nc, es, ring=8):
        self.nc = nc
        self.engs = {"pe": nc.tensor, "act": nc.scalar, "dve": nc.vector, "pool": nc.gpsimd, "sp": nc.sync}
        self.csem = {}
        self.ccnt = {}
        for e in ("pe", "act", "dve", "pool"):
            self.csem[e] = es.enter_context(nc.semaphore("c_" + e))
            self.ccnt[e] = 0
        self.know = {e: {} for e in self.engs}
        self.dring = {}
        self.dpos = {}
        self.dlast = {}
        for q in ("sp", "pool"):
            self.dring[q] = [es.enter_context(nc.semaphore("d_%s%d" % (q, i))) for i in range(ring)]
            self.dpos[q] = 0
        self.lastw = {}
        self.readers = {}
        self.pending = []
        self.all_dma_toks = []
        self.nwaits = 0
        self.ninst = 0

    def _wait(self, eng, tok):
        if eng == "pe" and tok.eng == "pe":
            return
        if tok.val is None:
            raise RuntimeError("dependency on an unsignaled PE op")
        k = self.know[eng]
        sid = id(tok.sem)
        if k.get(sid, 0) >= tok.val:
            return
        own = tok.eng == eng and tok.sem is self.csem.get(eng)
        if own and (eng == "pe" or not SAME_ENGINE_SYNC):
            k[sid] = tok.val
            return
        self.engs[eng].wait_ge(tok.sem, tok.val)
        self.nwaits += 1
        for s, v in tok.know.items():
            if k.get(s, 0) < v:
                k[s] = v
        k[sid] = tok.val

    def _deps(self, reads, writes):
        deps = []
        for c in reads:
            t = self.lastw.get(c)
            if t is not None:
                deps.append(t)
        for c in writes:
            t = self.lastw.get(c)
            if t is not None:
                deps.append(t)
            r = self.readers.get(c)
            if r:
                deps.extend(r.values())
        return deps

    def _record(self, tok, reads, writes):
        for c in writes:
            self.lastw[c] = tok
            self.readers[c] = {}
        for c in reads:
            self.readers.setdefault(c, {})[id(tok.sem)] = tok

    def op(self, eng, fn, reads=(), writes=(), signal=True):
        ps_r = [c for c in reads if c[0] == "ps"]
        if ps_r:
            reads = [c for c in reads if c[0] != "ps"]
            writes = list(writes) + ps_r
        for t in self._deps(reads, writes):
            self._wait(eng, t)
        inst = fn(self.engs[eng])
        self.ninst += 1
        if signal:
            inst.then_inc(self.csem[eng], 1)
            self.ccnt[eng] += 1
            tok = Tok(self.csem[eng], self.ccnt[eng], dict(self.know[eng]), eng)
            if eng == "pe" and self.pending:
                for p in self.pending:
                    p.val = tok.val
                    p.know = tok.know
                self.pending = []
        else:
            assert eng == "pe"
            tok = Tok(self.csem[eng], None, None, eng)
            self.pending.append(tok)
        self._record(tok, reads, writes)
        return tok

    def dma(self, q, out, in_, reads=(), writes=()):
        ring = self.dring[q]
        sem = ring[self.dpos[q] % len(ring)]
        self.dpos[q] += 1
        prev = self.dlast.get(id(sem))
        for t in self._deps(reads, writes):
            self._wait(q, t)
        if prev is not None:
            self._wait(q, prev)
        val = (prev.val if prev is not None else 0) + 16
        self.engs[q].dma_start(out=out, in_=in_).then_inc(sem, 16)
        self.ninst += 1
        tok = Tok(sem, val, dict(self.know[q]), q)
        self.dlast[id(sem)] = tok
        self._record(tok, reads, writes)
        return tok

    def drain(self, q):
        for t in list(self.dlast.values()):
            self._wait(q, t)


def build_program():
    nc = bass.Bass("TRN2", target_bir_lowering=False)

    def din(name, shape):
        return nc.dram_tensor(name, shape, F32, kind="ExternalInput").ap()

    def dout(name, shape):
        return nc.dram_tensor(name, shape, F32, kind="ExternalOutput").ap()

    xp = din("xp", [SEQ, D])
    xs = din("xs", [128, D])
    ck = din("ck", [2, 16, 128, 128])
    cv = din("cv", [2, 16, 128, 128])
    sc = din("sc", [2, 32, 512])
    gvec = [din(n, [2, D]) for n in ("g_mix_pre", "g_mix_post", "g_mlp_pre", "g_mlp_post")]
    w_in = din("w_in", [2, D, IN_DIM])
    sinks = din("attn_sinks", [2, 8])
    conv_w = din("conv_w", [2, 3, 512])
    w_ao = din("w_attn_o", [2, 512, D])
    w_co = din("w_conv_o", [2, 512, D])
    w_out = din("w_out", [2, D, D])
    w_up = din("w_up", [2, D, DFF])
    w_down = din("w_down", [2, DFF, D])

    yp = dout("yp", [SEQ, D])
    ys = dout("ys", [128, D])
    nkp = dout("nkp", [2, 128, 128])
    nvp = dout("nvp", [2, 128, 128])
    ncp = dout("ncp", [2, 2, 512])
    nks = dout("nks", [2, 16, 128, 128])
    nvs = dout("nvs", [2, 16, 128, 128])
    ncs = dout("ncs", [2, 32, 512])

    es = ExitStack()
    with es:
        S = Sched(nc, es)

        def sb(name, shape, dt):
            return es.enter_context(nc.sbuf_tensor(name, shape, dt))

        NB = BMAX
        R1BYTES = 35904

        class SB:
            pass
        B = []
        for s in range(2):
            o = SB()
            o.xT = sb("xT%d" % s, [128, KD, NB], F32)
            HR = sb("HR%d" % s, [128, KD * NB * 2], BF16)
            o.hT = HR[:, 0:KD * NB].rearrange("p (k t) -> p k t", k=KD)
            o.R2 = HR[:, :].bitcast(F32).rearrange("p (k t) -> p k t", k=KD)
            R1 = sb("R1_%d" % s, [128, R1BYTES // 2], BF16)

            def carve(off_bytes, nelem, dt, R1=R1):
                if dt == BF16:
                    return R1[:, off_bytes // 2: off_bytes // 2 + nelem]
                return R1[:, off_bytes // 2: off_bytes // 2 + nelem * 2].bitcast(F32)
            o.qT = carve(0, 4 * NB, BF16).rearrange("p (c t) -> p c t", c=4)
            o.aT = carve(3072, 4 * NB, BF16).rearrange("p (c t) -> p c t", c=4)
            o.kT = carve(6144, 128 + NB, BF16)
            o.Vt = carve(7168, 4 * 128, BF16).rearrange("p (s d) -> p s d", s=4)
            o.Cf = carve(8192, 4 * NB, F32).rearrange("p (c t) -> p c t", c=4)
            o.Kc = carve(8192, 16 * 128, BF16).rearrange("p (s d) -> p s d", s=16)
            o.uf = carve(14336, 4 * (NB + 2), F32).rearrange("p (c t) -> p c t", c=4)
            o.BzT = carve(20544, 4 * NB, BF16).rearrange("p (c t) -> p c t", c=4)
            o.mT = carve(23616, KD * NB, BF16).rearrange("p (c t) -> p c t", c=KD)
            o.gt = carve(29760, 4 * NB, F32).rearrange("p (a t) -> p a t", a=4)
            o.actT = carve(0, 32 * NB, BF16).rearrange("p (c t) -> p c t", c=32)
            o.rtmp = carve(24576, 2 * NB, F32).rearrange("p (a t) -> p a t", a=2)
            o.xin = carve(0, 3 * 1024, F32).rearrange("p (a t) -> p a t", a=3)
            o.yout = carve(12288, 3 * 1024, F32).rearrange("p (a t) -> p a t", a=3)
            o.Pt = sb("Pt%d" % s, [128, 2, 2, 4, 128], BF16)
            o.sq = sb("sq%d" % s, [128, KD, NB], BF16)
            o.rt = sb("rt%d" % s, [128, NB], F32)
            o.rr = sb("rr%d" % s, [128, NB], F32)
            o.dent = sb("dent%d" % s, [128, 512], F32)
            o.R1U = ("R1use", s)
            o.HRU = ("HRuse", s)
            o.par = 0
            o.sqi = 0
            B.append(o)

        wslots = sb("wslots", [128, NSLOT, SLOT_ELEMS], BF16)
        KcT = sb("KcT", [128, 16, 128], BF16)
        Vc = sb("Vc", [128, 16, 128], BF16)
        identf = sb("identf", [128, 128], F32)
        identb = sb("identb", [128, 128], BF16)
        onesb = sb("onesb", [128, 128], BF16)
        M_prompt = sb("M_prompt", [128, 2, 128], BF16)
        M_first = sb("M_first", [128, 2, 128], BF16)
        M_sample = sb("M_sample", [128, 2, 128], BF16)
        gcol = sb("gcol", [128, 128], F32)
        esk = sb("esk", [128, 16], F32)
        sinkcol = sb("sinkcol", [128, 2, 4], F32)
        kcarry = sb("kcarry", [128, 2, 128], BF16)
        vcarry = sb("vcarry", [128, 2, 128], BF16)
        ucarry = sb("ucarry", [128, 2, 4, 2], F32)
        us = sb("us", [128, 4, 16, 10], F32)
        ncstg = sb("ncstg", [128, 4, 32], F32)
        ostg = sb("ostg", [128, 512], F32)
        fence_scr = sb("fence_scr", [128, 4], F32)
        mtmp = B[1].gt[:, :, 0:128]
        gstage = B[1].gt[:, 0, 128:256]

        psb = [es.enter_context(nc.psum_tensor("psb%d" % i, [128, 512], F32)) for i in range(8)]
        PB_ALL = [0, 1, 2, 3]
        projctr = [0]
        PB_S = [4, 5]
        PB_O = 6
        PB_D = 7
        PB_ST = [6, 7]

        def PS(b):
            return ("ps", b)

        def fence(cell):
            S.op("dve", lambda e: e.memset(fence_scr[:, 0:1], 0.0), reads=[], writes=[cell, ("fscr",)])

        def granules(l):
            g = []


            def cols(c0, n_):
                def f(slot):
                    src = w_in[l, :, c0:c0 + n_].rearrange("(k p) n -> p k n", p=128)
                    dst = slot[:, 0:KD * n_].rearrange("p (k n) -> p k n", k=KD)
                    return [(dst, src)]
                return f
            g.append(("q", cols(0, 512)))
            g.append(("kv", cols(512, 256)))
            g.append(("C", cols(1280, 512)))
            g.append(("u", cols(1792, 512)))
            g.append(("B", cols(768, 512)))
            for m in range(2):
                def gx(slot, m=m, half=0):
                    dst = slot[:, 0:KD * 512].rearrange("p (k n) -> p k n", k=KD)
                    c_a = 2304 + m * 512 + half * 256
                    c_c = 3328 + m * 512 + half * 256
                    return [(dst[:, :, 0:256], w_in[l, :, c_a:c_a + 256].rearrange("(k p) n -> p k n", p=128)),
                            (dst[:, :, 256:512], w_in[l, :, c_c:c_c + 256].rearrange("(k p) n -> p k n", p=128))]
                g.append(("gx%d_0" % m, lambda slot, m=m: gx(slot, m, 0)))

                def wo(slot, m=m):
                    dst = slot[:, 0:KD * 512].rearrange("p (k n) -> p k n", k=KD)
                    r = []
                    r.append((dst[0:64, 0:4, :], w_ao[l, 0:256, m * 512:(m + 1) * 512].rearrange("(c p) n -> p c n", p=64)))
                    r.append((dst[64:128, 0:4, :], w_ao[l, 256:512, m * 512:(m + 1) * 512].rearrange("(c p) n -> p c n", p=64)))
                    r.append((dst[:, 4:8, :], w_co[l, :, m * 512:(m + 1) * 512].rearrange("(k p) n -> p k n", p=128)))
                    return r
                g.append(("aoco%d" % m, wo))
                g.append(("gx%d_1" % m, lambda slot, m=m: gx(slot, m, 1)))
            for m in range(2):
                def wout(slot, m=m):
                    return [(slot[:, 0:KD * 512].rearrange("p (k n) -> p k n", k=KD),
                             w_out[l, :, m * 512:(m + 1) * 512].rearrange("(k p) n -> p k n", p=128))]
                g.append(("wout%d" % m, wout))
            for m in range(8):
                def wup(slot, m=m):
                    return [(slot[:, 0:KD * 512].rearrange("p (k n) -> p k n", k=KD),
                             w_up[l, :, m * 512:(m + 1) * 512].rearrange("(k p) n -> p k n", p=128))]
                g.append(("wup%d" % m, wup))
            for i in range(8):
                def wdn(slot, i=i):
                    return [(slot[:, 0:32 * 128].rearrange("p (k n) -> p k n", k=32),
                             w_down[l, :, i * 128:(i + 1) * 128].rearrange("(k p) n -> p k n", p=128))]
                g.append(("wdn%d" % i, wdn))
            return g

        gran_seq = []
        gran_idx = {}
        for pi in range(len(PARTS)):
            for l in range(2):
                for name, f in granules(l):
                    gran_idx[(pi, l, name)] = len(gran_seq)
                    gran_seq.append(f)
        G = dict(issued=0, done=[0] * len(gran_seq))

        def g_issuable(i):
            j = i - NSLOT
            return j < 0 or G["done"][j] >= 2

        def g_prefetch():
            while G["issued"] < len(gran_seq) and g_issuable(G["issued"]):
                i = G["issued"]
                slot_i = i % NSLOT
                for dst, src in gran_seq[i](wslots[:, slot_i, :]):
                    S.dma("pool", dst, src, writes=[("w", slot_i)])
                G["issued"] += 1

        def g_available(i):
            return i < G["issued"]

        def g_done(i):
            G["done"][i] += 1
            g_prefetch()


        SETUP = [B[1].R1U]
        for tj_, (kind_j, g_j) in enumerate(PARTS[0][0]):
            S.dma("sp", B[0].xin[:, tj_, :], xp[g_j * 128:(g_j + 1) * 128, :], reads=[B[0].R1U], writes=[("xin", 0, tj_)])
        S.op("pool", lambda e: e.memset(identf[:], 1.0), writes=[("identf",)])
        S.op("pool", lambda e: e.affine_select(out=identf[:], in_=identf[:], pattern=[[-1, 128]],
                                                compare_op=ALU.is_equal, fill=0.0, base=0, channel_multiplier=1),
             reads=[("identf",)], writes=[("identf",)])
        S.op("dve", lambda e: e.memset(gstage, 0.0), reads=SETUP, writes=[("gstage", i) for i in range(6)])
        for v in range(4):
            S.dma("sp", gstage[v * 16:(v + 1) * 16, :], gvec[v].rearrange("l (k p) -> (l k) p", p=128),
                  reads=SETUP, writes=[("gstage", v)])
        S.dma("sp", gstage[64:88, :], conv_w.rearrange("l i (j p) -> (l i j) p", p=128), reads=SETUP, writes=[("gstage", 4)])
        S.dma("sp", esk[:], sinks.rearrange("l h -> (l h)").partition_broadcast(128), writes=[("esk",)])
        g_prefetch()
        S.op("dve", lambda e: e.tensor_copy(out=identb[:], in_=identf[:]), reads=[("identf",)], writes=[("identb",)])
        S.op("dve", lambda e: e.memset(onesb[:], 1.0), writes=[("onesb",)])
        S.op("pe", lambda e: e.transpose(out=psb[6][:, 0:128], in_=gstage, identity=identf[:]),
             reads=SETUP + [("gstage", i) for i in range(6)] + [("identf",)], writes=[PS(6)])
        S.op("act", lambda e: e.copy(out=gcol[:], in_=psb[6][:, 0:128]), reads=[PS(6)], writes=[("gcol",)])

        def gc_(vec, l, k):
            i = vec * 16 + l * 8 + k
            return gcol[:, i:i + 1]

        def cw_(l, i, j):
            n_ = 64 + l * 12 + i * 4 + j
            return gcol[:, n_:n_ + 1]

        S.op("act", lambda e: e.activation(out=esk[:], in_=esk[:], func=AF.Exp), reads=[("esk",)], writes=[("esk",)])
        for l in range(2):
            S.op("dve", lambda e, l=l: e.tensor_copy(out=sinkcol[0:64, l, :], in_=esk[0:64, l * 8:l * 8 + 4]),
                 reads=[("esk",)], writes=[("sinkcol",)])
            S.op("dve", lambda e, l=l: e.tensor_copy(out=sinkcol[64:128, l, :], in_=esk[64:128, l * 8 + 4:l * 8 + 8]),
                 reads=[("esk",)], writes=[("sinkcol",)])
        S.op("pool", lambda e: e.memset(mtmp, 1.0), reads=SETUP, writes=[("mtmp",)])
        S.op("pool", lambda e: e.affine_select(out=mtmp[:, 0, :], in_=mtmp[:, 0, :], pattern=[[1, 128]],
                                                compare_op=ALU.is_ge, fill=0.0, base=0, channel_multiplier=-1),
             reads=SETUP + [("mtmp",)], writes=[("mtmp",)])
        S.op("pool", lambda e: e.affine_select(out=mtmp[:, 1, :], in_=mtmp[:, 1, :], pattern=[[-1, 128]],
                                                compare_op=ALU.is_gt, fill=0.0, base=0, channel_multiplier=1),
             reads=SETUP + [("mtmp",)], writes=[("mtmp",)])
        m2v = mtmp[:, 2, :].rearrange("p (s t) -> p s t", s=16)
        S.op("pool", lambda e: e.affine_select(out=m2v, in_=m2v, pattern=[[8, 16], [1, 8]],
                                                compare_op=ALU.is_ge, fill=0.0, base=0, channel_multiplier=-1),
             reads=SETUP + [("mtmp",)], writes=[("mtmp",)])
        S.op("pool", lambda e: e.affine_select(out=m2v, in_=m2v, pattern=[[-8, 16], [0, 8]],
                                                compare_op=ALU.is_ge, fill=0.0, base=0, channel_multiplier=1),
             reads=SETUP + [("mtmp",)], writes=[("mtmp",)])
        m3v = mtmp[:, 3, :].rearrange("p (s t) -> p s t", s=16)
        S.op("pool", lambda e: e.affine_select(out=m3v, in_=m3v, pattern=[[0, 16], [-1, 8]],
                                                compare_op=ALU.is_gt, fill=0.0, base=0, channel_multiplier=1),
             reads=SETUP + [("mtmp",)], writes=[("mtmp",)])
        S.op("dve", lambda e: e.memset(M_first[:, 0, :], 0.0), writes=[("masks",)])
        for dst, srci in ((M_prompt[:, 0, :], 1), (M_prompt[:, 1, :], 0), (M_first[:, 1, :], 0),
                          (M_sample[:, 0, :], 3), (M_sample[:, 1, :], 2)):
            S.op("dve", lambda e, dst=dst, srci=srci: e.tensor_copy(out=dst, in_=mtmp[:, srci, :]),
                 reads=SETUP + [("mtmp",)], writes=[("masks",)])
        for l in range(2):
            S.dma("sp", nks[l, :, 0:120, :], ck[l, :, 8:128, :])
            S.dma("sp", nvs[l, :, 0:120, :], cv[l, :, 8:128, :])

        marks = set()

        cur_gran = [-1, -1]
        alive = [True, True]
        MAXLEAD = 2

        def check_need(nd, s_):
            if nd[0] == "gran":
                if alive[1 - s_] and nd[1] - cur_gran[1 - s_] > MAXLEAD:
                    return False
                return g_available(nd[1])
            if nd[0] == "mark":
                return nd[1] in marks
            raise AssertionError(nd)

        def item(cost, *needs):
            return (cost, needs)

        def stream(s):
            o = B[s]
            xT, hT, R2 = o.xT, o.hT, o.R2
            R1U, HRU = o.R1U, o.HRU

            def Xc(k):
                return ("x", s, k)

            def Hc(k):
                return ("h", s, k)

            def R2c(k):
                return ("r2", s, k)

            def proj(K, n, lhs_fn, rhs_fn, reads_fn, wcell, after_mm=None):
                bank = PB_ALL[projctr[0] % 4]
                projctr[0] += 1
                for k in range(K):
                    S.op("pe", lambda e, k=k: e.matmul(psb[bank][:, 0:n], lhsT=lhs_fn(k), rhs=rhs_fn(k),
                                                       start=(k == 0), stop=(k == K - 1)),
                         reads=[wcell] + reads_fn(k), writes=[PS(bank)], signal=(k == K - 1))
                return psb[bank][:, 0:n], PS(bank)

            def stat_sq(src_ap, n, reads, i):
                S.op("act", lambda e: e.activation(out=o.sq[:, i, 0:n], in_=src_ap, func=AF.Square),
                     reads=reads, writes=[("sq", s, i)])

            def stat_mms(n):
                bank = PB_ST[s]
                for k in range(KD):
                    S.op("pe", lambda e, k=k: e.matmul(psb[bank][:, 0:n], lhsT=onesb[:], rhs=o.sq[:, k, 0:n],
                                                       start=(k == 0), stop=(k == KD - 1)),
                         reads=[("sq", s, k), ("onesb",)], writes=[PS(bank)], signal=(k == KD - 1))

            def stat_fin(n):
                bank = PB_ST[s]
                S.op("act", lambda e: e.activation(out=o.rt[:, 0:n], in_=psb[bank][:, 0:n], func=AF.Ln,
                                                   bias=EPS, scale=1.0 / D),
                     reads=[PS(bank)], writes=[("rt", s)])
                S.op("act", lambda e: e.activation(out=o.rr[:, 0:n], in_=o.rt[:, 0:n], func=AF.Exp, scale=-0.5),
                     reads=[("rt", s)], writes=[("r", s)])

            def stat_fin_mlp(n):
                bank = PB_ST[s]
                S.op("dve", lambda e: e.scalar_tensor_tensor(out=o.rr[:, 0:n], in0=o.rt[:, 0:n], scalar=EPS * D,
                                                             in1=psb[bank][:, 0:n], op0=ALU.mult, op1=ALU.add),
                     reads=[("rt", s), PS(bank)], writes=[("r", s)])
                S.op("act", lambda e: e.activation(out=o.rr[:, 0:n], in_=o.rr[:, 0:n], func=AF.Ln, scale=1.0 / D),
                     reads=[("r", s)], writes=[("r", s)])
                S.op("act", lambda e: e.activation(out=o.rr[:, 0:n], in_=o.rr[:, 0:n], func=AF.Exp, scale=-0.5),
                     reads=[("r", s)], writes=[("r", s)])

            for pi in range(len(PARTS)):
                tiles = PARTS[pi][s]
                NT = len(tiles)
                n = NT * 128
                has_sample = tiles[-1][0] == "s"
                NTP = NT - 1 if has_sample else NT
                TP = NTP * 128
                scol = TP
                last_block = (pi == len(PARTS) - 1 and s == 1)
                prev_blk = (pi, 0) if s == 1 else ((pi - 1, 1) if pi > 0 else None)
                JOB = KD * BMAX

                def gi_(l, name):
                    return gran_idx[(pi, l, name)]

                def issue_xin(pj):
                    for tj, (kind_j, g_j) in enumerate(PARTS[pj][s]):
                        src = xp[g_j * 128:(g_j + 1) * 128, :] if kind_j == "p" else xs[:, :]
                        S.dma("sp", o.xin[:, tj, :], src, reads=[R1U], writes=[("xin", s, tj)])

                if pi == 0:
                    yield item(0)
                    fence(R1U)
                    if s == 1:
                        issue_xin(0)
                for ti, (kind, g) in enumerate(tiles):
                    yield item(2048)
                    for half in range(2):
                        bank = (PB_ST + PB_S)[(ti * 2 + half) % 4]
                        for kk in range(4):
                            k = half * 4 + kk
                            S.op("pe", lambda e, k=k, kk=kk, bank=bank: e.transpose(
                                out=psb[bank][:, kk * 128:(kk + 1) * 128], in_=o.xin[:, ti, k * 128:(k + 1) * 128],
                                identity=identf[:]),
                                reads=[("xin", s, ti), ("identf",), R1U], writes=[PS(bank)], signal=(kk == 3))
                        S.op("act", lambda e, half=half, bank=bank: e.copy(
                            out=xT[:, half * 4:(half + 1) * 4, ti * 128:(ti + 1) * 128],
                            in_=psb[bank][:, :].rearrange("p (k t) -> p k t", k=4)),
                            reads=[PS(bank)], writes=[Xc(half * 4 + kk) for kk in range(4)])

                def make_h(vec, l):
                    yield item(0)
                    fence(HRU)
                    yield item(8000)
                    for k in range(KD):
                        stat_sq(xT[:, k, 0:n], n, [Xc(k)], k)
                    yield item(3500)
                    stat_mms(n)
                    stat_fin(n)
                    for k in range(KD):
                        yield item(1000)
                        S.op("dve", lambda e, k=k: e.scalar_tensor_tensor(
                            out=hT[:, k, 0:n], in0=xT[:, k, 0:n], scalar=gc_(vec, l, k),
                            in1=o.rr[:, 0:n], op0=ALU.mult, op1=ALU.mult),
                            reads=[Xc(k), ("r", s), ("gcol",), HRU], writes=[Hc(k)])

                def make_xg(l):
                    for k in range(KD):
                        yield item(700)
                        S.op("act", lambda e, k=k: e.mul(hT[:, k, 0:n], xT[:, k, 0:n], gc_(2, l, k)),
                             reads=[Xc(k), ("gcol",)], writes=[Hc(k), R2c(k // 2)])
                    yield item(1500)
                    for k in range(KD):
                        stat_sq(xT[:, k, 0:n], n, [Xc(k)], k)
                    yield item(1500)
                    stat_mms(n)
                    bank_ = PB_ST[s]
                    S.op("act", lambda e: e.activation(out=o.rt[:, 0:n], in_=psb[bank_][:, 0:n], func=AF.Square,
                                                       bias=EPS, scale=1.0 / D),
                         reads=[PS(bank_)], writes=[("rt", s)])

                def post_norm(vec, l):
                    yield item(3000)
                    for k in range(KD):
                        yield item(2400)
                        S.op("dve", lambda e, k=k: e.scalar_tensor_tensor(
                            out=R2[:, k, 0:n], in0=R2[:, k, 0:n], scalar=gc_(vec, l, k),
                            in1=o.rr[:, 0:n], op0=ALU.mult, op1=ALU.mult),
                            reads=[R2c(k), ("r", s), ("gcol",), HRU], writes=[R2c(k)])
                        S.op("dve", lambda e, k=k: e.tensor_tensor(
                            out=xT[:, k, 0:n], in0=xT[:, k, 0:n], in1=R2[:, k, 0:n], op=ALU.add),
                            reads=[R2c(k), Xc(k), HRU], writes=[Xc(k)])

                def proj_to_R2(l, names, K, nper, lhs_of, rhs_of, rcell_of, rfence, fin=None):
                    for i in range(KD):
                        gname = names[i // nper]
                        gidx = gi_(l, gname)
                        first = (i % nper == 0)
                        yield item(K * BMAX, ("gran", gidx)) if first else item(K * BMAX)
                        sl_ = gidx % NSLOT
                        ps, pcell = proj(K, n, lambda k, i=i, sl_=sl_: lhs_of(sl_, i, k), rhs_of,
                                         lambda k: [rcell_of(k), rfence], ("w", sl_))
                        S.op("act", lambda e, i=i, ps=ps: e.copy(out=R2[:, i, 0:n], in_=ps), reads=[pcell, HRU],
                             writes=[R2c(i)])
                        stat_sq(ps, n, [pcell], i)
                        if i % nper == nper - 1:
                            g_done(gidx)
                    yield item(3500)
                    stat_mms(n)
                    (fin or stat_fin)(n)

                for l in range(2):
                    yield item(0)
                    fence(R1U)
                    for it in make_h(0, l):
                        yield it

                    if has_sample:
                        yield item(4000)
                        S.dma("pool", o.Kc, ck[l].rearrange("s k d -> k s d"), reads=[R1U], writes=[("Kc",), ("C", s)])
                        S.dma("pool", Vc[:], cv[l].rearrange("s k d -> k s d"), writes=[("Vc",)])
                        for grp in range(2):
                            bank = PB_ST[grp]
                            pbf = psb[bank][:, :].bitcast(BF16).rearrange("p (s k) -> p s k", s=8)
                            for s8 in range(8):
                                sq_ = grp * 8 + s8
                                S.op("pe", lambda e, sq_=sq_, s8=s8, pbf=pbf: e.transpose(out=pbf[:, s8, :], in_=o.Kc[:, sq_, :],
                                                                                         identity=identb[:]),
                                     reads=[("Kc",), ("identb",), R1U], writes=[PS(bank)], signal=(s8 == 7))
                            S.op("act", lambda e, grp=grp, pbf=pbf: e.copy(out=KcT[:, grp * 8:(grp + 1) * 8, :], in_=pbf),
                                 reads=[PS(bank)], writes=[("KcT",)])
                        S.dma("sp", ostg[0:32, :], sc[l], writes=[("ostg",)])
                        bank = PB_ST[0]
                        for j in range(4):
                            S.op("pe", lambda e, j=j: e.transpose(out=psb[bank][:, j * 32:(j + 1) * 32],
                                                                  in_=ostg[0:32, j * 128:(j + 1) * 128],
                                                                  identity=identf[0:32, 0:32]),
                                 reads=[("ostg",), ("identf",)], writes=[PS(bank)], signal=(j == 3))
                        S.op("act", lambda e: e.copy(out=us[:, :, :, 0:2],
                                                     in_=psb[bank][:, 0:128].rearrange("p (j s r) -> p j s r", j=4, s=16)),
                             reads=[PS(bank)], writes=[("us",)])

                    gq = gi_(l, "q")
                    slq = gq % NSLOT
                    wq = wslots[:, slq, 0:KD * 512].rearrange("p (k n) -> p k n", k=KD)
                    for c in range(4):
                        yield item(JOB, ("gran", gq))
                        bank = PB_ALL[projctr[0] % 4]
                        projctr[0] += 1
                        for k in range(KD):
                            for hf_, h_ in enumerate((c, 4 + c)):
                                S.op("pe", lambda e, k=k, hf_=hf_, h_=h_: e.matmul(
                                    psb[bank][hf_ * 64:(hf_ + 1) * 64, 0:n], lhsT=wq[:, k, h_ * 64:(h_ + 1) * 64],
                                    rhs=hT[:, k, 0:n], start=(k == 0), stop=(k == KD - 1)),
                                    reads=[("w", slq), Hc(k), HRU], writes=[PS(bank)], signal=(k == KD - 1 and hf_ == 1))
                        ps, pcell = psb[bank][:, 0:n], PS(bank)
                        S.op("act", lambda e, c=c, ps=ps: e.copy(out=o.qT[:, c, 0:n], in_=ps), reads=[pcell, R1U],
                             writes=[("q", s, c)])
                    g_done(gq)
                    gkv = gi_(l, "kv")
                    slkv = gkv % NSLOT
                    wkv = wslots[:, slkv, 0:KD * 256].rearrange("p (k n) -> p k n", k=KD)
                    yield item(JOB, ("gran", gkv))
                    ps, pcell = proj(KD, n, lambda k: wkv[:, k, 0:128], lambda k: hT[:, k, 0:n],
                                     lambda k: [Hc(k), HRU], ("w", slkv))
                    S.op("act", lambda e, ps=ps: e.copy(out=o.kT[:, 128:128 + n], in_=ps), reads=[pcell, R1U], writes=[("k", s)])
                    for ti, (kind, g) in enumerate(tiles):
                        yield item(KD * 256)
                        need_out = (kind == "s") or (kind == "p" and g == 15)
                        bank = PB_ST[ti % 2]
                        c0w, nw = (0, 256) if need_out else (128, 128)
                        for k in range(KD):
                            S.op("pe", lambda e, k=k, bank=bank, c0w=c0w, nw=nw: e.matmul(
                                psb[bank][:, 0:nw], lhsT=hT[:, k, ti * 128:(ti + 1) * 128], rhs=wkv[:, k, c0w:c0w + nw],
                                start=(k == 0), stop=(k == KD - 1)),
                                reads=[("w", slkv), Hc(k), HRU], writes=[PS(bank)], signal=(k == KD - 1))
                        voff = 128 if need_out else 0
                        S.op("act", lambda e, bank=bank, voff=voff: e.copy(out=o.Vt[:, 1 + ti, :], in_=psb[bank][:, voff:voff + 128]),
                             reads=[PS(bank), R1U], writes=[("v", s)])
                        if need_out:
                            S.op("dve", lambda e, bank=bank: e.tensor_copy(out=ostg[:, 0:256], in_=psb[bank][:, 0:256]),
                                 reads=[PS(bank)], writes=[("ostg",)])
                            if kind == "p":
                                S.dma("sp", nkp[l], ostg[:, 0:128], reads=[("ostg",)])
                                S.dma("sp", nvp[l], ostg[:, 128:256], reads=[("ostg",)])
                            else:
                                for sq_ in range(16):
                                    S.dma("sp", nks[l, sq_, 120:128, :], ostg[sq_ * 8:(sq_ + 1) * 8, 0:128], reads=[("ostg",)])
                                    S.dma("sp", nvs[l, sq_, 120:128, :], ostg[sq_ * 8:(sq_ + 1) * 8, 128:256], reads=[("ostg",)])
                    g_done(gkv)
                    if prev_blk is None:
                        yield item(0)
                        S.op("dve", lambda e: e.memset(o.kT[:, 0:128], 0.0), reads=[R1U], writes=[("k", s)])
                        S.op("dve", lambda e: e.memset(o.Vt[:, 0, :], 0.0), reads=[R1U], writes=[("v", s)])
                    else:
                        yield item(0, ("mark", ("kv", prev_blk, l)))
                        S.op("dve", lambda e: e.tensor_copy(out=o.kT[:, 0:128], in_=kcarry[:, l, :]),
                             reads=[("kcarry", l), R1U], writes=[("k", s)])
                        S.op("dve", lambda e: e.tensor_copy(out=o.Vt[:, 0, :], in_=vcarry[:, l, :]),
                             reads=[("vcarry", l), R1U], writes=[("v", s)])
                    if not last_block:
                        S.op("dve", lambda e: e.tensor_copy(out=kcarry[:, l, :], in_=o.kT[:, TP:TP + 128]),
                             reads=[("k", s), R1U], writes=[("kcarry", l)])
                        S.op("dve", lambda e: e.tensor_copy(out=vcarry[:, l, :], in_=o.Vt[:, NTP, :]),
                             reads=[("v", s), R1U], writes=[("vcarry", l)])
                    marks.add(("kv", (pi, s), l))

                    def att_S(ti, gi):
                        kind, g = tiles[ti]
                        tcols = slice(ti * 128, (ti + 1) * 128)
                        prevc = slice(ti * 128, (ti + 1) * 128)
                        ownc = slice(128 + ti * 128, 128 + (ti + 1) * 128)
                        if True:
                            for hf in range(2):
                                bank = PB_S[hf]
                                rows = slice(hf * 64, (hf + 1) * 64)
                                for c2 in range(2):
                                    c = gi * 2 + c2
                                    base = c2 * 256
                                    if kind == "p":
                                        S.op("pe", lambda e, c=c, base=base: e.matmul(
                                            psb[bank][:, base:base + 128], lhsT=o.kT[rows, prevc], rhs=o.qT[rows, c, tcols],
                                            start=True, stop=True),
                                            reads=[("k", s), ("q", s, c), R1U], writes=[PS(bank)], signal=False)
                                    else:
                                        for sq_ in range(16):
                                            S.op("pe", lambda e, c=c, base=base, sq_=sq_: e.matmul(
                                                psb[bank][:, base + sq_ * 8:base + sq_ * 8 + 8], lhsT=KcT[rows, sq_, :],
                                                rhs=o.qT[rows, c, scol + sq_ * 8:scol + sq_ * 8 + 8], start=True, stop=True),
                                                reads=[("KcT",), ("q", s, c), R1U], writes=[PS(bank)], signal=False)
                                    S.op("pe", lambda e, c=c, base=base: e.matmul(
                                        psb[bank][:, base + 128:base + 256], lhsT=o.kT[rows, ownc], rhs=o.qT[rows, c, tcols],
                                        start=True, stop=True),
                                        reads=[("k", s), ("q", s, c), R1U], writes=[PS(bank)], signal=(c2 == 1))
                                S.op("act", lambda e, gi=gi, hf=hf, bank=bank: e.activation(
                                    out=o.Pt[:, hf, :, gi * 2:gi * 2 + 2, :].rearrange("p a c q -> p c a q"),
                                    in_=psb[bank][:, :].rearrange("p (c a q) -> p c a q", c=2, a=2),
                                    func=AF.Exp, scale=0.125),
                                    reads=[PS(bank)], writes=[("Pt", s, hf, gi)])
                        if gi == 0:
                            return
                        if kind == "s":
                            M = M_sample
                        elif g == 0:
                            M = M_first
                        else:
                            M = M_prompt
                        for hf in range(2):
                            S.op("dve", lambda e, hf=hf, M=M: e.tensor_tensor(
                                out=o.Pt[:, hf], in0=o.Pt[:, hf], in1=M[:, :, None, :].to_broadcast([128, 2, 4, 128]), op=ALU.mult),
                                reads=[("Pt", s, hf, 0), ("Pt", s, hf, 1), ("masks",)],
                                writes=[("Pt", s, hf, 0), ("Pt", s, hf, 1)])

                    def att_O(ti):
                        kind, g = tiles[ti]
                        Pt = o.Pt
                        for hf in range(2):
                            rows = slice(hf * 64, (hf + 1) * 64)
                            pcells = [("Pt", s, hf, 0), ("Pt", s, hf, 1)]
                            if kind == "p":
                                S.op("pe", lambda e, hf=hf: e.matmul(
                                    psb[PB_O][rows, :], lhsT=o.Vt[:, ti, hf * 64:(hf + 1) * 64],
                                    rhs=Pt[:, hf, 0].rearrange("p c q -> p (c q)"), start=True, stop=False),
                                    reads=pcells + [("v", s), R1U], writes=[PS(PB_O)], signal=False)
                                S.op("pe", lambda e, hf=hf: e.matmul(
                                    psb[PB_O][rows, :], lhsT=o.Vt[:, 1 + ti, hf * 64:(hf + 1) * 64],
                                    rhs=Pt[:, hf, 1].rearrange("p c q -> p (c q)"), start=False, stop=True),
                                    reads=pcells + [("v", s), R1U], writes=[PS(PB_O)], signal=False)
                                r0 = Pt[:, hf, 0].rearrange("p c q -> p (c q)")
                                r1 = Pt[:, hf, 1].rearrange("p c q -> p (c q)")
                            else:
                                S.op("pe", lambda e, hf=hf: e.matmul(
                                    psb[PB_O][rows, :], lhsT=o.Vt[:, 1 + ti, hf * 64:(hf + 1) * 64],
                                    rhs=Pt[:, hf, 1].rearrange("p c (s t) -> p s c t", s=16), start=True, stop=False),
                                    reads=pcells + [("v", s), R1U], writes=[PS(PB_O)], signal=False)
                                for sq_ in range(16):
                                    S.op("pe", lambda e, hf=hf, sq_=sq_: e.matmul(
                                        psb[PB_O][rows, sq_ * 32:(sq_ + 1) * 32],
                                        lhsT=Vc[:, sq_, hf * 64:(hf + 1) * 64],
                                        rhs=Pt[:, hf, 0, :, sq_ * 8:(sq_ + 1) * 8], start=False, stop=(sq_ == 15)),
                                        reads=pcells + [("Vc",)], writes=[PS(PB_O)], signal=False)
                                r0 = Pt[:, hf, 0].rearrange("p c (s t) -> p s c t", s=16)
                                r1 = Pt[:, hf, 1].rearrange("p c (s t) -> p s c t", s=16)
                            S.op("pe", lambda e, hf=hf, r0=r0: e.matmul(
                                psb[PB_D][rows, :], lhsT=onesb[:, 0:64], rhs=r0, start=True, stop=False),
                                reads=pcells + [("onesb",)], writes=[PS(PB_D)], signal=False)
                            S.op("pe", lambda e, hf=hf, r1=r1: e.matmul(
                                psb[PB_D][rows, :], lhsT=onesb[:, 0:64], rhs=r1, start=False, stop=True),
                                reads=pcells + [("onesb",)], writes=[PS(PB_D)], signal=(hf == 1))
                        if kind == "p":
                            dview = lambda ap: ap.rearrange("p (c q) -> p c q", c=4)
                            aview = o.aT[:, :, ti * 128:(ti + 1) * 128]
                        else:
                            dview = lambda ap: ap.rearrange("p (s c t) -> p c s t", s=16, c=4)
                            aview = o.aT[:, :, ti * 128:(ti + 1) * 128].rearrange("p c (s t) -> p c s t", s=16)
                        for c in range(4):
                            S.op("act", lambda e, c=c: e.activation(
                                out=dview(o.dent[:])[:, c], in_=dview(psb[PB_D][:, :])[:, c], func=AF.Ln,
                                bias=sinkcol[:, l, c:c + 1]),
                                reads=[PS(PB_D), ("sinkcol",)], writes=[("dent", s)])
                        S.op("act", lambda e: e.activation(out=o.dent[:], in_=o.dent[:], func=AF.Exp, scale=-1.0),
                             reads=[("dent", s)], writes=[("dent", s)])
                        S.op("dve", lambda e: e.tensor_tensor(
                            out=aview, in0=dview(psb[PB_O][:, :]), in1=dview(o.dent[:]), op=ALU.mult),
                            reads=[PS(PB_O), ("dent", s), R1U], writes=[("a", s)])

                    att_stages = []
                    for ti in range(NT):
                        att_stages.append(("S0", ti))
                        att_stages.append(("S1", ti))
                        att_stages.append(("O", ti))

                    def conv_job(which, j):
                        gname = {"C": "C", "u": "u", "B": "B"}[which]
                        gidx = gi_(l, gname)
                        sl_ = gidx % NSLOT
                        wv = wslots[:, sl_, 0:KD * 512].rearrange("p (k n) -> p k n", k=KD)
                        ps, pcell = proj(KD, n, lambda k: wv[:, k, j * 128:(j + 1) * 128], lambda k: hT[:, k, 0:n],
                                         lambda k: [Hc(k), HRU], ("w", sl_))
                        if which == "C":
                            wr = [("C", s, j)] + ([("Kc",)] if has_sample else [])
                            S.op("act", lambda e: e.copy(out=o.Cf[:, j, 0:n], in_=ps), reads=[pcell, R1U], writes=wr)
                        elif which == "u":
                            S.op("dve", lambda e: e.tensor_tensor(out=o.uf[:, j, 2:2 + n], in0=ps, in1=o.Cf[:, j, 0:n], op=ALU.mult),
                                 reads=[pcell, ("C", s, j), R1U], writes=[("u", s, j)])
                        else:
                            S.op("dve", lambda e: e.tensor_tensor(out=o.BzT[:, j, 0:n], in0=ps, in1=o.Cf[:, j, 0:n], op=ALU.mult),
                                 reads=[pcell, ("C", s, j), R1U], writes=[("bz", s, j)])
                        if j == 3:
                            g_done(gidx)

                    def conv_prefix():
                        if prev_blk is None:
                            S.op("dve", lambda e: e.memset(o.uf[:, :, 0:2], 0.0), reads=[R1U], writes=[("upre", s)])
                        else:
                            S.op("dve", lambda e: e.tensor_copy(out=o.uf[:, :, 0:2], in_=ucarry[:, l, :, :]),
                                 reads=[("ucarry", l), R1U], writes=[("upre", s)])

                    def conv_chunk(j):
                        ucells = [("u", s, j), ("upre", s)]
                        ccells = [("C", s, j)]
                        S.op("dve", lambda e: e.tensor_scalar(out=o.Cf[:, j, 0:TP], in0=o.uf[:, j, 0:TP], scalar1=cw_(l, 0, j),
                                                              scalar2=None, op0=ALU.mult),
                             reads=ucells + [("gcol",), R1U], writes=ccells)
                        for i in (1, 2):
                            S.op("dve", lambda e, i=i: e.scalar_tensor_tensor(
                                out=o.Cf[:, j, 0:TP], in0=o.uf[:, j, i:i + TP], scalar=cw_(l, i, j), in1=o.Cf[:, j, 0:TP],
                                op0=ALU.mult, op1=ALU.add),
                                reads=ucells + ccells + [("gcol",), R1U], writes=ccells)
                        if has_sample:
                            S.op("dve", lambda e: e.tensor_copy(
                                out=us[:, j, :, 2:10], in_=o.uf[:, j, 2 + scol:2 + scol + 128].rearrange("p (s t) -> p s t", s=16)),
                                reads=ucells + [R1U], writes=[("us",)])
                            zs = o.Cf[:, j, scol:scol + 128].rearrange("p (s t) -> p s t", s=16)
                            S.op("dve", lambda e: e.tensor_scalar(out=zs, in0=us[:, j, :, 0:8], scalar1=cw_(l, 0, j),
                                                                  scalar2=None, op0=ALU.mult),
                                 reads=[("us",), ("gcol",), R1U], writes=ccells)
                            for i in (1, 2):
                                S.op("dve", lambda e, i=i: e.scalar_tensor_tensor(
                                    out=zs, in0=us[:, j, :, i:i + 8], scalar=cw_(l, i, j), in1=zs, op0=ALU.mult, op1=ALU.add),
                                    reads=[("us",), ("gcol",), R1U] + ccells, writes=ccells)
                            S.op("dve", lambda e: e.tensor_copy(out=ncstg[:, j, :].rearrange("p (s r) -> p s r", s=16),
                                                                in_=us[:, j, :, 8:10]),
                                 reads=[("us",)], writes=[("ncstg", j)])

                    def conv_publish():
                        allu = [("u", s, j) for j in range(4)]
                        if not last_block:
                            S.op("dve", lambda e: e.tensor_copy(out=ucarry[:, l, :, :], in_=o.uf[:, :, TP:TP + 2]),
                                 reads=allu + [("upre", s), R1U], writes=[("ucarry", l)])
                        marks.add(("u", (pi, s), l))
                        if has_sample:
                            bank = PB_ST[0]
                            for j in range(4):
                                S.op("pe", lambda e, j=j: e.transpose(out=psb[bank][0:2, j * 128:(j + 1) * 128],
                                                                      in_=o.uf[:, j, TP:TP + 2], identity=identf[:]),
                                     reads=[("u", s, j), ("upre", s), ("identf",), R1U], writes=[PS(bank)], signal=(j == 3))
                            S.op("act", lambda e: e.copy(out=ostg[0:2, :], in_=psb[bank][0:2, :]), reads=[PS(bank)], writes=[("ostg",)])
                            S.dma("sp", ncp[l], ostg[0:2, :], reads=[("ostg",)])
                            bank = PB_ST[1]
                            for j in range(4):
                                S.op("pe", lambda e, j=j: e.transpose(out=psb[bank][0:32, j * 128:(j + 1) * 128],
                                                                      in_=ncstg[:, j, :], identity=identf[:]),
                                     reads=[("ncstg", j), ("identf",)], writes=[PS(bank)], signal=(j == 3))
                            S.op("act", lambda e: e.copy(out=ostg[0:32, :], in_=psb[bank][0:32, :]), reads=[PS(bank)], writes=[("ostg",)])
                            S.dma("sp", ncs[l], ostg[0:32, :], reads=[("ostg",)])

                    conv_seq = [("C", j) for j in range(4)] + [("prefix", 0)]
                    for j in range(4):
                        conv_seq += [("u", j), ("conv", j)]
                    conv_seq += [("publish", 0)] + [("B", j) for j in range(4)]
                    ai = 0
                    ji = 0
                    while ai < len(att_stages) or ji < len(conv_seq):
                        if ai < len(att_stages):
                            kind_, ti_ = att_stages[ai]
                            if kind_ in ("S0", "S1"):
                                yield item(2500)
                                att_S(ti_, 0 if kind_ == "S0" else 1)
                            else:
                                yield item(4500)
                                att_O(ti_)
                            ai += 1
                        if ji < len(conv_seq):
                            which, j = conv_seq[ji]
                            if which == "prefix":
                                if prev_blk is None:
                                    yield item(0)
                                else:
                                    yield item(0, ("mark", ("u", prev_blk, l)))
                                conv_prefix()
                            elif which == "conv":
                                yield item(1500)
                                conv_chunk(j)
                            elif which == "publish":
                                yield item(500)
                                conv_publish()
                            else:
                                yield item(JOB, ("gran", gi_(l, which)))
                                conv_job(which, j)
                            ji += 1

                    for m in range(2):
                        g_x = [gi_(l, "gx%d_0" % m), gi_(l, "gx%d_1" % m)]
                        g_oc = gi_(l, "aoco%d" % m)
                        s_oc = g_oc % NSLOT
                        woc = wslots[:, s_oc, 0:KD * 512].rearrange("p (k n) -> p k n", k=KD)
                        for jj in range(4):
                            j = m * 4 + jj
                            half, jl = jj // 2, jj % 2
                            g_gx = g_x[half]
                            s_gx = g_gx % NSLOT
                            wgx = wslots[:, s_gx, 0:KD * 512].rearrange("p (k n) -> p k n", k=KD)
                            gp = j % 2
                            A = gp * 2
                            Cc = gp * 2 + 1
                            yield item(JOB, ("gran", g_gx))
                            ps, pcell = proj(KD, n, lambda k: wgx[:, k, jl * 128:(jl + 1) * 128], lambda k: hT[:, k, 0:n],
                                             lambda k: [Hc(k), HRU], ("w", s_gx))
                            S.op("act", lambda e, ps=ps: e.activation(out=o.gt[:, A, 0:n], in_=ps, func=AF.Sigmoid),
                                 reads=[pcell, R1U], writes=[("gt", s, A)])
                            yield item(4 * BMAX, ("gran", g_oc))
                            ps, pcell = proj(4, n, lambda k: woc[:, k, jj * 128:(jj + 1) * 128], lambda k: o.aT[:, k, 0:n],
                                             lambda k: [("a", s), R1U], ("w", s_oc))
                            S.op("dve", lambda e, ps=ps: e.tensor_tensor(out=o.gt[:, A, 0:n], in0=ps, in1=o.gt[:, A, 0:n], op=ALU.mult),
                                 reads=[pcell, ("gt", s, A), R1U], writes=[("gt", s, A)])
                            yield item(JOB)
                            ps, pcell = proj(KD, n, lambda k: wgx[:, k, 256 + jl * 128:256 + (jl + 1) * 128], lambda k: hT[:, k, 0:n],
                                             lambda k: [Hc(k), HRU], ("w", s_gx))
                            S.op("act", lambda e, ps=ps: e.activation(out=o.gt[:, Cc, 0:n], in_=ps, func=AF.Sigmoid),
                                 reads=[pcell, R1U], writes=[("gt", s, Cc)])
                            yield item(4 * BMAX)
                            ps, pcell = proj(4, n, lambda k: woc[:, 4 + k, jj * 128:(jj + 1) * 128], lambda k: o.BzT[:, k, 0:n],
                                             lambda k: [("bz", s, k), R1U], ("w", s_oc))
                            S.op("dve", lambda e, ps=ps: e.tensor_tensor(out=o.gt[:, Cc, 0:n], in0=ps, in1=o.gt[:, Cc, 0:n], op=ALU.mult),
                                 reads=[pcell, ("gt", s, Cc), R1U], writes=[("gt", s, Cc)])
                            S.op("dve", lambda e, j=j: e.tensor_tensor(out=o.mT[:, j, 0:n], in0=o.gt[:, A, 0:n],
                                                                       in1=o.gt[:, Cc, 0:n], op=ALU.add),
                                 reads=[("gt", s, A), ("gt", s, Cc), R1U], writes=[("m", s, j)])
                            if jl == 1:
                                g_done(g_gx)
                        g_done(g_oc)

                    yield item(0)
                    fence(HRU)
                    for it in proj_to_R2(l, ["wout0", "wout1"], KD, 4,
                                         lambda sl_, i, k: wslots[:, sl_, 0:KD * 512].rearrange("p (k n) -> p k n", k=KD)[:, k, (i % 4) * 128:(i % 4 + 1) * 128],
                                         lambda k: o.mT[:, k, 0:n], lambda k: ("m", s, k), R1U):
                        yield it
                    for it in post_norm(1, l):
                        yield it

                    yield item(0)
                    fence(R1U)
                    for it in make_xg(l):
                        yield it
                    for m in range(8):
                        gidx = gi_(l, "wup%d" % m)
                        sl_ = gidx % NSLOT
                        wu_ = wslots[:, sl_, 0:KD * 512].rearrange("p (k n) -> p k n", k=KD)
                        for jj in range(4):
                            f = m * 4 + jj
                            yield item(JOB, ("gran", gidx))
                            ps, pcell = proj(KD, n, lambda k: wu_[:, k, jj * 128:(jj + 1) * 128], lambda k: hT[:, k, 0:n],
                                             lambda k: [Hc(k), HRU], ("w", sl_))
                            rb = f % 2
                            S.op("act", lambda e, ps=ps, rb=rb: e.activation(out=o.rtmp[:, rb, 0:n], in_=ps, func=AF.Relu),
                                 reads=[pcell, R1U], writes=[("rtmp", s, rb)])
                            S.op("dve", lambda e, f=f, rb=rb: e.tensor_tensor(out=o.actT[:, f, 0:n], in0=o.rtmp[:, rb, 0:n],
                                                                              in1=o.rtmp[:, rb, 0:n], op=ALU.mult),
                                 reads=[("rtmp", s, rb), R1U], writes=[("act", s, f)])
                        g_done(gidx)
                    yield item(0)
                    fence(HRU)
                    for it in proj_to_R2(l, ["wdn%d" % i for i in range(8)], 32, 1,
                                         lambda sl_, i, k: wslots[:, sl_, 0:32 * 128].rearrange("p (k n) -> p k n", k=32)[:, k, :],
                                         lambda k: o.actT[:, k, 0:n], lambda k: ("act", s, k), R1U, fin=stat_fin_mlp):
                        yield it
                    if l == 1:
                        yield item(0)
                        fence(R1U)
                        if pi + 1 < len(PARTS):
                            issue_xin(pi + 1)
                    for it in post_norm(3, l):
                        yield it

                for ti, (kind, g) in enumerate(tiles):
                    yield item(2048)
                    for half in range(2):
                        bank = (PB_ST + PB_S)[(ti * 2 + half) % 4]
                        for kk in range(4):
                            k = half * 4 + kk
                            S.op("pe", lambda e, k=k, kk=kk, bank=bank: e.transpose(
                                out=psb[bank][:, kk * 128:(kk + 1) * 128], in_=xT[:, k, ti * 128:(ti + 1) * 128],
                                identity=identf[:]),
                                reads=[Xc(k), ("identf",)], writes=[PS(bank)], signal=(kk == 3))
                        S.op("act", lambda e, half=half, bank=bank: e.copy(
                            out=o.yout[:, ti, half * 512:(half + 1) * 512], in_=psb[bank][:, :]),
                            reads=[PS(bank), R1U], writes=[("yout", s, ti)])
                    dst = yp[g * 128:(g + 1) * 128, :] if kind == "p" else ys[:, :]
                    S.dma("sp", dst, o.yout[:, ti, :], reads=[("yout", s, ti), R1U])

        gens = [stream(0), stream(1)]
        clocks = [0, SKEW]
        prev = [next(g) for g in gens]
        while any(alive):
            order = sorted([s for s in (0, 1) if alive[s]], key=lambda s: (clocks[s], s))
            for s in order:
                cost, needs = prev[s]
                if all(check_need(nd, s) for nd in needs):
                    clocks[s] += cost
                    for nd in needs:
                        if nd[0] == "gran":
                            cur_gran[s] = max(cur_gran[s], nd[1])
                    try:
                        prev[s] = next(gens[s])
                    except StopIteration:
                        alive[s] = False
                    break
            else:
                raise RuntimeError("both streams blocked: %r" % (prev,))

        assert G["issued"] == len(gran_seq) and all(d == 2 for d in G["done"])
        S.drain("sp")
        build_program.stats = dict(ninst=S.ninst, nwaits=S.nwaits, cnt=dict(S.ccnt), sbuf_left=nc.sbuf_bytes_remaining)
    return nc


_CACHE = {}


def kernel(**inputs):
    f32 = lambda a: np.ascontiguousarray(np.asarray(a, dtype=np.float32))
    x_prompt = f32(inputs["x_prompt"])
    x_sample = f32(inputs["x_sample"])
    cache_k = f32(inputs["cache_k"])
    cache_v = f32(inputs["cache_v"])
    state_conv = f32(inputs["state_conv"])
    shared = {n: f32(inputs[n]) for n in ("g_mix_pre", "g_mix_post", "g_mlp_pre", "g_mlp_post", "w_in", "attn_sinks",
                                          "conv_w", "w_attn_o", "w_conv_o", "w_out", "w_up", "w_down")}
    if "nc" not in _CACHE:
        _CACHE["nc"] = build_program()
    nc = _CACHE["nc"]
    in_maps = []
    for c in range(NCORES):
        s0, s1 = 16 * c, 16 * (c + 1)
        m = dict(shared)
        m["xp"] = x_prompt[c]
        m["xs"] = np.ascontiguousarray(x_sample[s0:s1].reshape(128, D))
        m["ck"] = np.ascontiguousarray(cache_k[:, s0:s1].reshape(2, 16, 128, 128))
        m["cv"] = np.ascontiguousarray(cache_v[:, s0:s1].reshape(2, 16, 128, 128))
        m["sc"] = np.ascontiguousarray(state_conv[:, s0:s1].reshape(2, 32, 512))
        in_maps.append(m)
    res = run_bass_kernel_spmd(nc, in_maps, core_ids=list(range(NCORES)))
    R = res.results
    y_prompt = np.stack([R[c]["yp"] for c in range(NCORES)], axis=0).astype(np.float32)
    y_sample = np.concatenate([R[c]["ys"].reshape(16, 8, D) for c in range(NCORES)], axis=0).astype(np.float32)
    nk_p = np.stack([R[c]["nkp"].reshape(2, 128, 2, 64) for c in range(NCORES)], axis=1).astype(np.float32)
    nv_p = np.stack([R[c]["nvp"].reshape(2, 128, 2, 64) for c in range(NCORES)], axis=1).astype(np.float32)
    nc_p = np.stack([R[c]["ncp"] for c in range(NCORES)], axis=1).astype(np.float32)
    nk_s = np.concatenate([R[c]["nks"].reshape(2, 16, 128, 2, 64) for c in range(NCORES)], axis=1).astype(np.float32)
    nv_s = np.concatenate([R[c]["nvs"].reshape(2, 16, 128, 2, 64) for c in range(NCORES)], axis=1).astype(np.float32)
    nc_s = np.concatenate([R[c]["ncs"].reshape(2, 16, 2, 512) for c in range(NCORES)], axis=1).astype(np.float32)
    return (y_prompt, y_sample, nk_p, nv_p, nc_p, nk_s, nv_s, nc_s)
```

```python
import numpy as np
from contextlib import ExitStack

import concourse.bass as bass
import concourse.mybir as mybir
from concourse.bass_utils import run_bass_kernel_spmd

F32 = mybir.dt.float32
BF16 = mybir.dt.bfloat16
AF = mybir.ActivationFunctionType
ALU = mybir.AluOpType

NCORES = 8
D = 1024
KD = 8
SEQ = 2048
IN_DIM = 4352
DFF = 4096
EPS = 1e-6
NSLOT = 5
SLOT_ELEMS = 4096
SAME_ENGINE_SYNC = True

PARTS = [
    [[("p", 0), ("p", 1), ("p", 2)], [("p", 3), ("p", 4), ("p", 5)]],
    [[("p", 6), ("p", 7), ("p", 8)], [("p", 9), ("p", 10), ("p", 11)]],
    [[("p", 12), ("p", 13), ("p", 14)], [("p", 15), ("s", 0)]],
]
BMAX = 384
SKEW = 26000


class Tok:
    __slots__ = ("sem", "val", "know", "eng")

    def __init__(self, sem, val, know, eng):
        self.sem = sem
        self.val = val
        self.know = know
        self.eng = eng


class Sched:
    def __init__(self, nc, es, ring=8):
        self.nc = nc
        self.engs = {"pe": nc.tensor, "act": nc.scalar, "dve": nc.vector, "pool": nc.gpsimd, "sp": nc.sync}
        self.csem = {}
        self.ccnt = {}
        for e in ("pe", "act", "dve", "pool"):
            self.csem[e] = es.enter_context(nc.semaphore("c_" + e))
            self.ccnt[e] = 0
        self.know = {e: {} for e in self.engs}
        self.dring = {}
        self.dpos = {}
        self.dlast = {}
        for q in ("sp", "pool"):
            self.dring[q] = [es.enter_context(nc.semaphore("d_%s%d" % (q, i))) for i in range(ring)]
            self.dpos[q] = 0
        self.lastw = {}
        self.readers = {}
        self.pending = []
        self.all_dma_toks = []
        self.nwaits = 0
        self.ninst = 0

    def _wait(self, eng, tok):
        if eng == "pe" and tok.eng == "pe":
            return
        if tok.val is None:
            raise RuntimeError("dependency on an unsignaled PE op")
        k = self.know[eng]
        sid = id(tok.sem)
        if k.get(sid, 0) >= tok.val:
            return
        own = tok.eng == eng and tok.sem is self.csem.get(eng)
        if own and (eng == "pe" or not SAME_ENGINE_SYNC):
            k[sid] = tok.val
            return
        self.engs[eng].wait_ge(tok.sem, tok.val)
        self.nwaits += 1
        for s, v in tok.know.items():
            if k.get(s, 0) < v:
                k[s] = v
        k[sid] = tok.val

    def _deps(self, reads, writes):
        deps = []
        for c in reads:
            t = self.lastw.get(c)
            if t is not None:
                deps.append(t)
        for c in writes:
            t = self.lastw.get(c)
            if t is not None:
                deps.append(t)
            r = self.readers.get(c)
            if r:
                deps.extend(r.values())
        return deps

    def _record(self, tok, reads, writes):
        for c in writes:
            self.lastw[c] = tok
            self.readers[c] = {}
        for c in reads:
            self.readers.setdefault(c, {})[id(tok.sem)] = tok

    def op(self, eng, fn, reads=(), writes=(), signal=True):
        ps_r = [c for c in reads if c[0] == "ps"]
        if ps_r:
            reads = [c for c in reads if c[0] != "ps"]
            writes = list(writes) + ps_r
        for t in self._deps(reads, writes):
            self._wait(eng, t)
        inst = fn(self.engs[eng])
        self.ninst += 1
        if signal:
            inst.then_inc(self.csem[eng], 1)
            self.ccnt[eng] += 1
            tok = Tok(self.csem[eng], self.ccnt[eng], dict(self.know[eng]), eng)
            if eng == "pe" and self.pending:
                for p in self.pending:
                    p.val = tok.val
                    p.know = tok.know
                self.pending = []
        else:
            assert eng == "pe"
            tok = Tok(self.csem[eng], None, None, eng)
            self.pending.append(tok)
        self._record(tok, reads, writes)
        return tok

    def dma(self, q, out, in_, reads=(), writes=()):
        ring = self.dring[q]
        sem = ring[self.dpos[q] % len(ring)]
        self.dpos[q] += 1
        prev = self.dlast.get(id(sem))
        for t in self._deps(reads, writes):
            self._wait(q, t)
        if prev is not None:
            self._wait(q, prev)
        val = (prev.val if prev is not None else 0) + 16
        self.engs[q].dma_start(out=out, in_=in_).then_inc(sem, 16)
        self.ninst += 1
        tok = Tok(sem, val, dict(self.know[q]), q)
        self.dlast[id(sem)] = tok
        self._record(tok, reads, writes)
        return tok

    def drain(self, q):
        for t in list(self.dlast.values()):
            self._wait(q, t)


def build_program():
    nc = bass.Bass("TRN2", target_bir_lowering=False)

    def din(name, shape):
        return nc.dram_tensor(name, shape, F32, kind="ExternalInput").ap()

    def dout(name, shape):
        return nc.dram_tensor(name, shape, F32, kind="ExternalOutput").ap()

    xp = din("xp", [SEQ, D])
    xs = din("xs", [128, D])
    ck = din("ck", [2, 16, 128, 128])
    cv = din("cv", [2, 16, 128, 128])
    sc = din("sc", [2, 32, 512])
    gvec = [din(n, [2, D]) for n in ("g_mix_pre", "g_mix_post", "g_mlp_pre", "g_mlp_post")]
    w_in = din("w_in", [2, D, IN_DIM])
    sinks = din("attn_sinks", [2, 8])
    conv_w = din("conv_w", [2, 3, 512])
    w_ao = din("w_attn_o", [2, 512, D])
    w_co = din("w_conv_o", [2, 512, D])
    w_out = din("w_out", [2, D, D])
    w_up = din("w_up", [2, D, DFF])
    w_down = din("w_down", [2, DFF, D])

    yp = dout("yp", [SEQ, D])
    ys = dout("ys", [128, D])
    nkp = dout("nkp", [2, 128, 128])
    nvp = dout("nvp", [2, 128, 128])
    ncp = dout("ncp", [2, 2, 512])
    nks = dout("nks", [2, 16, 128, 128])
    nvs = dout("nvs", [2, 16, 128, 128])
    ncs = dout("ncs", [2, 32, 512])

    es = ExitStack()
    with es:
        S = Sched(nc, es)

        def sb(name, shape, dt):
            return es.enter_context(nc.sbuf_tensor(name, shape, dt))

        NB = BMAX
        R1BYTES = 35904

        class SB:
            pass
        B = []
        for s in range(2):
            o = SB()
            o.xT = sb("xT%d" % s, [128, KD, NB], F32)
            HR = sb("HR%d" % s, [128, KD * NB * 2], BF16)
            o.hT = HR[:, 0:KD * NB].rearrange("p (k t) -> p k t", k=KD)
            o.R2 = HR[:, :].bitcast(F32).rearrange("p (k t) -> p k t", k=KD)
            R1 = sb("R1_%d" % s, [128, R1BYTES // 2], BF16)

            def carve(off_bytes, nelem, dt, R1=R1):
                if dt == BF16:
                    return R1[:, off_bytes // 2: off_bytes // 2 + nelem]
                return R1[:, off_bytes // 2: off_bytes // 2 + nelem * 2].bitcast(F32)
            o.qT = carve(0, 4 * NB, BF16).rearrange("p (c t) -> p c t", c=4)
            o.aT = carve(3072, 4 * NB, BF16).rearrange("p (c t) -> p c t", c=4)
            o.kT = carve(6144, 128 + NB, BF16)
            o.Vt = carve(7168, 4 * 128, BF16).rearrange("p (s d) -> p s d", s=4)
            o.Cf = carve(8192, 4 * NB, F32).rearrange("p (c t) -> p c t", c=4)
            o.Kc = carve(8192, 16 * 128, BF16).rearrange("p (s d) -> p s d", s=16)
            o.uf = carve(14336, 4 * (NB + 2), F32).rearrange("p (c t) -> p c t", c=4)
            o.BzT = carve(20544, 4 * NB, BF16).rearrange("p (c t) -> p c t", c=4)
            o.mT = carve(23616, KD * NB, BF16).rearrange("p (c t) -> p c t", c=KD)
            o.gt = carve(29760, 4 * NB, F32).rearrange("p (a t) -> p a t", a=4)
            o.actT = carve(0, 32 * NB, BF16).rearrange("p (c t) -> p c t", c=32)
            o.rtmp = carve(24576, 2 * NB, F32).rearrange("p (a t) -> p a t", a=2)
            o.xin = carve(0, 3 * 1024, F32).rearrange("p (a t) -> p a t", a=3)
            o.yout = carve(12288, 3 * 1024, F32).rearrange("p (a t) -> p a t", a=3)
            o.Pt = sb("Pt%d" % s, [128, 2, 2, 4, 128], BF16)
            o.sq = sb("sq%d" % s, [128, KD, NB], BF16)
            o.rt = sb("rt%d" % s, [128, NB], F32)
            o.rr = sb("rr%d" % s, [128, NB], F32)
            o.dent = sb("dent%d" % s, [128, 512], F32)
            o.R1U = ("R1use", s)
            o.HRU = ("HRuse", s)
            o.par = 0
            o.sqi = 0
            B.append(o)

        wslots = sb("wslots", [128, NSLOT, SLOT_ELEMS], BF16)
        KcT = sb("KcT", [128, 16, 128], BF16)
        Vc = sb("Vc", [128, 16, 128], BF16)
        identf = sb("identf", [128, 128], F32)
        identb = sb("identb", [128, 128], BF16)
        onesb = sb("onesb", [128, 128], BF16)
        M_prompt = sb("M_prompt", [128, 2, 128], BF16)
        M_first = sb("M_first", [128, 2, 128], BF16)
        M_sample = sb("M_sample", [128, 2, 128], BF16)
        gcol = sb("gcol", [128, 128], F32)
        esk = sb("esk", [128, 16], F32)
        sinkcol = sb("sinkcol", [128, 2, 4], F32)
        kcarry = sb("kcarry", [128, 2, 128], BF16)
        vcarry = sb("vcarry", [128, 2, 128], BF16)
        ucarry = sb("ucarry", [128, 2, 4, 2], F32)
        us = sb("us", [128, 4, 16, 10], F32)
        ncstg = sb("ncstg", [128, 4, 32], F32)
        ostg = sb("ostg", [128, 512], F32)
        fence_scr = sb("fence_scr", [128, 4], F32)
        mtmp = B[1].gt[:, :, 0:128]
        gstage = B[1].gt[:, 0, 128:256]

        psb = [es.enter_context(nc.psum_tensor("psb%d" % i, [128, 512], F32)) for i in range(8)]
        PB_ALL = [0, 1, 2, 3]
        projctr = [0]
        PB_S = [4, 5]
        PB_O = 6
        PB_D = 7
        PB_ST = [6, 7]

        def PS(b):
            return ("ps", b)

        def fence(cell):
            S.op("dve", lambda e: e.memset(fence_scr[:, 0:1], 0.0), reads=[], writes=[cell, ("fscr",)])

        def granules(l):
            g = []


            def cols(c0, n_):
                def f(slot):
                    src = w_in[l, :, c0:c0 + n_].rearrange("(k p) n -> p k n", p=128)
                    dst = slot[:, 0:KD * n_].rearrange("p (k n) -> p k n", k=KD)
                    return [(dst, src)]
                return f
            g.append(("q", cols(0, 512)))
            g.append(("kv", cols(512, 256)))
            g.append(("C", cols(1280, 512)))
            g.append(("u", cols(1792, 512)))
            g.append(("B", cols(768, 512)))
            for m in range(2):
                def gx(slot, m=m, half=0):
                    dst = slot[:, 0:KD * 512].rearrange("p (k n) -> p k n", k=KD)
                    c_a = 2304 + m * 512 + half * 256
                    c_c = 3328 + m * 512 + half * 256
                    return [(dst[:, :, 0:256], w_in[l, :, c_a:c_a + 256].rearrange("(k p) n -> p k n", p=128)),
                            (dst[:, :, 256:512], w_in[l, :, c_c:c_c + 256].rearrange("(k p) n -> p k n", p=128))]
                g.append(("gx%d_0" % m, lambda slot, m=m: gx(slot, m, 0)))

                def wo(slot, m=m):
                    dst = slot[:, 0:KD * 512].rearrange("p (k n) -> p k n", k=KD)
                    r = []
                    r.append((dst[0:64, 0:4, :], w_ao[l, 0:256, m * 512:(m + 1) * 512].rearrange("(c p) n -> p c n", p=64)))
                    r.append((dst[64:128, 0:4, :], w_ao[l, 256:512, m * 512:(m + 1) * 512].rearrange("(c p) n -> p c n", p=64)))
                    r.append((dst[:, 4:8, :], w_co[l, :, m * 512:(m + 1) * 512].rearrange("(k p) n -> p k n", p=128)))
                    return r
                g.append(("aoco%d" % m, wo))
                g.append(("gx%d_1" % m, lambda slot, m=m: gx(slot, m, 1)))
            for m in range(2):
                def wout(slot, m=m):
                    return [(slot[:, 0:KD * 512].rearrange("p (k n) -> p k n", k=KD),
                             w_out[l, :, m * 512:(m + 1) * 512].rearrange("(k p) n -> p k n", p=128))]
                g.append(("wout%d" % m, wout))
            for m in range(8):
                def wup(slot, m=m):
                    return [(slot[:, 0:KD * 512].rearrange("p (k n) -> p k n", k=KD),
                             w_up[l, :, m * 512:(m + 1) * 512].rearrange("(k p) n -> p k n", p=128))]
                g.append(("wup%d" % m, wup))
            for i in range(8):
                def wdn(slot, i=i):
                    return [(slot[:, 0:32 * 128].rearrange("p (k n) -> p k n", k=32),
                             w_down[l, :, i * 128:(i + 1) * 128].rearrange("(k p) n -> p k n", p=128))]
                g.append(("wdn%d" % i, wdn))
            return g

        gran_seq = []
        gran_idx = {}
        for pi in range(len(PARTS)):
            for l in range(2):
                for name, f in granules(l):
                    gran_idx[(pi, l, name)] = len(gran_seq)
                    gran_seq.append(f)
        G = dict(issued=0, done=[0] * len(gran_seq))

        def g_issuable(i):
            j = i - NSLOT
            return j < 0 or G["done"][j] >= 2

        def g_prefetch():
            while G["issued"] < len(gran_seq) and g_issuable(G["issued"]):
                i = G["issued"]
                slot_i = i % NSLOT
                for dst, src in gran_seq[i](wslots[:, slot_i, :]):
                    S.dma("pool", dst, src, writes=[("w", slot_i)])
                G["issued"] += 1

        def g_available(i):
            return i < G["issued"]

        def g_done(i):
            G["done"][i] += 1
            g_prefetch()


        SETUP = [B[1].R1U]
        for tj_, (kind_j, g_j) in enumerate(PARTS[0][0]):
            S.dma("sp", B[0].xin[:, tj_, :], xp[g_j * 128:(g_j + 1) * 128, :], reads=[B[0].R1U], writes=[("xin", 0, tj_)])
        S.op("pool", lambda e: e.memset(identf[:], 1.0), writes=[("identf",)])
        S.op("pool", lambda e: e.affine_select(out=identf[:], in_=identf[:], pattern=[[-1, 128]],
                                                compare_op=ALU.is_equal, fill=0.0, base=0, channel_multiplier=1),
             reads=[("identf",)], writes=[("identf",)])
        S.op("dve", lambda e: e.memset(gstage, 0.0), reads=SETUP, writes=[("gstage", i) for i in range(6)])
        for v in range(4):
            S.dma("sp", gstage[v * 16:(v + 1) * 16, :], gvec[v].rearrange("l (k p) -> (l k) p", p=128),
                  reads=SETUP, writes=[("gstage", v)])
        S.dma("sp", gstage[64:88, :], conv_w.rearrange("l i (j p) -> (l i j) p", p=128), reads=SETUP, writes=[("gstage", 4)])
        S.dma("sp", esk[:], sinks.rearrange("l h -> (l h)").partition_broadcast(128), writes=[("esk",)])
        g_prefetch()
        S.op("dve", lambda e: e.tensor_copy(out=identb[:], in_=identf[:]), reads=[("identf",)], writes=[("identb",)])
        S.op("dve", lambda e: e.memset(onesb[:], 1.0), writes=[("onesb",)])
        S.op("pe", lambda e: e.transpose(out=psb[6][:, 0:128], in_=gstage, identity=identf[:]),
             reads=SETUP + [("gstage", i) for i in range(6)] + [("identf",)], writes=[PS(6)])
        S.op("act", lambda e: e.copy(out=gcol[:], in_=psb[6][:, 0:128]), reads=[PS(6)], writes=[("gcol",)])

        def gc_(vec, l, k):
            i = vec * 16 + l * 8 + k
            return gcol[:, i:i + 1]

        def cw_(l, i, j):
            n_ = 64 + l * 12 + i * 4 + j
            return gcol[:, n_:n_ + 1]

        S.op("act", lambda e: e.activation(out=esk[:], in_=esk[:], func=AF.Exp), reads=[("esk",)], writes=[("esk",)])
        for l in range(2):
            S.op("dve", lambda e, l=l: e.tensor_copy(out=sinkcol[0:64, l, :], in_=esk[0:64, l * 8:l * 8 + 4]),
                 reads=[("esk",)], writes=[("sinkcol",)])
            S.op("dve", lambda e, l=l: e.tensor_copy(out=sinkcol[64:128, l, :], in_=esk[64:128, l * 8 + 4:l * 8 + 8]),
                 reads=[("esk",)], writes=[("sinkcol",)])
        S.op("pool", lambda e: e.memset(mtmp, 1.0), reads=SETUP, writes=[("mtmp",)])
        S.op("pool", lambda e: e.affine_select(out=mtmp[:, 0, :], in_=mtmp[:, 0, :], pattern=[[1, 128]],
                                                compare_op=ALU.is_ge, fill=0.0, base=0, channel_multiplier=-1),
             reads=SETUP + [("mtmp",)], writes=[("mtmp",)])
        S.op("pool", lambda e: e.affine_select(out=mtmp[:, 1, :], in_=mtmp[:, 1, :], pattern=[[-1, 128]],
                                                compare_op=ALU.is_gt, fill=0.0, base=0, channel_multiplier=1),
             reads=SETUP + [("mtmp",)], writes=[("mtmp",)])
        m2v = mtmp[:, 2, :].rearrange("p (s t) -> p s t", s=16)
        S.op("pool", lambda e: e.affine_select(out=m2v, in_=m2v, pattern=[[8, 16], [1, 8]],
                                                compare_op=ALU.is_ge, fill=0.0, base=0, channel_multiplier=-1),
             reads=SETUP + [("mtmp",)], writes=[("mtmp",)])
        S.op("pool", lambda e: e.affine_select(out=m2v, in_=m2v, pattern=[[-8, 16], [0, 8]],
                                                compare_op=ALU.is_ge, fill=0.0, base=0, channel_multiplier=1),
             reads=SETUP + [("mtmp",)], writes=[("mtmp",)])
        m3v = mtmp[:, 3, :].rearrange("p (s t) -> p s t", s=16)
        S.op("pool", lambda e: e.affine_select(out=m3v, in_=m3v, pattern=[[0, 16], [-1, 8]],
                                                compare_op=ALU.is_gt, fill=0.0, base=0, channel_multiplier=1),
             reads=SETUP + [("mtmp",)], writes=[("mtmp",)])
        S.op("dve", lambda e: e.memset(M_first[:, 0, :], 0.0), writes=[("masks",)])
        for dst, srci in ((M_prompt[:, 0, :], 1), (M_prompt[:, 1, :], 0), (M_first[:, 1, :], 0),
                          (M_sample[:, 0, :], 3), (M_sample[:, 1, :], 2)):
            S.op("dve", lambda e, dst=dst, srci=srci: e.tensor_copy(out=dst, in_=mtmp[:, srci, :]),
                 reads=SETUP + [("mtmp",)], writes=[("masks",)])
        for l in range(2):
            S.dma("sp", nks[l, :, 0:120, :], ck[l, :, 8:128, :])
            S.dma("sp", nvs[l, :, 0:120, :], cv[l, :, 8:128, :])

        marks = set()

        cur_gran = [-1, -1]
        alive = [True, True]
        MAXLEAD = 2

        def check_need(nd, s_):
            if nd[0] == "gran":
                if alive[1 - s_] and nd[1] - cur_gran[1 - s_] > MAXLEAD:
                    return False
                return g_available(nd[1])
            if nd[0] == "mark":
                return nd[1] in marks
            raise AssertionError(nd)

        def item(cost, *needs):
            return (cost, needs)

        def stream(s):
            o = B[s]
            xT, hT, R2 = o.xT, o.hT, o.R2
            R1U, HRU = o.R1U, o.HRU

            def Xc(k):
                return ("x", s, k)

            def Hc(k):
                return ("h", s, k)

            def R2c(k):
                return ("r2", s, k)

            def proj(K, n, lhs_fn, rhs_fn, reads_fn, wcell, after_mm=None):
                bank = PB_ALL[projctr[0] % 4]
                projctr[0] += 1
                for k in range(K):
                    S.op("pe", lambda e, k=k: e.matmul(psb[bank][:, 0:n], lhsT=lhs_fn(k), rhs=rhs_fn(k),
                                                       start=(k == 0), stop=(k == K - 1)),
                         reads=[wcell] + reads_fn(k), writes=[PS(bank)], signal=(k == K - 1))
                return psb[bank][:, 0:n], PS(bank)

            def stat_sq(src_ap, n, reads, i):
                S.op("act", lambda e: e.activation(out=o.sq[:, i, 0:n], in_=src_ap, func=AF.Square),
                     reads=reads, writes=[("sq", s, i)])

            def stat_mms(n):
                bank = PB_ST[s]
                for k in range(KD):
                    S.op("pe", lambda e, k=k: e.matmul(psb[bank][:, 0:n], lhsT=onesb[:], rhs=o.sq[:, k, 0:n],
                                                       start=(k == 0), stop=(k == KD - 1)),
                         reads=[("sq", s, k), ("onesb",)], writes=[PS(bank)], signal=(k == KD - 1))

            def stat_fin(n):
                bank = PB_ST[s]
                S.op("act", lambda e: e.activation(out=o.rt[:, 0:n], in_=psb[bank][:, 0:n], func=AF.Ln,
                                                   bias=EPS, scale=1.0 / D),
                     reads=[PS(bank)], writes=[("rt", s)])
                S.op("act", lambda e: e.activation(out=o.rr[:, 0:n], in_=o.rt[:, 0:n], func=AF.Exp, scale=-0.5),
                     reads=[("rt", s)], writes=[("r", s)])

            def stat_fin_mlp(n):
                bank = PB_ST[s]
                S.op("dve", lambda e: e.scalar_tensor_tensor(out=o.rr[:, 0:n], in0=o.rt[:, 0:n], scalar=EPS * D,
                                                             in1=psb[bank][:, 0:n], op0=ALU.mult, op1=ALU.add),
                     reads=[("rt", s), PS(bank)], writes=[("r", s)])
                S.op("act", lambda e: e.activation(out=o.rr[:, 0:n], in_=o.rr[:, 0:n], func=AF.Ln, scale=1.0 / D),
                     reads=[("r", s)], writes=[("r", s)])
                S.op("act", lambda e: e.activation(out=o.rr[:, 0:n], in_=o.rr[:, 0:n], func=AF.Exp, scale=-0.5),
                     reads=[("r", s)], writes=[("r", s)])

            for pi in range(len(PARTS)):
                tiles = PARTS[pi][s]
                NT = len(tiles)
                n = NT * 128
                has_sample = tiles[-1][0] == "s"
                NTP = NT - 1 if has_sample else NT
                TP = NTP * 128
                scol = TP
                last_block = (pi == len(PARTS) - 1 and s == 1)
                prev_blk = (pi, 0) if s == 1 else ((pi - 1, 1) if pi > 0 else None)
                JOB = KD * BMAX

                def gi_(l, name):
                    return gran_idx[(pi, l, name)]

                def issue_xin(pj):
                    for tj, (kind_j, g_j) in enumerate(PARTS[pj][s]):
                        src = xp[g_j * 128:(g_j + 1) * 128, :] if kind_j == "p" else xs[:, :]
                        S.dma("sp", o.xin[:, tj, :], src, reads=[R1U], writes=[("xin", s, tj)])

                if pi == 0:
                    yield item(0)
                    fence(R1U)
                    if s == 1:
                        issue_xin(0)
                for ti, (kind, g) in enumerate(tiles):
                    yield item(2048)
                    for half in range(2):
                        bank = (PB_ST + PB_S)[(ti * 2 + half) % 4]
                        for kk in range(4):
                            k = half * 4 + kk
                            S.op("pe", lambda e, k=k, kk=kk, bank=bank: e.transpose(
                                out=psb[bank][:, kk * 128:(kk + 1) * 128], in_=o.xin[:, ti, k * 128:(k + 1) * 128],
                                identity=identf[:]),
                                reads=[("xin", s, ti), ("identf",), R1U], writes=[PS(bank)], signal=(kk == 3))
                        S.op("act", lambda e, half=half, bank=bank: e.copy(
                            out=xT[:, half * 4:(half + 1) * 4, ti * 128:(ti + 1) * 128],
                            in_=psb[bank][:, :].rearrange("p (k t) -> p k t", k=4)),
                            reads=[PS(bank)], writes=[Xc(half * 4 + kk) for kk in range(4)])

                def make_h(vec, l):
                    yield item(0)
                    fence(HRU)
                    yield item(8000)
                    for k in range(KD):
                        stat_sq(xT[:, k, 0:n], n, [Xc(k)], k)
                    yield item(3500)
                    stat_mms(n)
                    stat_fin(n)
                    for k in range(KD):
                        yield item(1000)
                        S.op("dve", lambda e, k=k: e.scalar_tensor_tensor(
                            out=hT[:, k, 0:n], in0=xT[:, k, 0:n], scalar=gc_(vec, l, k),
                            in1=o.rr[:, 0:n], op0=ALU.mult, op1=ALU.mult),
                            reads=[Xc(k), ("r", s), ("gcol",), HRU], writes=[Hc(k)])

                def make_xg(l):
                    for k in range(KD):
                        yield item(700)
                        S.op("act", lambda e, k=k: e.mul(hT[:, k, 0:n], xT[:, k, 0:n], gc_(2, l, k)),
                             reads=[Xc(k), ("gcol",)], writes=[Hc(k), R2c(k // 2)])
                    yield item(1500)
                    for k in range(KD):
                        stat_sq(xT[:, k, 0:n], n, [Xc(k)], k)

                def xstats_finish():
                    stat_mms(n)
                    bank_ = PB_ST[s]
                    S.op("act", lambda e: e.activation(out=o.rt[:, 0:n], in_=psb[bank_][:, 0:n], func=AF.Square,
                                                       bias=EPS, scale=1.0 / D),
                         reads=[PS(bank_)], writes=[("rt", s)])

                def post_norm(vec, l):
                    yield item(3000)
                    for k in range(KD):
                        yield item(2400)
                        S.op("dve", lambda e, k=k: e.scalar_tensor_tensor(
                            out=R2[:, k, 0:n], in0=R2[:, k, 0:n], scalar=gc_(vec, l, k),
                            in1=o.rr[:, 0:n], op0=ALU.mult, op1=ALU.mult),
                            reads=[R2c(k), ("r", s), ("gcol",), HRU], writes=[R2c(k)])
                        S.op("dve", lambda e, k=k: e.tensor_tensor(
                            out=xT[:, k, 0:n], in0=xT[:, k, 0:n], in1=R2[:, k, 0:n], op=ALU.add),
                            reads=[R2c(k), Xc(k), HRU], writes=[Xc(k)])

                def proj_to_R2(l, names, K, nper, lhs_of, rhs_of, rcell_of, rfence, fin=None):
                    for i in range(KD):
                        gname = names[i // nper]
                        gidx = gi_(l, gname)
                        first = (i % nper == 0)
                        yield item(K * BMAX, ("gran", gidx)) if first else item(K * BMAX)
                        sl_ = gidx % NSLOT
                        ps, pcell = proj(K, n, lambda k, i=i, sl_=sl_: lhs_of(sl_, i, k), rhs_of,
                                         lambda k: [rcell_of(k), rfence], ("w", sl_))
                        stat_sq(ps, n, [pcell], i)
                        S.op("act", lambda e, i=i, ps=ps: e.copy(out=R2[:, i, 0:n], in_=ps), reads=[pcell, HRU],
                             writes=[R2c(i)])
                        if i % nper == nper - 1:
                            g_done(gidx)
                    yield item(3500)
                    stat_mms(n)
                    (fin or stat_fin)(n)

                for l in range(2):
                    yield item(0)
                    fence(R1U)
                    for it in make_h(0, l):
                        yield it

                    if has_sample:
                        yield item(4000)
                        S.dma("pool", o.Kc, ck[l].rearrange("s k d -> k s d"), reads=[R1U], writes=[("Kc",), ("C", s)])
                        S.dma("pool", Vc[:], cv[l].rearrange("s k d -> k s d"), writes=[("Vc",)])
                        for grp in range(2):
                            bank = PB_ST[grp]
                            pbf = psb[bank][:, :].bitcast(BF16).rearrange("p (s k) -> p s k", s=8)
                            for s8 in range(8):
                                sq_ = grp * 8 + s8
                                S.op("pe", lambda e, sq_=sq_, s8=s8, pbf=pbf: e.transpose(out=pbf[:, s8, :], in_=o.Kc[:, sq_, :],
                                                                                         identity=identb[:]),
                                     reads=[("Kc",), ("identb",), R1U], writes=[PS(bank)], signal=(s8 == 7))
                            S.op("act", lambda e, grp=grp, pbf=pbf: e.copy(out=KcT[:, grp * 8:(grp + 1) * 8, :], in_=pbf),
                                 reads=[PS(bank)], writes=[("KcT",)])
                        S.dma("sp", ostg[0:32, :], sc[l], writes=[("ostg",)])
                        bank = PB_ST[0]
                        for j in range(4):
                            S.op("pe", lambda e, j=j: e.transpose(out=psb[bank][:, j * 32:(j + 1) * 32],
                                                                  in_=ostg[0:32, j * 128:(j + 1) * 128],
                                                                  identity=identf[0:32, 0:32]),
                                 reads=[("ostg",), ("identf",)], writes=[PS(bank)], signal=(j == 3))
                        S.op("act", lambda e: e.copy(out=us[:, :, :, 0:2],
                                                     in_=psb[bank][:, 0:128].rearrange("p (j s r) -> p j s r", j=4, s=16)),
                             reads=[PS(bank)], writes=[("us",)])

                    gq = gi_(l, "q")
                    slq = gq % NSLOT
                    wq = wslots[:, slq, 0:KD * 512].rearrange("p (k n) -> p k n", k=KD)
                    for c in range(4):
                        yield item(JOB, ("gran", gq))
                        bank = PB_ALL[projctr[0] % 4]
                        projctr[0] += 1
                        for k in range(KD):
                            for hf_, h_ in enumerate((c, 4 + c)):
                                S.op("pe", lambda e, k=k, hf_=hf_, h_=h_: e.matmul(
                                    psb[bank][hf_ * 64:(hf_ + 1) * 64, 0:n], lhsT=wq[:, k, h_ * 64:(h_ + 1) * 64],
                                    rhs=hT[:, k, 0:n], start=(k == 0), stop=(k == KD - 1)),
                                    reads=[("w", slq), Hc(k), HRU], writes=[PS(bank)], signal=(k == KD - 1 and hf_ == 1))
                        ps, pcell = psb[bank][:, 0:n], PS(bank)
                        S.op("act", lambda e, c=c, ps=ps: e.copy(out=o.qT[:, c, 0:n], in_=ps), reads=[pcell, R1U],
                             writes=[("q", s, c)])
                    g_done(gq)
                    gkv = gi_(l, "kv")
                    slkv = gkv % NSLOT
                    wkv = wslots[:, slkv, 0:KD * 256].rearrange("p (k n) -> p k n", k=KD)
                    yield item(JOB, ("gran", gkv))
                    ps, pcell = proj(KD, n, lambda k: wkv[:, k, 0:128], lambda k: hT[:, k, 0:n],
                                     lambda k: [Hc(k), HRU], ("w", slkv))
                    S.op("act", lambda e, ps=ps: e.copy(out=o.kT[:, 128:128 + n], in_=ps), reads=[pcell, R1U], writes=[("k", s)])
                    for ti, (kind, g) in enumerate(tiles):
                        yield item(KD * 256)
                        need_out = (kind == "s") or (kind == "p" and g == 15)
                        bank = PB_ST[ti % 2]
                        c0w, nw = (0, 256) if need_out else (128, 128)
                        for k in range(KD):
                            S.op("pe", lambda e, k=k, bank=bank, c0w=c0w, nw=nw: e.matmul(
                                psb[bank][:, 0:nw], lhsT=hT[:, k, ti * 128:(ti + 1) * 128], rhs=wkv[:, k, c0w:c0w + nw],
                                start=(k == 0), stop=(k == KD - 1)),
                                reads=[("w", slkv), Hc(k), HRU], writes=[PS(bank)], signal=(k == KD - 1))
                        voff = 128 if need_out else 0
                        S.op("act", lambda e, bank=bank, voff=voff: e.copy(out=o.Vt[:, 1 + ti, :], in_=psb[bank][:, voff:voff + 128]),
                             reads=[PS(bank), R1U], writes=[("v", s)])
                        if need_out:
                            S.op("dve", lambda e, bank=bank: e.tensor_copy(out=ostg[:, 0:256], in_=psb[bank][:, 0:256]),
                                 reads=[PS(bank)], writes=[("ostg",)])
                            if kind == "p":
                                S.dma("sp", nkp[l], ostg[:, 0:128], reads=[("ostg",)])
                                S.dma("sp", nvp[l], ostg[:, 128:256], reads=[("ostg",)])
                            else:
                                for sq_ in range(16):
                                    S.dma("sp", nks[l, sq_, 120:128, :], ostg[sq_ * 8:(sq_ + 1) * 8, 0:128], reads=[("ostg",)])
                                    S.dma("sp", nvs[l, sq_, 120:128, :], ostg[sq_ * 8:(sq_ + 1) * 8, 128:256], reads=[("ostg",)])
                    g_done(gkv)
                    if prev_blk is None:
                        yield item(0)
                        S.op("dve", lambda e: e.memset(o.kT[:, 0:128], 0.0), reads=[R1U], writes=[("k", s)])
                        S.op("dve", lambda e: e.memset(o.Vt[:, 0, :], 0.0), reads=[R1U], writes=[("v", s)])
                    else:
                        yield item(0, ("mark", ("kv", prev_blk, l)))
                        S.op("dve", lambda e: e.tensor_copy(out=o.kT[:, 0:128], in_=kcarry[:, l, :]),
                             reads=[("kcarry", l), R1U], writes=[("k", s)])
                        S.op("dve", lambda e: e.tensor_copy(out=o.Vt[:, 0, :], in_=vcarry[:, l, :]),
                             reads=[("vcarry", l), R1U], writes=[("v", s)])
                    if not last_block:
                        S.op("dve", lambda e: e.tensor_copy(out=kcarry[:, l, :], in_=o.kT[:, TP:TP + 128]),
                             reads=[("k", s), R1U], writes=[("kcarry", l)])
                        S.op("dve", lambda e: e.tensor_copy(out=vcarry[:, l, :], in_=o.Vt[:, NTP, :]),
                             reads=[("v", s), R1U], writes=[("vcarry", l)])
                    marks.add(("kv", (pi, s), l))

                    def att_S(ti, gi):
                        kind, g = tiles[ti]
                        tcols = slice(ti * 128, (ti + 1) * 128)
                        prevc = slice(ti * 128, (ti + 1) * 128)
                        ownc = slice(128 + ti * 128, 128 + (ti + 1) * 128)
                        if True:
                            for hf in range(2):
                                bank = PB_S[hf]
                                rows = slice(hf * 64, (hf + 1) * 64)
                                for c2 in range(2):
                                    c = gi * 2 + c2
                                    base = c2 * 256
                                    if kind == "p":
                                        S.op("pe", lambda e, c=c, base=base: e.matmul(
                                            psb[bank][:, base:base + 128], lhsT=o.kT[rows, prevc], rhs=o.qT[rows, c, tcols],
                                            start=True, stop=True),
                                            reads=[("k", s), ("q", s, c), R1U], writes=[PS(bank)], signal=False)
                                    else:
                                        for sq_ in range(16):
                                            S.op("pe", lambda e, c=c, base=base, sq_=sq_: e.matmul(
                                                psb[bank][:, base + sq_ * 8:base + sq_ * 8 + 8], lhsT=KcT[rows, sq_, :],
                                                rhs=o.qT[rows, c, scol + sq_ * 8:scol + sq_ * 8 + 8], start=True, stop=True),
                                                reads=[("KcT",), ("q", s, c), R1U], writes=[PS(bank)], signal=False)
                                    S.op("pe", lambda e, c=c, base=base: e.matmul(
                                        psb[bank][:, base + 128:base + 256], lhsT=o.kT[rows, ownc], rhs=o.qT[rows, c, tcols],
                                        start=True, stop=True),
                                        reads=[("k", s), ("q", s, c), R1U], writes=[PS(bank)], signal=(c2 == 1))
                                S.op("act", lambda e, gi=gi, hf=hf, bank=bank: e.activation(
                                    out=o.Pt[:, hf, :, gi * 2:gi * 2 + 2, :].rearrange("p a c q -> p c a q"),
                                    in_=psb[bank][:, :].rearrange("p (c a q) -> p c a q", c=2, a=2),
                                    func=AF.Exp, scale=0.125),
                                    reads=[PS(bank)], writes=[("Pt", s, hf, gi)])
                        if gi == 0:
                            return
                        if kind == "s":
                            M = M_sample
                        elif g == 0:
                            M = M_first
                        else:
                            M = M_prompt
                        for hf in range(2):
                            S.op("dve", lambda e, hf=hf, M=M: e.tensor_tensor(
                                out=o.Pt[:, hf], in0=o.Pt[:, hf], in1=M[:, :, None, :].to_broadcast([128, 2, 4, 128]), op=ALU.mult),
                                reads=[("Pt", s, hf, 0), ("Pt", s, hf, 1), ("masks",)],
                                writes=[("Pt", s, hf, 0), ("Pt", s, hf, 1)])

                    def att_O(ti):
                        kind, g = tiles[ti]
                        Pt = o.Pt
                        for hf in range(2):
                            rows = slice(hf * 64, (hf + 1) * 64)
                            pcells = [("Pt", s, hf, 0), ("Pt", s, hf, 1)]
                            if kind == "p":
                                S.op("pe", lambda e, hf=hf: e.matmul(
                                    psb[PB_O][rows, :], lhsT=o.Vt[:, ti, hf * 64:(hf + 1) * 64],
                                    rhs=Pt[:, hf, 0].rearrange("p c q -> p (c q)"), start=True, stop=False),
                                    reads=pcells + [("v", s), R1U], writes=[PS(PB_O)], signal=False)
                                S.op("pe", lambda e, hf=hf: e.matmul(
                                    psb[PB_O][rows, :], lhsT=o.Vt[:, 1 + ti, hf * 64:(hf + 1) * 64],
                                    rhs=Pt[:, hf, 1].rearrange("p c q -> p (c q)"), start=False, stop=True),
                                    reads=pcells + [("v", s), R1U], writes=[PS(PB_O)], signal=False)
                                r0 = Pt[:, hf, 0].rearrange("p c q -> p (c q)")
                                r1 = Pt[:, hf, 1].rearrange("p c q -> p (c q)")
                            else:
                                S.op("pe", lambda e, hf=hf: e.matmul(
                                    psb[PB_O][rows, :], lhsT=o.Vt[:, 1 + ti, hf * 64:(hf + 1) * 64],
                                    rhs=Pt[:, hf, 1].rearrange("p c (s t) -> p s c t", s=16), start=True, stop=False),
                                    reads=pcells + [("v", s), R1U], writes=[PS(PB_O)], signal=False)
                                for sq_ in range(16):
                                    S.op("pe", lambda e, hf=hf, sq_=sq_: e.matmul(
                                        psb[PB_O][rows, sq_ * 32:(sq_ + 1) * 32],
                                        lhsT=Vc[:, sq_, hf * 64:(hf + 1) * 64],
                                        rhs=Pt[:, hf, 0, :, sq_ * 8:(sq_ + 1) * 8], start=False, stop=(sq_ == 15)),
                                        reads=pcells + [("Vc",)], writes=[PS(PB_O)], signal=False)
                                r0 = Pt[:, hf, 0].rearrange("p c (s t) -> p s c t", s=16)
                                r1 = Pt[:, hf, 1].rearrange("p c (s t) -> p s c t", s=16)
                            S.op("pe", lambda e, hf=hf, r0=r0: e.matmul(
                                psb[PB_D][rows, :], lhsT=onesb[:, 0:64], rhs=r0, start=True, stop=False),
                                reads=pcells + [("onesb",)], writes=[PS(PB_D)], signal=False)
                            S.op("pe", lambda e, hf=hf, r1=r1: e.matmul(
                                psb[PB_D][rows, :], lhsT=onesb[:, 0:64], rhs=r1, start=False, stop=True),
                                reads=pcells + [("onesb",)], writes=[PS(PB_D)], signal=(hf == 1))
                        if kind == "p":
                            dview = lambda ap: ap.rearrange("p (c q) -> p c q", c=4)
                            aview = o.aT[:, :, ti * 128:(ti + 1) * 128]
                        else:
                            dview = lambda ap: ap.rearrange("p (s c t) -> p c s t", s=16, c=4)
                            aview = o.aT[:, :, ti * 128:(ti + 1) * 128].rearrange("p c (s t) -> p c s t", s=16)
                        for c in range(4):
                            S.op("act", lambda e, c=c: e.activation(
                                out=dview(o.dent[:])[:, c], in_=dview(psb[PB_D][:, :])[:, c], func=AF.Ln,
                                bias=sinkcol[:, l, c:c + 1]),
                                reads=[PS(PB_D), ("sinkcol",)], writes=[("dent", s)])
                        S.op("act", lambda e: e.activation(out=o.dent[:], in_=o.dent[:], func=AF.Exp, scale=-1.0),
                             reads=[("dent", s)], writes=[("dent", s)])
                        S.op("dve", lambda e: e.tensor_tensor(
                            out=aview, in0=dview(psb[PB_O][:, :]), in1=dview(o.dent[:]), op=ALU.mult),
                            reads=[PS(PB_O), ("dent", s), R1U], writes=[("a", s)])

                    att_stages = []
                    for ti in range(NT):
                        att_stages.append(("S0", ti))
                        att_stages.append(("S1", ti))
                        att_stages.append(("O", ti))

                    def conv_job(which, j):
                        gname = {"C": "C", "u": "u", "B": "B"}[which]
                        gidx = gi_(l, gname)
                        sl_ = gidx % NSLOT
                        wv = wslots[:, sl_, 0:KD * 512].rearrange("p (k n) -> p k n", k=KD)
                        ps, pcell = proj(KD, n, lambda k: wv[:, k, j * 128:(j + 1) * 128], lambda k: hT[:, k, 0:n],
                                         lambda k: [Hc(k), HRU], ("w", sl_))
                        if which == "C":
                            wr = [("C", s, j)] + ([("Kc",)] if has_sample else [])
                            S.op("act", lambda e: e.copy(out=o.Cf[:, j, 0:n], in_=ps), reads=[pcell, R1U], writes=wr)
                        elif which == "u":
                            S.op("dve", lambda e: e.tensor_tensor(out=o.uf[:, j, 2:2 + n], in0=ps, in1=o.Cf[:, j, 0:n], op=ALU.mult),
                                 reads=[pcell, ("C", s, j), R1U], writes=[("u", s, j)])
                        else:
                            S.op("dve", lambda e: e.tensor_tensor(out=o.BzT[:, j, 0:n], in0=ps, in1=o.Cf[:, j, 0:n], op=ALU.mult),
                                 reads=[pcell, ("C", s, j), R1U], writes=[("bz", s, j)])
                        if j == 3:
                            g_done(gidx)

                    def conv_prefix():
                        if prev_blk is None:
                            S.op("dve", lambda e: e.memset(o.uf[:, :, 0:2], 0.0), reads=[R1U], writes=[("upre", s)])
                        else:
                            S.op("dve", lambda e: e.tensor_copy(out=o.uf[:, :, 0:2], in_=ucarry[:, l, :, :]),
                                 reads=[("ucarry", l), R1U], writes=[("upre", s)])

                    def conv_chunk(j):
                        ucells = [("u", s, j), ("upre", s)]
                        ccells = [("C", s, j)]
                        S.op("dve", lambda e: e.tensor_scalar(out=o.Cf[:, j, 0:TP], in0=o.uf[:, j, 0:TP], scalar1=cw_(l, 0, j),
                                                              scalar2=None, op0=ALU.mult),
                             reads=ucells + [("gcol",), R1U], writes=ccells)
                        for i in (1, 2):
                            S.op("dve", lambda e, i=i: e.scalar_tensor_tensor(
                                out=o.Cf[:, j, 0:TP], in0=o.uf[:, j, i:i + TP], scalar=cw_(l, i, j), in1=o.Cf[:, j, 0:TP],
                                op0=ALU.mult, op1=ALU.add),
                                reads=ucells + ccells + [("gcol",), R1U], writes=ccells)
                        if has_sample:
                            S.op("dve", lambda e: e.tensor_copy(
                                out=us[:, j, :, 2:10], in_=o.uf[:, j, 2 + scol:2 + scol + 128].rearrange("p (s t) -> p s t", s=16)),
                                reads=ucells + [R1U], writes=[("us",)])
                            zs = o.Cf[:, j, scol:scol + 128].rearrange("p (s t) -> p s t", s=16)
                            S.op("dve", lambda e: e.tensor_scalar(out=zs, in0=us[:, j, :, 0:8], scalar1=cw_(l, 0, j),
                                                                  scalar2=None, op0=ALU.mult),
                                 reads=[("us",), ("gcol",), R1U], writes=ccells)
                            for i in (1, 2):
                                S.op("dve", lambda e, i=i: e.scalar_tensor_tensor(
                                    out=zs, in0=us[:, j, :, i:i + 8], scalar=cw_(l, i, j), in1=zs, op0=ALU.mult, op1=ALU.add),
                                    reads=[("us",), ("gcol",), R1U] + ccells, writes=ccells)
                            S.op("dve", lambda e: e.tensor_copy(out=ncstg[:, j, :].rearrange("p (s r) -> p s r", s=16),
                                                                in_=us[:, j, :, 8:10]),
                                 reads=[("us",)], writes=[("ncstg", j)])

                    def conv_publish():
                        allu = [("u", s, j) for j in range(4)]
                        if not last_block:
                            S.op("dve", lambda e: e.tensor_copy(out=ucarry[:, l, :, :], in_=o.uf[:, :, TP:TP + 2]),
                                 reads=allu + [("upre", s), R1U], writes=[("ucarry", l)])
                        marks.add(("u", (pi, s), l))
                        if has_sample:
                            bank = PB_ST[0]
                            for j in range(4):
                                S.op("pe", lambda e, j=j: e.transpose(out=psb[bank][0:2, j * 128:(j + 1) * 128],
                                                                      in_=o.uf[:, j, TP:TP + 2], identity=identf[:]),
                                     reads=[("u", s, j), ("upre", s), ("identf",), R1U], writes=[PS(bank)], signal=(j == 3))
                            S.op("act", lambda e: e.copy(out=ostg[0:2, :], in_=psb[bank][0:2, :]), reads=[PS(bank)], writes=[("ostg",)])
                            S.dma("sp", ncp[l], ostg[0:2, :], reads=[("ostg",)])
                            bank = PB_ST[1]
                            for j in range(4):
                                S.op("pe", lambda e, j=j: e.transpose(out=psb[bank][0:32, j * 128:(j + 1) * 128],
                                                                      in_=ncstg[:, j, :], identity=identf[:]),
                                     reads=[("ncstg", j), ("identf",)], writes=[PS(bank)], signal=(j == 3))
                            S.op("act", lambda e: e.copy(out=ostg[0:32, :], in_=psb[bank][0:32, :]), reads=[PS(bank)], writes=[("ostg",)])
                            S.dma("sp", ncs[l], ostg[0:32, :], reads=[("ostg",)])

                    conv_seq = [("C", j) for j in range(4)] + [("prefix", 0)]
                    for j in range(4):
                        conv_seq += [("u", j), ("conv", j)]
                    conv_seq += [("publish", 0)] + [("B", j) for j in range(4)]
                    ai = 0
                    ji = 0
                    while ai < len(att_stages) or ji < len(conv_seq):
                        if ai < len(att_stages):
                            kind_, ti_ = att_stages[ai]
                            if kind_ in ("S0", "S1"):
                                yield item(2500)
                                att_S(ti_, 0 if kind_ == "S0" else 1)
                            else:
                                yield item(4500)
                                att_O(ti_)
                            ai += 1
                        if ji < len(conv_seq):
                            which, j = conv_seq[ji]
                            if which == "prefix":
                                if prev_blk is None:
                                    yield item(0)
                                else:
                                    yield item(0, ("mark", ("u", prev_blk, l)))
                                conv_prefix()
                            elif which == "conv":
                                yield item(1500)
                                conv_chunk(j)
                            elif which == "publish":
                                yield item(500)
                                conv_publish()
                            else:
                                yield item(JOB, ("gran", gi_(l, which)))
                                conv_job(which, j)
                            ji += 1

                    for m in range(2):
                        g_x = [gi_(l, "gx%d_0" % m), gi_(l, "gx%d_1" % m)]
                        g_oc = gi_(l, "aoco%d" % m)
                        s_oc = g_oc % NSLOT
                        woc = wslots[:, s_oc, 0:KD * 512].rearrange("p (k n) -> p k n", k=KD)
                        for jj in range(4):
                            j = m * 4 + jj
                            half, jl = jj // 2, jj % 2
                            g_gx = g_x[half]
                            s_gx = g_gx % NSLOT
                            wgx = wslots[:, s_gx, 0:KD * 512].rearrange("p (k n) -> p k n", k=KD)
                            gp = j % 2
                            A = gp * 2
                            Cc = gp * 2 + 1
                            yield item(JOB, ("gran", g_gx))
                            ps, pcell = proj(KD, n, lambda k: wgx[:, k, jl * 128:(jl + 1) * 128], lambda k: hT[:, k, 0:n],
                                             lambda k: [Hc(k), HRU], ("w", s_gx))
                            S.op("act", lambda e, ps=ps: e.activation(out=o.gt[:, A, 0:n], in_=ps, func=AF.Sigmoid),
                                 reads=[pcell, R1U], writes=[("gt", s, A)])
                            yield item(4 * BMAX, ("gran", g_oc))
                            ps, pcell = proj(4, n, lambda k: woc[:, k, jj * 128:(jj + 1) * 128], lambda k: o.aT[:, k, 0:n],
                                             lambda k: [("a", s), R1U], ("w", s_oc))
                            S.op("dve", lambda e, ps=ps: e.tensor_tensor(out=o.gt[:, A, 0:n], in0=ps, in1=o.gt[:, A, 0:n], op=ALU.mult),
                                 reads=[pcell, ("gt", s, A), R1U], writes=[("gt", s, A)])
                            yield item(JOB)
                            ps, pcell = proj(KD, n, lambda k: wgx[:, k, 256 + jl * 128:256 + (jl + 1) * 128], lambda k: hT[:, k, 0:n],
                                             lambda k: [Hc(k), HRU], ("w", s_gx))
                            S.op("act", lambda e, ps=ps: e.activation(out=o.gt[:, Cc, 0:n], in_=ps, func=AF.Sigmoid),
                                 reads=[pcell, R1U], writes=[("gt", s, Cc)])
                            yield item(4 * BMAX)
                            ps, pcell = proj(4, n, lambda k: woc[:, 4 + k, jj * 128:(jj + 1) * 128], lambda k: o.BzT[:, k, 0:n],
                                             lambda k: [("bz", s, k), R1U], ("w", s_oc))
                            S.op("dve", lambda e, ps=ps: e.tensor_tensor(out=o.gt[:, Cc, 0:n], in0=ps, in1=o.gt[:, Cc, 0:n], op=ALU.mult),
                                 reads=[pcell, ("gt", s, Cc), R1U], writes=[("gt", s, Cc)])
                            S.op("dve", lambda e, j=j: e.tensor_tensor(out=o.mT[:, j, 0:n], in0=o.gt[:, A, 0:n],
                                                                       in1=o.gt[:, Cc, 0:n], op=ALU.add),
                                 reads=[("gt", s, A), ("gt", s, Cc), R1U], writes=[("m", s, j)])
                            if jl == 1:
                                g_done(g_gx)
                        g_done(g_oc)

                    yield item(0)
                    fence(HRU)
                    for it in proj_to_R2(l, ["wout0", "wout1"], KD, 4,
                                         lambda sl_, i, k: wslots[:, sl_, 0:KD * 512].rearrange("p (k n) -> p k n", k=KD)[:, k, (i % 4) * 128:(i % 4 + 1) * 128],
                                         lambda k: o.mT[:, k, 0:n], lambda k: ("m", s, k), R1U):
                        yield it
                    for it in post_norm(1, l):
                        yield it

                    yield item(0)
                    fence(R1U)
                    for it in make_xg(l):
                        yield it
                    for m in range(8):
                        gidx = gi_(l, "wup%d" % m)
                        sl_ = gidx % NSLOT
                        wu_ = wslots[:, sl_, 0:KD * 512].rearrange("p (k n) -> p k n", k=KD)
                        for jj in range(4):
                            f = m * 4 + jj
                            yield item(JOB, ("gran", gidx))
                            ps, pcell = proj(KD, n, lambda k: wu_[:, k, jj * 128:(jj + 1) * 128], lambda k: hT[:, k, 0:n],
                                             lambda k: [Hc(k), HRU], ("w", sl_))
                            rb = f % 2
                            S.op("act", lambda e, ps=ps, rb=rb: e.activation(out=o.rtmp[:, rb, 0:n], in_=ps, func=AF.Relu),
                                 reads=[pcell, R1U], writes=[("rtmp", s, rb)])
                            S.op("dve", lambda e, f=f, rb=rb: e.tensor_tensor(out=o.actT[:, f, 0:n], in0=o.rtmp[:, rb, 0:n],
                                                                              in1=o.rtmp[:, rb, 0:n], op=ALU.mult),
                                 reads=[("rtmp", s, rb), R1U], writes=[("act", s, f)])
                        g_done(gidx)
                        if m == 0:
                            yield item(1500)
                            xstats_finish()
                    yield item(0)
                    fence(HRU)
                    for it in proj_to_R2(l, ["wdn%d" % i for i in range(8)], 32, 1,
                                         lambda sl_, i, k: wslots[:, sl_, 0:32 * 128].rearrange("p (k n) -> p k n", k=32)[:, k, :],
                                         lambda k: o.actT[:, k, 0:n], lambda k: ("act", s, k), R1U, fin=stat_fin_mlp):
                        yield it
                    if l == 1:
                        yield item(0)
                        fence(R1U)
                        if pi + 1 < len(PARTS):
                            issue_xin(pi + 1)
                    for it in post_norm(3, l):
                        yield it

                for ti, (kind, g) in enumerate(tiles):
                    yield item(2048)
                    for half in range(2):
                        bank = (PB_ST + PB_S)[(ti * 2 + half) % 4]
                        for kk in range(4):
                            k = half * 4 + kk
                            S.op("pe", lambda e, k=k, kk=kk, bank=bank: e.transpose(
                                out=psb[bank][:, kk * 128:(kk + 1) * 128], in_=xT[:, k, ti * 128:(ti + 1) * 128],
                                identity=identf[:]),
                                reads=[Xc(k), ("identf",)], writes=[PS(bank)], signal=(kk == 3))
                        S.op("act", lambda e, half=half, bank=bank: e.copy(
                            out=o.yout[:, ti, half * 512:(half + 1) * 512], in_=psb[bank][:, :]),
                            reads=[PS(bank), R1U], writes=[("yout", s, ti)])
                    dst = yp[g * 128:(g + 1) * 128, :] if kind == "p" else ys[:, :]
                    S.dma("sp", dst, o.yout[:, ti, :], reads=[("yout", s, ti), R1U])

        gens = [stream(0), stream(1)]
        clocks = [0, SKEW]
        prev = [next(g) for g in gens]
        while any(alive):
            order = sorted([s for s in (0, 1) if alive[s]], key=lambda s: (clocks[s], s))
            for s in order:
                cost, needs = prev[s]
                if all(check_need(nd, s) for nd in needs):
                    clocks[s] += cost
                    for nd in needs:
                        if nd[0] == "gran":
                            cur_gran[s] = max(cur_gran[s], nd[1])
                    try:
                        prev[s] = next(gens[s])
                    except StopIteration:
                        alive[s] = False
                    break
            else:
                raise RuntimeError("both streams blocked: %r" % (prev,))

        assert G["issued"] == len(gran_seq) and all(d == 2 for d in G["done"])
        S.drain("sp")
        build_program.stats = dict(ninst=S.ninst, nwaits=S.nwaits, cnt=dict(S.ccnt), sbuf_left=nc.sbuf_bytes_remaining)
    return nc


_CACHE = {}


def kernel(**inputs):
    f32 = lambda a: np.ascontiguousarray(np.asarray(a, dtype=np.float32))
    x_prompt = f32(inputs["x_prompt"])
    x_sample = f32(inputs["x_sample"])
    cache_k = f32(inputs["cache_k"])
    cache_v = f32(inputs["cache_v"])
    state_conv = f32(inputs["state_conv"])
    shared = {n: f32(inputs[n]) for n in ("g_mix_pre", "g_mix_post", "g_mlp_pre", "g_mlp_post", "w_in", "attn_sinks",
                                          "conv_w", "w_attn_o", "w_conv_o", "w_out", "w_up", "w_down")}
    if "nc" not in _CACHE:
        _CACHE["nc"] = build_program()
    nc = _CACHE["nc"]
    in_maps = []
    for c in range(NCORES):
        s0, s1 = 16 * c, 16 * (c + 1)
        m = dict(shared)
        m["xp"] = x_prompt[c]
        m["xs"] = np.ascontiguousarray(x_sample[s0:s1].reshape(128, D))
        m["ck"] = np.ascontiguousarray(cache_k[:, s0:s1].reshape(2, 16, 128, 128))
        m["cv"] = np.ascontiguousarray(cache_v[:, s0:s1].reshape(2, 16, 128, 128))
        m["sc"] = np.ascontiguousarray(state_conv[:, s0:s1].reshape(2, 32, 512))
        in_maps.append(m)
    res = run_bass_kernel_spmd(nc, in_maps, core_ids=list(range(NCORES)))
    R = res.results
    y_prompt = np.stack([R[c]["yp"] for c in range(NCORES)], axis=0).astype(np.float32)
    y_sample = np.concatenate([R[c]["ys"].reshape(16, 8, D) for c in range(NCORES)], axis=0).astype(np.float32)
    nk_p = np.stack([R[c]["nkp"].reshape(2, 128, 2, 64) for c in range(NCORES)], axis=1).astype(np.float32)
    nv_p = np.stack([R[c]["nvp"].reshape(2, 128, 2, 64) for c in range(NCORES)], axis=1).astype(np.float32)
    nc_p = np.stack([R[c]["ncp"] for c in range(NCORES)], axis=1).astype(np.float32)
    nk_s = np.concatenate([R[c]["nks"].reshape(2, 16, 128, 2, 64) for c in range(NCORES)], axis=1).astype(np.float32)
    nv_s = np.concatenate([R[c]["nvs"].reshape(2, 16, 128, 2, 64) for c in range(NCORES)], axis=1).astype(np.float32)
    nc_s = np.concatenate([R[c]["ncs"].reshape(2, 16, 2, 512) for c in range(NCORES)], axis=1).astype(np.float32)
    return (y_prompt, y_sample, nk_p, nv_p, nc_p, nk_s, nv_s, nc_s)
```
